# Optimizing a Trainium2 kernel written in Bass

```python
import math
import jax
import jax.numpy as jnp
from jax import lax
import numpy as np

D_MODEL = 2048
BATCH = 8
SEQ = 4096
DEPTH = 4

CHUNK = 64
N_MIXERS = 2
N_RWKV = (DEPTH + N_MIXERS - 1) // N_MIXERS
N_MAMBA = DEPTH // N_MIXERS
RWKV_HEAD = 64
RWKV_HEADS = D_MODEL // RWKV_HEAD
LORA_DECAY = 96
LORA_ICLR = 96
LORA_VALUE = 64
LORA_GATE = 256
GN_EPS = 64e-5
SSM_INNER = 2 * D_MODEL
SSM_HEAD = 64
SSM_HEADS = SSM_INNER // SSM_HEAD
SSM_GROUPS = 8
SSM_HPG = SSM_HEADS // SSM_GROUPS
SSM_STATE = 128
SSM_CONV = 4
SSM_BC = SSM_GROUPS * SSM_STATE
SSM_CONV_DIM = SSM_INNER + 2 * SSM_BC
SSM_PROJ = SSM_INNER + SSM_CONV_DIM + SSM_HEADS
FFN_HIDDEN = 5504
FFN_CONV = 3
RMS_EPS = 1e-6
N_MOD = 6

kernel_name = "rwkv7_mamba2_convffn_sandwich_adaln_trunk"


def _rms(x, g):
    xf = x.astype(jnp.float32)
    y = xf * lax.rsqrt(jnp.mean(xf * xf, axis=-1, keepdims=True) + RMS_EPS)
    return (y * g).astype(x.dtype)


def _token_shift(x):
    return jnp.pad(x, ((0, 0), (1, 0), (0, 0)))[:, :-1]


def _causal_dwconv(x, w, b):
    width, ch = w.shape
    y = lax.conv_general_dilated(x, w[:, None, :], window_strides=(1,), padding=[(width - 1, 0)],
                                 dimension_numbers=("NWC", "WIO", "NWC"), feature_group_count=ch)
    return y + b


def _rwkv7_scan(r, decay, k, v, a_vec, b_vec):
    bsz, _, nh, hd = r.shape
    xs = tuple(jnp.moveaxis(t, 1, 0) for t in (r, decay, k, v, a_vec, b_vec))

    def step(state, inp):
        r_t, w_t, k_t, v_t, a_t, b_t = inp
        sa = jnp.einsum("bhvk,bhk->bhv", state, a_t)
        state = (state * w_t[:, :, None, :] + sa[..., None] * b_t[:, :, None, :]
                 + v_t[..., None] * k_t[:, :, None, :])
        return state, jnp.einsum("bhvk,bhk->bhv", state, r_t)

    s0 = jnp.zeros((bsz, nh, hd, hd), jnp.float32)
    _, y = lax.scan(step, s0, xs)
    return jnp.moveaxis(y, 0, 1)


def _rwkv7_mix(h, v_first, mu, w_rkv, w0, w1, w2, a0, a1, a2, v_lora, g1, g2,
               k_k, k_a, r_k, ln_w, ln_b, w_o):
    bsz, seq, d = h.shape
    xx = _token_shift(h) - h
    x_rkv = h[None] + xx[None] * mu[:3, None, None, :]
    r, k, v = jnp.einsum("nbtd,nde->nbte", x_rkv, w_rkv)
    xw = h + xx * mu[3]
    xa = h + xx * mu[4]
    xg = h + xx * mu[5]
    w = -jax.nn.softplus(-(w0 + jnp.tanh(xw @ w1) @ w2)) - 0.5
    decay = jnp.exp(-jnp.exp(w.astype(jnp.float32)))
    a = jax.nn.sigmoid(a0 + (xa @ a1) @ a2)
    g = jax.nn.sigmoid(xg @ g1) @ g2
    if v_first is None:
        v_first = v
    else:
        v0, v1, v2 = v_lora
        v = v + (v_first - v) * jax.nn.sigmoid(v0 + (x_rkv[2] @ v1) @ v2)
    heads = lambda t: t.astype(jnp.float32).reshape(bsz, seq, RWKV_HEADS, RWKV_HEAD)
    kk = heads(k * k_k)
    kk = kk * lax.rsqrt(jnp.maximum(jnp.sum(kk * kk, axis=-1, keepdims=True), 1e-24))
    k = k * (1 + (a - 1) * k_a)
    rh, kh, vh, ah = heads(r), heads(k), heads(v), heads(a)
    y = _rwkv7_scan(rh, heads(decay), kh, vh, -kk, kk * ah)
    mean = jnp.mean(y, axis=-1, keepdims=True)
    var = jnp.mean(jnp.square(y - mean), axis=-1, keepdims=True)
    y = (y - mean) * lax.rsqrt(var + GN_EPS)
    y = y * ln_w.reshape(RWKV_HEADS, RWKV_HEAD) + ln_b.reshape(RWKV_HEADS, RWKV_HEAD)
    y = y + jnp.sum(rh * kh * r_k, axis=-1, keepdims=True) * vh
    y = y.reshape(bsz, seq, d).astype(h.dtype) * g
    return y @ w_o, v_first


def _ssd_scan(xdt, adt, bm, cm):
    bsz, seq = xdt.shape[:2]
    nc = seq // CHUNK

    def chunks(t):
        return jnp.moveaxis(t.reshape(bsz, nc, CHUNK, *t.shape[2:]), 1, 0)

    mask = jnp.tril(jnp.ones((CHUNK, CHUNK), bool))[None, :, :, None, None]

    def step(state, inp):
        x_c, a_c, b_c, c_c = inp
        acs = jnp.cumsum(a_c, axis=1)
        seg = acs[:, :, None] - acs[:, None, :]
        lmat = jnp.exp(jnp.where(mask, seg, -jnp.inf))
        cb = jnp.einsum("blgn,bsgn->blsg", c_c, b_c)
        y = jnp.einsum("blsg,blsgr,bsgrp->blgrp", cb, lmat, x_c)
        y = y + jnp.einsum("blgn,bgrpn->blgrp", c_c, state) * jnp.exp(acs)[..., None]
        w_end = jnp.exp(acs[:, -1:] - acs)
        state = (state * jnp.exp(acs[:, -1])[..., None, None]
                 + jnp.einsum("blgn,blgr,blgrp->bgrpn", b_c, w_end, x_c))
        return state, y

    s0 = jnp.zeros((bsz, SSM_GROUPS, SSM_HPG, SSM_HEAD, SSM_STATE), jnp.float32)
    _, y = lax.scan(step, s0, (chunks(xdt), chunks(adt), chunks(bm), chunks(cm)))
    return jnp.moveaxis(y, 0, 1).reshape(bsz, seq, SSM_GROUPS, SSM_HPG, SSM_HEAD)


def _mamba2_mix(h, w_in, conv_w, conv_b, dt_bias, a_log, d_skip, norm_w, w_out):
    bsz, seq, _ = h.shape
    zxbcdt = h @ w_in
    z = zxbcdt[..., :SSM_INNER]
    xbc = zxbcdt[..., SSM_INNER:SSM_INNER + SSM_CONV_DIM]
    dt = zxbcdt[..., SSM_INNER + SSM_CONV_DIM:]
    xbc = jax.nn.silu(_causal_dwconv(xbc, conv_w, conv_b)).astype(jnp.float32)
    xs = xbc[..., :SSM_INNER].reshape(bsz, seq, SSM_GROUPS, SSM_HPG, SSM_HEAD)
    bm = xbc[..., SSM_INNER:SSM_INNER + SSM_BC].reshape(bsz, seq, SSM_GROUPS, SSM_STATE)
    cm = xbc[..., SSM_INNER + SSM_BC:].reshape(bsz, seq, SSM_GROUPS, SSM_STATE)
    dt = jax.nn.softplus(dt.astype(jnp.float32) + dt_bias).reshape(bsz, seq, SSM_GROUPS, SSM_HPG)
    a = -jnp.exp(a_log.astype(jnp.float32)).reshape(SSM_GROUPS, SSM_HPG)
    y = _ssd_scan(xs * dt[..., None], a * dt, bm, cm)
    y = y + xs * d_skip.reshape(SSM_GROUPS, SSM_HPG, 1)
    y = y.reshape(bsz, seq, SSM_INNER) * jax.nn.silu(z.astype(jnp.float32))
    y = _rms(y, norm_w).astype(h.dtype)
    return y @ w_out


def _conv_ffn(h, w_in, conv_w, conv_b, w_out):
    u = _causal_dwconv(h @ w_in, conv_w, conv_b)
    gate, val = jnp.split(u, 2, axis=-1)
    return (jax.nn.silu(gate) * val) @ w_out


def setup_inputs(seed: int = 0) -> dict:
    key = jax.random.key(seed)
    ks = iter(jax.random.split(key, 64))
    nrm = lambda shape, scale: scale * jax.random.normal(next(ks), shape, jnp.float32)
    uni = lambda shape, lo, hi: jax.random.uniform(next(ks), shape, jnp.float32, lo, hi)
    d = D_MODEL
    sd = d ** -0.5
    nv = max(N_RWKV - 1, 0)
    dt0 = jnp.exp(uni((N_MAMBA, SSM_HEADS), math.log(1e-3), math.log(1e-1)))
    return {
        "x": nrm((BATCH, SEQ, d), 1.0),
        "c": nrm((BATCH, d), 1.0),
        "ada_w": nrm((d, N_MOD * d), 0.5 * sd),
        "ada_b": nrm((N_MOD * d,), 0.02),
        "ada_table": nrm((DEPTH, N_MOD * d), 0.1),
        "norm_mix_pre": 1.0 + nrm((DEPTH, d), 0.1),
        "norm_mix_post": 1.0 + nrm((DEPTH, d), 0.1),
        "norm_ffn_pre": 1.0 + nrm((DEPTH, d), 0.1),
        "norm_ffn_post": 1.0 + nrm((DEPTH, d), 0.1),
        "rwkv_mu": uni((N_RWKV, 6, d), 0.0, 1.0),
        "rwkv_w_rkv": nrm((N_RWKV, 3, d, d), sd),
        "rwkv_w0": uni((N_RWKV, d), -6.0, -1.0),
        "rwkv_w1": nrm((N_RWKV, d, LORA_DECAY), sd),
        "rwkv_w2": nrm((N_RWKV, LORA_DECAY, d), 0.5 * LORA_DECAY ** -0.5),
        "rwkv_a0": nrm((N_RWKV, d), 0.1),
        "rwkv_a1": nrm((N_RWKV, d, LORA_ICLR), sd),
        "rwkv_a2": nrm((N_RWKV, LORA_ICLR, d), 0.5 * LORA_ICLR ** -0.5),
        "rwkv_v0": 1.0 + nrm((nv, d), 0.1),
        "rwkv_v1": nrm((nv, d, LORA_VALUE), sd),
        "rwkv_v2": nrm((nv, LORA_VALUE, d), 0.5 * LORA_VALUE ** -0.5),
        "rwkv_g1": nrm((N_RWKV, d, LORA_GATE), sd),
        "rwkv_g2": nrm((N_RWKV, LORA_GATE, d), LORA_GATE ** -0.5),
        "rwkv_k_k": 0.85 + nrm((N_RWKV, d), 0.05),
        "rwkv_k_a": 1.0 + nrm((N_RWKV, d), 0.05),
        "rwkv_r_k": nrm((N_RWKV, RWKV_HEADS, RWKV_HEAD), 0.1),
        "rwkv_ln_w": 1.0 + nrm((N_RWKV, d), 0.1),
        "rwkv_ln_b": nrm((N_RWKV, d), 0.01),
        "rwkv_w_o": nrm((N_RWKV, d, d), sd),
        "ssm_w_in": nrm((N_MAMBA, d, SSM_PROJ), sd),
        "ssm_conv_w": nrm((N_MAMBA, SSM_CONV, SSM_CONV_DIM), SSM_CONV ** -0.5),
        "ssm_conv_b": nrm((N_MAMBA, SSM_CONV_DIM), 0.01),
        "ssm_dt_bias": dt0 + jnp.log(-jnp.expm1(-dt0)),
        "ssm_a_log": jnp.log(uni((N_MAMBA, SSM_HEADS), 1.0, 16.0)),
        "ssm_d": 1.0 + nrm((N_MAMBA, SSM_HEADS), 0.1),
        "ssm_norm": 1.0 + nrm((N_MAMBA, SSM_INNER), 0.1),
        "ssm_w_out": nrm((N_MAMBA, SSM_INNER, d), SSM_INNER ** -0.5),
        "ffn_w_in": nrm((DEPTH, d, 2 * FFN_HIDDEN), sd),
        "ffn_conv_w": nrm((DEPTH, FFN_CONV, 2 * FFN_HIDDEN), FFN_CONV ** -0.5),
        "ffn_conv_b": nrm((DEPTH, 2 * FFN_HIDDEN), 0.01),
        "ffn_w_out": nrm((DEPTH, FFN_HIDDEN, d), FFN_HIDDEN ** -0.5),
    }


def reference(x, c, ada_w, ada_b, ada_table, norm_mix_pre, norm_mix_post, norm_ffn_pre,
              norm_ffn_post, rwkv_mu, rwkv_w_rkv, rwkv_w0, rwkv_w1, rwkv_w2, rwkv_a0, rwkv_a1,
              rwkv_a2, rwkv_v0, rwkv_v1, rwkv_v2, rwkv_g1, rwkv_g2, rwkv_k_k, rwkv_k_a, rwkv_r_k,
              rwkv_ln_w, rwkv_ln_b, rwkv_w_o, ssm_w_in, ssm_conv_w, ssm_conv_b, ssm_dt_bias,
              ssm_a_log, ssm_d, ssm_norm, ssm_w_out, ffn_w_in, ffn_conv_w, ffn_conv_b, ffn_w_out):
    mod = jax.nn.silu(c) @ ada_w + ada_b
    v_first = None
    for layer in range(DEPTH):
        sh_m, sc_m, g_m, sh_f, sc_f, g_f = jnp.split((mod + ada_table[layer])[:, None, :], N_MOD, axis=-1)
        h = _rms(x, norm_mix_pre[layer]) * (1 + sc_m) + sh_m
        idx = layer // N_MIXERS
        if layer % N_MIXERS == 0:
            v_lora = None if idx == 0 else (rwkv_v0[idx - 1], rwkv_v1[idx - 1], rwkv_v2[idx - 1])
            y, v_first = _rwkv7_mix(h, v_first, rwkv_mu[idx], rwkv_w_rkv[idx], rwkv_w0[idx],
                                    rwkv_w1[idx], rwkv_w2[idx], rwkv_a0[idx], rwkv_a1[idx],
                                    rwkv_a2[idx], v_lora, rwkv_g1[idx], rwkv_g2[idx],
                                    rwkv_k_k[idx], rwkv_k_a[idx], rwkv_r_k[idx],
                                    rwkv_ln_w[idx], rwkv_ln_b[idx], rwkv_w_o[idx])
        else:
            y = _mamba2_mix(h, ssm_w_in[idx], ssm_conv_w[idx], ssm_conv_b[idx], ssm_dt_bias[idx],
                            ssm_a_log[idx], ssm_d[idx], ssm_norm[idx], ssm_w_out[idx])
        x = x + g_m * _rms(y, norm_mix_post[layer])
        h = _rms(x, norm_ffn_pre[layer]) * (1 + sc_f) + sh_f
        f = _conv_ffn(h, ffn_w_in[layer], ffn_conv_w[layer], ffn_conv_b[layer], ffn_w_out[layer])
        x = x + g_f * _rms(f, norm_ffn_post[layer])
    return x
```

```python
import numpy as np
import concourse.bass as bass
import concourse.mybir as mybir
from concourse.bass_utils import run_bass_kernel_spmd
from contextlib import ExitStack

F32 = mybir.dt.float32
BF16 = mybir.dt.bfloat16
AF = mybir.ActivationFunctionType
ALU = mybir.AluOpType
AX = mybir.AxisListType

D = 2048
DC = D // 128
TT = 512
FH = 5504
FHC = FH // 128
SLOT = 3072
NSLOT = 6
ENGS = ("pe", "act", "dve", "pool", "sp")


class Buf:
    __slots__ = ("name", "w", "r", "excl")

    def __init__(self, name="", excl=False):
        self.name = name
        self.w = None
        self.r = {}
        self.excl = excl


class Op:
    __slots__ = ("fn", "deps", "inc", "dma_sem", "dma_val")

    def __init__(self, fn):
        self.fn = fn
        self.deps = []
        self.inc = False
        self.dma_sem = None
        self.dma_val = 0


class FW:
    def __init__(self, nc):
        self.nc = nc
        self.ops = {e: [] for e in ENGS}
        self.dma_sems = []

    def new_dma_sem(self):
        self.dma_sems.append(0)
        return len(self.dma_sems) - 1

    def _add(self, eng, fn, reads, writes, extra=()):
        writes = list(writes) + [b for b in reads if b.excl]
        reads = [b for b in reads if not b.excl]
        op = Op(fn)
        idx = len(self.ops[eng])
        deps = list(extra)
        for b in reads:
            if b.w is not None:
                deps.append(b.w)
        for b in writes:
            if b.w is not None:
                deps.append(b.w)
            deps.extend(b.r.values())
        out = []
        seen = set()
        for t in deps:
            if t in seen:
                continue
            seen.add(t)
            if t[0] == "e" and t[1] == eng and eng in ("pe", "sp"):
                continue
            out.append(t)
        op.deps = out
        for t in out:
            if t[0] == "e":
                self.ops[t[1]][t[2]].inc = True
        self.ops[eng].append(op)
        return op, idx

    def op(self, eng, fn, reads=(), writes=()):
        op, idx = self._add(eng, fn, reads, writes)
        writes = list(writes) + [b for b in reads if b.excl]
        reads = [b for b in reads if not b.excl]
        tok = ("e", eng, idx)
        for b in reads:
            b.r[("e", eng)] = tok
        for b in writes:
            b.w = tok
            b.r = {}
        return tok

    def dma(self, eng, fn, semidx, reads=(), writes=()):
        extra = []
        if self.dma_sems[semidx] > 0:
            extra.append(("d", semidx, self.dma_sems[semidx]))
        op, idx = self._add(eng, fn, reads, writes, extra)
        self.dma_sems[semidx] += 16
        op.dma_sem = semidx
        op.dma_val = self.dma_sems[semidx]
        tok = ("d", semidx, op.dma_val)
        for b in reads:
            b.r[("d", semidx)] = tok
        for b in writes:
            b.w = tok
            b.r = {}
        return tok

    def emit(self, final_wait_tokens=()):
        nc = self.nc
        with ExitStack() as st:
            esem = {e: st.enter_context(nc.semaphore("s_" + e)) for e in ENGS}
            dsem = [st.enter_context(nc.semaphore("d%d" % i)) for i in range(len(self.dma_sems))]
            val = {}
            for e in ENGS:
                c = 0
                v = []
                for op in self.ops[e]:
                    if op.inc:
                        c += 1
                    v.append(c)
                val[e] = v

            def run(e, engine):
                waited = {}
                for op in self.ops[e]:
                    for t in op.deps:
                        if t[0] == "e":
                            key = ("e", t[1])
                            need = val[t[1]][t[2]]
                            sem = esem[t[1]]
                        else:
                            key = ("d", t[1])
                            need = t[2]
                            sem = dsem[t[1]]
                        if waited.get(key, 0) >= need:
                            continue
                        waited[key] = need
                        engine.wait_ge(sem, need)
                    ins = op.fn(engine)
                    if op.dma_sem is not None:
                        ins.then_inc(dsem[op.dma_sem], 16)
                    elif op.inc:
                        ins.then_inc(esem[e], 1)
                if e == "sp":
                    for t in final_wait_tokens:
                        if t[0] == "d":
                            engine.wait_ge(dsem[t[1]], t[2])
                        else:
                            engine.wait_ge(esem[t[1]], val[t[1]][t[2]])

            with nc.Block() as block:
                @block.tensor
                def _(eng):
                    run("pe", eng)

                @block.scalar
                def _(eng):
                    run("act", eng)

                @block.vector
                def _(eng):
                    run("dve", eng)

                @block.gpsimd
                def _(eng):
                    run("pool", eng)

                @block.sync
                def _(eng):
                    run("sp", eng)


class PV:
    def __init__(self):
        self.cols = []
        self.off = {}
        self.n = 0

    def add(self, name, vec):
        vec = np.asarray(vec, np.float32).reshape(-1)
        assert vec.size % 128 == 0, (name, vec.size)
        m = vec.size // 128
        self.cols.append(vec.reshape(m, 128).T)
        self.off[name] = (self.n, m)
        self.n += m

    def addraw(self, name, arr):
        arr = np.asarray(arr, np.float32)
        self.cols.append(arr)
        self.off[name] = (self.n, arr.shape[1])
        self.n += arr.shape[1]

    def build(self):
        return np.ascontiguousarray(np.concatenate(self.cols, axis=1))


class WL:
    def __init__(self):
        self.blocks = []
        self.off = {}
        self.n = 0

    def add_raw(self, name, arrs):
        lst = []
        for arr in arrs:
            arr = np.asarray(arr, np.float32)
            assert arr.shape[0] == 128
            lst.append((self.n, arr.shape[1], 0, 0, 0, 0))
            self.blocks.append(arr)
            self.n += arr.shape[1]
        self.off[name] = lst

    def add_mat(self, name, W, eb=128, kmax=24):
        Kd, E = W.shape
        assert Kd % 128 == 0
        KC = Kd // 128
        nk = (KC + kmax - 1) // kmax
        ksz = (KC + nk - 1) // nk
        kr = [(a, min(a + ksz, KC)) for a in range(0, KC, ksz)]
        Wr = W.reshape(KC, 128, E)
        lst = []
        for e0 in range(0, E, eb):
            e1 = min(e0 + eb, E)
            for (a, b) in kr:
                blk = Wr[a:b, :, e0:e1].transpose(1, 0, 2).reshape(128, (b - a) * (e1 - e0))
                lst.append((self.n, blk.shape[1], a, b, e0, e1))
                self.blocks.append(blk)
                self.n += blk.shape[1]
        self.off[name] = lst

    def build(self):
        return np.ascontiguousarray(np.concatenate(self.blocks, axis=1))


def wl_layout(name_shapes, eb_map):
    off = {}
    n = 0
    for name, (Kd, E) in name_shapes:
        if Kd == "raw":
            lst = []
            for sz in E:
                lst.append((n, sz, 0, 0, 0, 0))
                n += sz
            off[name] = lst
            continue
        eb = eb_map.get(name, 128)
        kmax = 24
        KC = Kd // 128
        nk = (KC + kmax - 1) // kmax
        ksz = (KC + nk - 1) // nk
        kr = [(a, min(a + ksz, KC)) for a in range(0, KC, ksz)]
        lst = []
        for e0 in range(0, E, eb):
            e1 = min(e0 + eb, E)
            for (a, b) in kr:
                sz = (b - a) * (e1 - e0)
                lst.append((n, sz, a, b, e0, e1))
                n += sz
        off[name] = lst
    return off, n


NL_DEFAULT = 4


SI = 4096


def use_m(mixers):
    return mixers is True or (isinstance(mixers, str) and "m" in mixers)


def use_r(mixers):
    return mixers is True or (isinstance(mixers, str) and "r" in mixers)


def weight_names(nl, mixers=True):
    names = []
    for l in range(nl):
        if use_r(mixers) and l % 2 == 0:
            names.append(("r%d_w1" % l, (D, 96)))
            names.append(("r%d_a1" % l, (D, 96)))
            names.append(("r%d_g1" % l, (D, 256)))
            if l >= 2:
                names.append(("r%d_v1" % l, (D, 64)))
            names.append(("r%d_wv" % l, (D, D)))
            names.append(("r%d_wr" % l, (D, D)))
            names.append(("r%d_wk" % l, (D, D)))
            names.append(("r%d_l2" % l, ("raw", [320] * 32)))
            names.append(("r%d_wo" % l, ("raw", [2048] * 32)))
        if use_m(mixers) and l % 2 == 1:
            names.append(("m%d_dt" % l, (D, 64)))
            for g in range(8):
                names.append(("m%d_z%d" % (l, g), (D, 512)))
                names.append(("m%d_x%d" % (l, g), (D, 512)))
                names.append(("m%d_B%d" % (l, g), (D, 128)))
                names.append(("m%d_C%d" % (l, g), (D, 128)))
            names.append(("m%d_out" % l, (SI, D)))
        names.append(("ffn_in%d" % l, (D, 2 * FH)))
        names.append(("ffn_out%d" % l, (FH, D)))
    return names


def pack_host(inp, b, nl, mixers=True):
    pv = PV()
    pv.add("c", inp["c"][b])
    pv.add("ada_b", inp["ada_b"])
    for l in range(nl):
        pv.add("tab%d" % l, inp["ada_table"][l])
        pv.add("nmpre%d" % l, inp["norm_mix_pre"][l])
        pv.add("nmpost%d" % l, inp["norm_mix_post"][l])
        pv.add("nfpre%d" % l, inp["norm_ffn_pre"][l])
        pv.add("nfpost%d" % l, inp["norm_ffn_post"][l])
        cw = inp["ffn_conv_w"][l]
        for j in range(3):
            pv.add("fcw%d_%d" % (l, j), cw[j])
        pv.add("fcb%d" % l, inp["ffn_conv_b"][l])
        if use_m(mixers) and l % 2 == 1:
            m = l // 2
            for j in range(4):
                pv.add("mcw%d_%d" % (l, j), inp["ssm_conv_w"][m][j])
            pv.add("mcb%d" % l, inp["ssm_conv_b"][m])
            pv.add("mnw%d" % l, inp["ssm_norm"][m])
            pv.addraw("mD%d" % l, np.tile(inp["ssm_d"][m][None, :], (128, 1)))
            pv.addraw("mdtb%d" % l, np.tile(inp["ssm_dt_bias"][m][None, :], (128, 1)))
            pv.addraw("malog%d" % l, np.tile(inp["ssm_a_log"][m][None, :], (128, 1)))
        if use_r(mixers) and l % 2 == 0:
            ri = l // 2

            def a64(name, vec):
                arr = np.asarray(vec, np.float32).reshape(32, 64).T
                pv.addraw(name, np.concatenate([arr, arr], axis=0))
            for i in range(6):
                pv.add("rmu%d_%d" % (l, i), inp["rwkv_mu"][ri][i])
            a64("rw0_%d" % l, inp["rwkv_w0"][ri])
            a64("ra0_%d" % l, inp["rwkv_a0"][ri])
            a64("rkk_%d" % l, inp["rwkv_k_k"][ri])
            a64("rka_%d" % l, inp["rwkv_k_a"][ri])
            a64("rrk_%d" % l, inp["rwkv_r_k"][ri])
            a64("rlw_%d" % l, inp["rwkv_ln_w"][ri])
            a64("rlb_%d" % l, inp["rwkv_ln_b"][ri])
            if l >= 2:
                a64("rv0_%d" % l, inp["rwkv_v0"][ri - 1])
    tri = np.triu(np.ones((64, 64), np.float32))
    pv.addraw("tri", np.concatenate([tri, tri], axis=0))
    pv.addraw("ident", np.eye(128, dtype=np.float32))
    return pv


def pack_weights(inp, nl, mixers=True):
    wl = WL()
    for l in range(nl):
        if use_r(mixers) and l % 2 == 0:
            ri = l // 2
            wl.add_mat("r%d_w1" % l, inp["rwkv_w1"][ri], eb=96)
            wl.add_mat("r%d_a1" % l, inp["rwkv_a1"][ri], eb=96)
            wl.add_mat("r%d_g1" % l, inp["rwkv_g1"][ri], eb=128)
            if l >= 2:
                wl.add_mat("r%d_v1" % l, inp["rwkv_v1"][ri - 1], eb=64)
            wl.add_mat("r%d_wv" % l, inp["rwkv_w_rkv"][ri][2], eb=64)
            wl.add_mat("r%d_wr" % l, inp["rwkv_w_rkv"][ri][0], eb=64)
            wl.add_mat("r%d_wk" % l, inp["rwkv_w_rkv"][ri][1], eb=64)
            blks = []
            for h in range(32):
                hs_ = slice(h * 64, (h + 1) * 64)
                b_ = np.zeros((128, 320), np.float32)
                b_[0:96, 0:64] = inp["rwkv_w2"][ri][:, hs_]
                b_[0:96, 64:128] = inp["rwkv_a2"][ri][:, hs_]
                b_[:, 128:192] = inp["rwkv_g2"][ri][0:128, hs_]
                b_[:, 192:256] = inp["rwkv_g2"][ri][128:256, hs_]
                if l >= 2:
                    b_[0:64, 256:320] = inp["rwkv_v2"][ri - 1][:, hs_]
                blks.append(b_)
            wl.add_raw("r%d_l2" % l, blks)
            Wo = inp["rwkv_w_o"][ri].reshape(32, 64, 16, 128)
            blks = []
            for i in range(16):
                for q in range(2):
                    b_ = np.zeros((128, 2048), np.float32)
                    b_[0:64, :] = Wo[q * 16:(q + 1) * 16, :, i, :].transpose(1, 0, 2).reshape(64, 2048)
                    blks.append(b_)
            wl.add_raw("r%d_wo" % l, blks)
        if use_m(mixers) and l % 2 == 1:
            m = l // 2
            W = inp["ssm_w_in"][m]
            wl.add_mat("m%d_dt" % l, W[:, 2 * SI + 2048:], eb=64)
            for g in range(8):
                wl.add_mat("m%d_z%d" % (l, g), W[:, g * 512:(g + 1) * 512])
                wl.add_mat("m%d_x%d" % (l, g), W[:, SI + g * 512:SI + (g + 1) * 512])
                wl.add_mat("m%d_B%d" % (l, g), W[:, 2 * SI + g * 128:2 * SI + (g + 1) * 128])
                wl.add_mat("m%d_C%d" % (l, g), W[:, 2 * SI + 1024 + g * 128:2 * SI + 1024 + (g + 1) * 128])
            wl.add_mat("m%d_out" % l, inp["ssm_w_out"][m])
        wl.add_mat("ffn_in%d" % l, inp["ffn_w_in"][l])
        wl.add_mat("ffn_out%d" % l, inp["ffn_w_out"][l])
    return wl


def pack_ada(inp):
    W = inp["ada_w"].reshape(16, 128, 96, 128)
    A = W.transpose(2, 0, 1, 3)
    A = A.reshape(96, 2, 8, 128, 128).transpose(3, 0, 1, 2, 4)
    return np.ascontiguousarray(A.reshape(128, 96 * 2 * 8 * 128))


def build_nc(T, nl, pvoff, npv, mixers=True):
    NT = T // TT
    ebm = {("m%d_dt" % l): 64 for l in range(nl)}
    for l in range(nl):
        ebm["r%d_w1" % l] = 96
        ebm["r%d_a1" % l] = 96
        ebm["r%d_v1" % l] = 64
        for nm in ("wr", "wk", "wv"):
            ebm["r%d_%s" % (l, nm)] = 64
    woff, WTOT = wl_layout(weight_names(nl, mixers), ebm)
    nc = bass.Bass("TRN2", target_bir_lowering=False)
    x_d = nc.dram_tensor("x", [D, T], F32, kind="ExternalInput").ap()
    pv_d = nc.dram_tensor("pv", [128, npv], F32, kind="ExternalInput").ap()
    wall_d = nc.dram_tensor("wall", [128, WTOT], F32, kind="ExternalInput").ap()
    ada_d = nc.dram_tensor("adaw", [128, 96 * 2048], F32, kind="ExternalInput").ap()
    y_d = nc.dram_tensor("y", [D, T], F32, kind="ExternalOutput").ap()
    _wn = weight_names(nl, mixers)
    _first = {}
    for _name, _ in _wn:
        _l = int(_name.split("_")[0][1:]) if _name[0] in "mr" else int(_name[-1])
        if _l not in _first:
            _first[_l] = woff[_name][0][0]
    seg_starts = sorted(_first.values())
    seg_ends = seg_starts[1:] + [WTOT]
    wbf_segs = [nc.dram_tensor("wbf%d" % i, [128, b - a], BF16, kind="Internal").ap()
                for i, (a, b) in enumerate(zip(seg_starts, seg_ends))]

    def wbf_ap(o, sz):
        for (a, b, t) in zip(seg_starts, seg_ends, wbf_segs):
            if a <= o and o + sz <= b:
                return t[:, o - a:o - a + sz]
        raise AssertionError((o, sz))
    fw = FW(nc)
    st = ExitStack()

    def sb(name, shape, dt):
        return st.enter_context(nc.sbuf_tensor(name, shape, dt))

    def ps(name, shape, dt=F32):
        return st.enter_context(nc.psum_tensor(name, shape, dt))

    with st:
        pvs = sb("pvs", [128, npv], F32)
        x_sb = sb("x_sb", [128, DC, TT], F32)
        h_sb = sb("h_sb", [128, DC, TT], BF16)
        SCRF = 15872
        scr = sb("scr", [128, SCRF], F32)
        scr_bf = scr[:].bitcast(BF16)
        scr2 = sb("scr2", [128, 4096], F32)
        f_flat = scr2[:].bitcast(BF16)
        lm_sb = scr2[:, 0:nl * 96].rearrange("p (l n) -> p l n", l=nl)
        a_flat = scr_bf[:, 0:FHC * TT]

        class _V3:
            def __init__(self, flat, n):
                self.flat, self.n = flat, n

            def __getitem__(self, key):
                p, j, t = key
                assert isinstance(j, int)
                return self.flat[p, j * self.n:(j + 1) * self.n][:, t]

        f_sb = _V3(f_flat, TT)
        a_sb = _V3(a_flat, TT)
        sq_sb = sb("sq_sb", [128, 2, TT], BF16)
        rstd_sb = sb("rstd_sb", [128, TT], F32)
        tmp_sb = sb("tmp_sb", [128, 2, TT], F32)
        ub_sb = sb("ub_sb", [128, 4, TT + 3], F32)
        cv_sb = sb("cv_sb", [128, 4, TT], BF16)
        sg_sb = sb("sg_sb", [128, 2, TT], BF16)
        ring = sb("ring", [128, NSLOT, SLOT], BF16)
        ones_bf = sb("ones_bf", [128, 128], BF16)
        mod_sb = sb("mod_sb", [128, 96], F32)
        coef = sb("coef", [128, nl, 6, DC], F32)
        sc_sb = sb("sc_sb", [128, DC], F32)
        fcar = sb("fcar", [128, nl, 2 * FHC, 2], F32)
        PS = [ps("ps%d" % i, [128, 512]) for i in range(7)]
        PT = ps("pt", [128, 1024], BF16)

        B_pv = Buf("pv")
        B_x = Buf("x")
        B_h = Buf("h")
        B_f = [Buf("f%d" % i) for i in range(DC)]
        B_a = [Buf("a%d" % i) for i in range(FHC)]
        B_sq = [Buf("sq0"), Buf("sq1")]
        B_rstd = Buf("rstd")
        B_tmp = [Buf("tmp%d" % i) for i in range(2)]
        B_ub = [Buf("ub%d" % i) for i in range(4)]
        B_cv = [Buf("cv%d" % i) for i in range(4)]
        B_sg = [Buf("sg%d" % i) for i in range(2)]
        B_ring = [Buf("ring%d" % i) for i in range(NSLOT)]
        B_ps = [Buf("ps%d" % i, excl=True) for i in range(7)]
        _bpt = Buf("pt", excl=True)
        B_pt = [_bpt, _bpt]
        B_const = Buf("const")
        B_mod = Buf("mod")
        B_coef = Buf("coef")
        B_fcar = Buf("fcar")
        ring_sem = [fw.new_dma_sem() for _ in range(NSLOT)]
        sem_misc = fw.new_dma_sem()
        sem_x = fw.new_dma_sem()
        sem_y = fw.new_dma_sem()

        def P(name, j=0, n=1):
            o, m = pvoff[name]
            return pvs[:, o + j:o + j + n]

        fw.dma("sp", lambda e: e.dma_start(out=pvs[:], in_=pv_d[:, :]), sem_misc, writes=[B_pv])
        fw.op("pool", lambda e: e.memset(ones_bf[:], 1.0), writes=[B_const])
        fw.op("pool", lambda e: e.memset(fcar[:], 0.0), writes=[B_fcar])

        CH = 32768
        pre_chunks = []
        pre_sems = [fw.new_dma_sem() for _ in range(4)]
        ci = 0
        for (sa, sbnd) in zip(seg_starts, seg_ends):
            for c0 in range(sa, sbnd, CH):
                c1 = min(c0 + CH, sbnd)
                bb = Buf("pre%d" % ci)
                fw.dma("pool", lambda e, c0=c0, c1=c1: e.dma_start(out=wbf_ap(c0, c1 - c0), in_=wall_d[:, c0:c1]),
                       pre_sems[ci % 4], writes=[bb])
                pre_chunks.append((c0, c1, bb))
                ci += 1

        def pre_bufs(o, sz):
            return [bb for (c0, c1, bb) in pre_chunks if c0 < o + sz and c1 > o]

        ring_state = {"i": 0}

        def load_block(o, sz):
            s = ring_state["i"] % NSLOT
            ring_state["i"] += 1
            fw.dma("sp", lambda e, s=s, o=o, sz=sz: e.dma_start(out=ring[:, s, 0:sz], in_=wbf_ap(o, sz)),
                   ring_sem[s], reads=pre_bufs(o, sz), writes=[B_ring[s]])
            return s

        fw.op("act", lambda e: e.activation(out=sc_sb[:], in_=P("c", 0, DC), func=AF.Silu),
              reads=[B_pv], writes=[B_mod])
        stg = scr[:, 0:2048].rearrange("p (s n) -> p s n", s=2)
        B_stg = [Buf("stg0"), Buf("stg1")]
        stg_sem = [fw.new_dma_sem(), fw.new_dma_sem()]
        k = 0
        for j in range(96):
            for half in range(2):
                s = k % 2
                o = (j * 2 + half) * 1024
                fw.dma("sp", lambda e, s=s, o=o: e.dma_start(out=stg[:, s, :], in_=ada_d[:, o:o + 1024]),
                       stg_sem[s], writes=[B_stg[s]])
                for kc in range(8):
                    kk = half * 8 + kc
                    fw.op("pe", lambda e, s=s, kc=kc, kk=kk, j=j: e.matmul(
                        PS[0][:, j:j + 1], stg[:, s, kc * 128:(kc + 1) * 128], sc_sb[:, kk:kk + 1],
                        start=(kk == 0), stop=(kk == 15)),
                        reads=[B_stg[s], B_mod], writes=[B_ps[0]])
                k += 1
        fw.op("dve", lambda e: e.tensor_tensor(out=mod_sb[:], in0=PS[0][:, 0:96], in1=P("ada_b", 0, 96), op=ALU.add),
              reads=[B_ps[0], B_pv], writes=[B_mod])
        for l in range(nl):
            fw.op("dve", lambda e, l=l: e.tensor_tensor(out=lm_sb[:, l, :], in0=mod_sb[:], in1=P("tab%d" % l, 0, 96), op=ALU.add),
                  reads=[B_mod, B_pv], writes=[B_coef])
            for (half, pre, post) in ((0, "nmpre", "nmpost"), (1, "nfpre", "nfpost")):
                base = half * 48
                fw.op("dve", lambda e, l=l, base=base, half=half, pre=pre: e.scalar_tensor_tensor(
                    out=coef[:, l, half * 3 + 0, :], in0=lm_sb[:, l, base + 16:base + 32], scalar=1.0,
                    in1=P("%s%d" % (pre, l), 0, DC), op0=ALU.add, op1=ALU.mult),
                    reads=[B_coef, B_pv], writes=[B_coef])
                fw.op("dve", lambda e, l=l, base=base, half=half: e.tensor_copy(
                    out=coef[:, l, half * 3 + 1, :], in_=lm_sb[:, l, base:base + 16]),
                    reads=[B_coef], writes=[B_coef])
                fw.op("dve", lambda e, l=l, base=base, half=half, post=post: e.tensor_tensor(
                    out=coef[:, l, half * 3 + 2, :], in0=lm_sb[:, l, base + 32:base + 48],
                    in1=P("%s%d" % (post, l), 0, DC), op=ALU.mult),
                    reads=[B_coef, B_pv], writes=[B_coef])

        def rstd_from(psb, n=float(D)):
            fw.op("dve", lambda e: e.tensor_scalar(out=rstd_sb[:], in0=PS[psb][:], scalar1=1.0 / n, scalar2=1e-6,
                                                   op0=ALU.mult, op1=ALU.add),
                  reads=[B_ps[psb]], writes=[B_rstd])
            fw.op("act", lambda e: e.activation(out=rstd_sb[:], in_=rstd_sb[:], func=AF.Sqrt),
                  reads=[B_rstd], writes=[B_rstd])
            fw.op("dve", lambda e: e.reciprocal(out=rstd_sb[:], in_=rstd_sb[:]),
                  reads=[B_rstd], writes=[B_rstd])

        def prenorm(l, half):
            for j in range(DC):
                s = j % 2
                fw.op("pool", lambda e, j=j, s=s: e.tensor_tensor(out=sq_sb[:, s, :], in0=x_sb[:, j, :], in1=x_sb[:, j, :], op=ALU.mult),
                      reads=[B_x], writes=[B_sq[s]])
                fw.op("pe", lambda e, j=j, s=s: e.matmul(PS[6][:], ones_bf[:], sq_sb[:, s, :], start=(j == 0), stop=(j == DC - 1)),
                      reads=[B_sq[s], B_const], writes=[B_ps[6]])
            rstd_from(6)
            for j in range(DC):
                s = j % 2
                fw.op("pool", lambda e, j=j, s=s: e.tensor_tensor(out=tmp_sb[:, s, :], in0=x_sb[:, j, :], in1=rstd_sb[:], op=ALU.mult),
                      reads=[B_x, B_rstd], writes=[B_tmp[s]])
                fw.op("dve", lambda e, j=j, s=s: e.tensor_scalar(
                    out=h_sb[:, j, :], in0=tmp_sb[:, s, :], scalar1=coef[:, l, half * 3 + 0, j:j + 1],
                    scalar2=coef[:, l, half * 3 + 1, j:j + 1], op0=ALU.mult, op1=ALU.add),
                    reads=[B_tmp[s], B_coef], writes=[B_h])

        def postnorm_residual(l, half):
            for j in range(DC):
                s = j % 2
                fw.op("pool", lambda e, j=j, s=s: e.tensor_tensor(out=sq_sb[:, s, :], in0=f_sb[:, j, :], in1=f_sb[:, j, :], op=ALU.mult),
                      reads=[B_f[j]], writes=[B_sq[s]])
                fw.op("pe", lambda e, j=j, s=s: e.matmul(PS[6][:], ones_bf[:], sq_sb[:, s, :], start=(j == 0), stop=(j == DC - 1)),
                      reads=[B_sq[s], B_const], writes=[B_ps[6]])
            rstd_from(6)
            for j in range(DC):
                s = j % 2
                fw.op("pool", lambda e, j=j, s=s: e.tensor_tensor(out=tmp_sb[:, s, :], in0=f_sb[:, j, :], in1=rstd_sb[:], op=ALU.mult),
                      reads=[B_f[j], B_rstd], writes=[B_tmp[s]])
                fw.op("dve", lambda e, j=j, s=s: e.scalar_tensor_tensor(
                    out=x_sb[:, j, :], in0=tmp_sb[:, s, :], scalar=coef[:, l, half * 3 + 2, j:j + 1],
                    in1=x_sb[:, j, :], op0=ALU.mult, op1=ALU.add),
                    reads=[B_tmp[s], B_coef, B_x], writes=[B_x])

        def ffn(l):
            blocks_in = woff["ffn_in%d" % l]
            blocks_out = woff["ffn_out%d" % l]
            for jp in range(FHC):
                for gv in range(2):
                    bi = gv * FHC + jp
                    o, sz, ka, kb, e0, e1 = blocks_in[bi]
                    s = load_block(o, sz)
                    pb = (jp * 2 + gv) % 4
                    for kc in range(DC):
                        fw.op("pe", lambda e, s=s, kc=kc, pb=pb: e.matmul(
                            PS[pb][:], ring[:, s, kc * 128:(kc + 1) * 128], h_sb[:, kc, :],
                            start=(kc == 0), stop=(kc == DC - 1)),
                            reads=[B_ring[s], B_h], writes=[B_ps[pb]])
                    u = pb
                    fw.op("pool", lambda e, u=u, bi=bi: e.tensor_copy(out=ub_sb[:, u, 0:2], in_=fcar[:, l, bi, :]),
                          reads=[B_fcar], writes=[B_ub[u]])
                    fw.op("dve", lambda e, u=u, pb=pb: e.tensor_copy(out=ub_sb[:, u, 2:TT + 2], in_=PS[pb][:]),
                          reads=[B_ps[pb]], writes=[B_ub[u]])
                    fw.op("pool", lambda e, u=u, bi=bi: e.tensor_copy(out=fcar[:, l, bi, :], in_=ub_sb[:, u, TT:TT + 2]),
                          reads=[B_ub[u]], writes=[B_fcar])
                    eng1 = "dve"
                    fw.op(eng1, lambda e, u=u, bi=bi: e.tensor_scalar(
                        out=cv_sb[:, u, :], in0=ub_sb[:, u, 2:TT + 2], scalar1=P("fcw%d_2" % l, bi), scalar2=P("fcb%d" % l, bi),
                        op0=ALU.mult, op1=ALU.add), reads=[B_ub[u], B_pv], writes=[B_cv[u]])
                    fw.op(eng1, lambda e, u=u, bi=bi: e.scalar_tensor_tensor(
                        out=cv_sb[:, u, :], in0=ub_sb[:, u, 1:TT + 1], scalar=P("fcw%d_1" % l, bi), in1=cv_sb[:, u, :],
                        op0=ALU.mult, op1=ALU.add), reads=[B_ub[u], B_pv, B_cv[u]], writes=[B_cv[u]])
                    fw.op(eng1, lambda e, u=u, bi=bi: e.scalar_tensor_tensor(
                        out=cv_sb[:, u, :], in0=ub_sb[:, u, 0:TT], scalar=P("fcw%d_0" % l, bi), in1=cv_sb[:, u, :],
                        op0=ALU.mult, op1=ALU.add), reads=[B_ub[u], B_pv, B_cv[u]], writes=[B_cv[u]])
                ug = (jp * 2) % 4
                uv = ug + 1
                sgi = jp % 2
                fw.op("act", lambda e, ug=ug, sgi=sgi: e.activation(out=sg_sb[:, sgi, :], in_=cv_sb[:, ug, :], func=AF.Silu),
                      reads=[B_cv[ug]], writes=[B_sg[sgi]])
                fw.op("pool", lambda e, uv=uv, sgi=sgi, jp=jp: e.tensor_tensor(
                    out=a_sb[:, jp, :], in0=sg_sb[:, sgi, :], in1=cv_sb[:, uv, :], op=ALU.mult),
                    reads=[B_sg[sgi], B_cv[uv]], writes=[B_a[jp]])
            nb = len(blocks_out) // DC
            for i in range(DC):
                pb = 4 + (i % 2)
                first = True
                for q in range(nb):
                    o, sz, ka, kb, e0, e1 = blocks_out[i * nb + q]
                    s = load_block(o, sz)
                    for kc in range(ka, kb):
                        last = (q == nb - 1 and kc == kb - 1)
                        fw.op("pe", lambda e, s=s, kc=kc, ka=ka, pb=pb, first=first, last=last: e.matmul(
                            PS[pb][:], ring[:, s, (kc - ka) * 128:(kc - ka + 1) * 128], a_sb[:, kc, :],
                            start=first, stop=last),
                            reads=[B_ring[s], B_a[kc]], writes=[B_ps[pb]])
                        first = False
                fw.op("dve", lambda e, i=i, pb=pb: e.tensor_copy(out=f_sb[:, i, :], in_=PS[pb][:]),
                      reads=[B_ps[pb]], writes=[B_f[i]])
            postnorm_residual(l, 1)


        ident_bf = sb("ident_bf", [128, 128], BF16)
        negm = sb("negm", [128, 64], F32)
        abc = sb("abc", [128, 2, 64], F32)
        mcar = sb("mcar", [128, 2, 48, 3], F32)
        B_mcar = Buf("mcar")
        sst_d = nc.dram_tensor("sst", [2, 8, 128, 512], F32, kind="Internal").ap()
        sem_st = [fw.new_dma_sem(), fw.new_dma_sem()]
        B_sst = [[Buf("sst%d_%d" % (m_, g_)) for g_ in range(8)] for m_ in range(2)]
        fw.op("pool", lambda e: e.memset(negm[:], 0.0), writes=[B_const])
        import os as _os3
        for _k in range(int(_os3.environ.get("DBG_PAD", "0"))):
            fw.op("dve", lambda e: e.memset(negm[:], 0.0), writes=[B_const])
        fw.op("pool", lambda e: e.memset(mcar[:], 0.0), writes=[B_mcar])
        fw.op("dve", lambda e: e.tensor_copy(out=ident_bf[:], in_=P("ident", 0, 128)), reads=[B_pv], writes=[B_const])
        fw.op("dve", lambda e: e.tensor_scalar(out=negm[:], in0=P("tri", 0, 64), scalar1=1e30, scalar2=-1e30,
                                               op0=ALU.mult, op1=ALU.add), reads=[B_pv], writes=[B_const])
        fw.op("dve", lambda e: e.tensor_copy(out=tri_bf[:], in_=P("tri", 0, 64)), reads=[B_pv], writes=[B_const])
        if use_m(mixers):
            for l_ in range(1, nl, 2):
                fw.op("act", lambda e, l_=l_: e.activation(out=abc[:, l_ // 2, :], in_=P("malog%d" % l_, 0, 64), func=AF.Exp),
                      reads=[B_pv], writes=[B_const])
                fw.op("dve", lambda e, l_=l_: e.tensor_scalar(out=abc[:, l_ // 2, :], in0=abc[:, l_ // 2, :], scalar1=-1.0,
                                                               scalar2=None, op0=ALU.mult), reads=[B_const], writes=[B_const])

        y_all = _V3(scr_bf[:, 0:32 * TT], TT)
        B_yall = [Buf("yall%d" % i) for i in range(32)]
        o_ = [8192]

        def carve_f(n):
            v = scr[:, o_[0]:o_[0] + n]
            o_[0] += n
            return v

        def carve_b(n):
            v = scr_bf[:, 2 * o_[0]:2 * o_[0] + n]
            o_[0] += (n + 1) // 2
            return v

        sz_sb = _V3(carve_b(4 * TT), TT)
        xc_sb = _V3(carve_b(6 * TT), TT)
        S_sb = carve_f(512)
        Sbf_sb = carve_b(512)
        E_sb = carve_b(512)
        Mt_sb = carve_b(512)
        xtok_sb = carve_b(640)
        xdt_sb = carve_b(512)
        xw_sb = carve_b(512)
        ybf_sb = carve_b(512)
        ahi_sb = carve_b(512)
        alo_sb = carve_b(512)
        r1h_sb = carve_b(512)
        r1l_sb = carve_b(512)
        tri_bf = sb("tri_bf", [128, 64], BF16)
        B_ahl = Buf("ahl")
        B_r1 = Buf("r1")
        assert o_[0] <= SCRF, o_[0]
        dt_sb = scr2[:, 0:512]
        adt_sb = scr2[:, 512:1024]
        acs_sb = scr2[:, 1024:1536]
        edec_sb = scr2[:, 1536:2048]
        wend_sb = scr2[:, 2048:2560]
        eacs_sb = scr2[:, 2560:3072]
        yt_sb = scr2[:, 3072:3584]
        t2_sb = scr2[:, 3584:4096]
        B_sz = [Buf("sz%d" % i) for i in range(4)]
        B_xc = [Buf("xc%d" % i) for i in range(6)]
        B_S, B_Sbf, B_E, B_Mt, B_xtok, B_xdt, B_xw, B_ybf = [Buf(n) for n in ("S", "Sbf", "E", "Mt", "xtok", "xdt", "xw", "ybf")]
        B_dt, B_adt, B_acs, B_edec, B_wend, B_eacs, B_yt, B_t2 = [Buf(n) for n in ("dt", "adt", "acs", "edec", "wend", "eacs", "yt", "t2")]

        def v3(ap, a, b):
            return ap.rearrange("p (a b) -> p a b", a=a, b=b)

        def bc_last(ap2, a, b):
            return ap2.unsqueeze(2).to_broadcast([ap2.shape[0], a, b])

        def bc_mid(ap2, a, b):
            return ap2.unsqueeze(1).to_broadcast([ap2.shape[0], a, b])

        def proj16(blk, pb, rhs_sb=None):
            o, sz, ka, kb, e0, e1 = blk
            s = load_block(o, sz)
            w = e1 - e0
            for kc in range(DC):
                fw.op("pe", lambda e, s=s, kc=kc, pb=pb, w=w: e.matmul(
                    PS[pb][0:w, :], ring[:, s, kc * w:(kc + 1) * w], h_sb[:, kc, :],
                    start=(kc == 0), stop=(kc == DC - 1)),
                    reads=[B_ring[s], B_h], writes=[B_ps[pb]])

        def mamba(l, it):
            m = l // 2
            import os as _os
            _st0 = float(_os.environ.get("MAMBA_STOP", "9"))
            if _st0 <= 0:
                return
            o, sz, ka, kb, e0, e1 = woff["m%d_dt" % l][0]
            s = load_block(o, sz)
            for c in range(8):
                for kc in range(DC):
                    fw.op("pe", lambda e, s=s, kc=kc, c=c: e.matmul(
                        PS[4][0:64, c * 64:(c + 1) * 64], h_sb[:, kc, c * 64:(c + 1) * 64], ring[:, s, kc * 64:(kc + 1) * 64],
                        start=(kc == 0), stop=(kc == DC - 1)),
                        reads=[B_ring[s], B_h], writes=[B_ps[4]])
            fw.op("dve", lambda e: e.tensor_tensor(out=v3(dt_sb[0:64, :], 8, 64), in0=v3(PS[4][0:64, :], 8, 64),
                                                   in1=bc_mid(P("mdtb%d" % l, 0, 64)[0:64, :], 8, 64), op=ALU.add),
                  reads=[B_ps[4], B_pv], writes=[B_dt])
            fw.op("act", lambda e: e.activation(out=dt_sb[0:64, :], in_=dt_sb[0:64, :], func=AF.Exp), reads=[B_dt], writes=[B_dt])
            fw.op("dve", lambda e: e.tensor_scalar(out=dt_sb[0:64, :], in0=dt_sb[0:64, :], scalar1=1.0, scalar2=None, op0=ALU.add),
                  reads=[B_dt], writes=[B_dt])
            fw.op("act", lambda e: e.activation(out=dt_sb[0:64, :], in_=dt_sb[0:64, :], func=AF.Ln), reads=[B_dt], writes=[B_dt])
            fw.op("dve", lambda e: e.tensor_tensor(out=v3(adt_sb[0:64, :], 8, 64), in0=v3(dt_sb[0:64, :], 8, 64),
                                                   in1=bc_mid(abc[0:64, m, :], 8, 64), op=ALU.mult),
                  reads=[B_dt, B_const], writes=[B_adt])
            if _st0 <= 0.5:
                return
            fw.op("dve", lambda e: e.tensor_copy(out=ahi_sb[0:64, :], in_=adt_sb[0:64, :]), reads=[B_adt], writes=[B_ahl])
            fw.op("dve", lambda e: e.tensor_tensor(out=alo_sb[0:64, :], in0=adt_sb[0:64, :], in1=ahi_sb[0:64, :], op=ALU.subtract),
                  reads=[B_adt, B_ahl], writes=[B_ahl])
            fw.op("pe", lambda e: e.matmul(PS[5][0:64, :], tri_bf[0:64, :], ahi_sb[0:64, :], start=True, stop=False),
                  reads=[B_ahl, B_const], writes=[B_ps[5]])
            fw.op("pe", lambda e: e.matmul(PS[5][0:64, :], tri_bf[0:64, :], alo_sb[0:64, :], start=False, stop=True),
                  reads=[B_ahl, B_const], writes=[B_ps[5]])
            fw.op("dve", lambda e: e.tensor_copy(out=acs_sb[0:64, :], in_=PS[5][0:64, :]), reads=[B_ps[5]], writes=[B_acs])
            if _st0 <= 0.6:
                return
            fw.op("pe", lambda e: e.matmul(PS[4][:, :], ones_bf[0:64, :], ahi_sb[0:64, :], start=True, stop=False),
                  reads=[B_ahl, B_const], writes=[B_ps[4]])
            fw.op("pe", lambda e: e.matmul(PS[4][:, :], ones_bf[0:64, :], alo_sb[0:64, :], start=False, stop=True),
                  reads=[B_ahl, B_const], writes=[B_ps[4]])
            fw.op("dve", lambda e: e.tensor_copy(out=tmp_sb[:, 0, :], in_=PS[4][:, :]), reads=[B_ps[4]], writes=[B_tmp[0]])
            fw.op("act", lambda e: e.activation(out=edec_sb[:, :], in_=tmp_sb[:, 0, :], func=AF.Exp), reads=[B_tmp[0]], writes=[B_edec])
            if _st0 <= 0.7:
                return
            _var = _os.environ.get("MAMBA_VAR", "")
            if "nosub" not in _var:
                fw.op("dve", lambda e: e.tensor_tensor(out=wend_sb[0:64, :], in0=tmp_sb[0:64, 0, :], in1=acs_sb[0:64, :], op=ALU.subtract),
                      reads=[B_tmp[0], B_acs], writes=[B_wend])
            if "nowend" not in _var:
                fw.op("act", lambda e: e.activation(out=wend_sb[0:64, :], in_=wend_sb[0:64, :], func=AF.Exp), reads=[B_wend], writes=[B_wend])
            if "noeacs" not in _var:
                fw.op("act", lambda e: e.activation(out=eacs_sb[0:64, :], in_=acs_sb[0:64, :], func=AF.Exp), reads=[B_acs], writes=[B_eacs])

            import os as _os
            _stop = float(_os.environ.get("MAMBA_STOP", "9"))
            if _stop <= 1:
                return
            for g in range(8):
                if it == 0 or _os.environ.get("DBG_ZSTATE"):
                    fw.op("pool", lambda e: e.memset(S_sb[:, :], 0.0), writes=[B_S])
                else:
                    fw.dma("sp", lambda e, g=g: e.dma_start(out=S_sb[:, :], in_=sst_d[m, g, :, :]), sem_st[0],
                           reads=[B_sst[m][g], B_h], writes=[B_S])
                fw.op("pool", lambda e: e.tensor_copy(out=Sbf_sb[:, :], in_=S_sb[:, :]), reads=[B_S], writes=[B_Sbf])
                for i in range(4):
                    pb = i % 2
                    proj16(woff["m%d_z%d" % (l, g)][i], pb)
                    fw.op("act", lambda e, i=i, pb=pb: e.activation(out=sz_sb[:, i, :], in_=PS[pb][:, :], func=AF.Silu),
                          reads=[B_ps[pb]], writes=[B_sz[i]])
                for i in range(6):
                    pb = i % 2
                    u = i % 4
                    if i < 4:
                        blk = woff["m%d_x%d" % (l, g)][i]
                        ci = g * 4 + i
                    elif i == 4:
                        blk = woff["m%d_B%d" % (l, g)][0]
                        ci = 32 + g
                    else:
                        blk = woff["m%d_C%d" % (l, g)][0]
                        ci = 40 + g
                    proj16(blk, pb)
                    fw.op("pool", lambda e, u=u, ci=ci: e.tensor_copy(out=ub_sb[:, u, 0:3], in_=mcar[:, m, ci, :]),
                          reads=[B_mcar], writes=[B_ub[u]])
                    fw.op("dve", lambda e, u=u, pb=pb: e.tensor_copy(out=ub_sb[:, u, 3:TT + 3], in_=PS[pb][:, :]),
                          reads=[B_ps[pb]], writes=[B_ub[u]])
                    fw.op("pool", lambda e, u=u, ci=ci: e.tensor_copy(out=mcar[:, m, ci, :], in_=ub_sb[:, u, TT:TT + 3]),
                          reads=[B_ub[u]], writes=[B_mcar])
                    fw.op("dve", lambda e, u=u, ci=ci: e.tensor_scalar(
                        out=cv_sb[:, u, :], in0=ub_sb[:, u, 3:TT + 3], scalar1=P("mcw%d_3" % l, ci), scalar2=P("mcb%d" % l, ci),
                        op0=ALU.mult, op1=ALU.add), reads=[B_ub[u], B_pv], writes=[B_cv[u]])
                    for j in (2, 1, 0):
                        fw.op("dve", lambda e, u=u, ci=ci, j=j: e.scalar_tensor_tensor(
                            out=cv_sb[:, u, :], in0=ub_sb[:, u, j:j + TT], scalar=P("mcw%d_%d" % (l, j), ci), in1=cv_sb[:, u, :],
                            op0=ALU.mult, op1=ALU.add), reads=[B_ub[u], B_pv, B_cv[u]], writes=[B_cv[u]])
                    fw.op("act", lambda e, u=u, i=i: e.activation(out=xc_sb[:, i, :], in_=cv_sb[:, u, :], func=AF.Silu),
                          reads=[B_cv[u]], writes=[B_xc[i]])
                if _stop <= 2:
                    continue
                for c in range(8):
                    cs = slice(c * 64, (c + 1) * 64)
                    hs = slice(c * 64 + g * 8, c * 64 + g * 8 + 8)
                    for i in range(5):
                        fw.op("pe", lambda e, i=i, cs=cs: e.transpose(PT[0:64, i * 128:(i + 1) * 128], xc_sb[:, i, cs], ident_bf[:, :]),
                              reads=[B_xc[i], B_const], writes=[B_pt[0]])
                    fw.op("dve", lambda e: e.tensor_copy(out=xtok_sb[0:64, :], in_=PT[0:64, 0:640]), reads=[B_pt[0]], writes=[B_xtok])
                    fw.op("dve", lambda e, hs=hs: e.tensor_tensor(out=v3(xdt_sb[0:64, :], 8, 64), in0=v3(xtok_sb[0:64, 0:512], 8, 64),
                                                           in1=bc_last(dt_sb[0:64, hs], 8, 64), op=ALU.mult),
                          reads=[B_xtok, B_dt], writes=[B_xdt])
                    fw.op("dve", lambda e, hs=hs: e.tensor_tensor(out=v3(xw_sb[0:64, :], 8, 64), in0=v3(xdt_sb[0:64, :], 8, 64),
                                                           in1=bc_last(wend_sb[0:64, hs], 8, 64), op=ALU.mult),
                          reads=[B_xdt, B_wend], writes=[B_xw])
                    fw.op("dve", lambda e, hs=hs: e.tensor_tensor(out=v3(r1h_sb[0:64, :], 8, 64), in0=bc_last(ahi_sb[0:64, hs], 8, 64),
                                                           in1=bc_mid(tri_bf[0:64, :], 8, 64), op=ALU.mult),
                          reads=[B_ahl, B_const], writes=[B_r1])
                    fw.op("dve", lambda e, hs=hs: e.tensor_tensor(out=v3(r1l_sb[0:64, :], 8, 64), in0=bc_last(alo_sb[0:64, hs], 8, 64),
                                                           in1=bc_mid(tri_bf[0:64, :], 8, 64), op=ALU.mult),
                          reads=[B_ahl, B_const], writes=[B_r1])
                    fw.op("pe", lambda e: e.matmul(PS[2][0:64, :], ones_bf[0:64, 0:64], r1h_sb[0:64, :], start=True, stop=False),
                          reads=[B_r1, B_const], writes=[B_ps[2]])
                    fw.op("pe", lambda e: e.matmul(PS[2][0:64, :], ones_bf[0:64, 0:64], r1l_sb[0:64, :], start=False, stop=True),
                          reads=[B_r1, B_const], writes=[B_ps[2]])
                    fw.op("dve", lambda e, hs=hs: e.tensor_tensor(out=v3(tmp_sb[0:64, 1, :], 8, 64), in0=v3(PS[2][0:64, :], 8, 64),
                                                           in1=bc_last(acs_sb[0:64, hs], 8, 64), op=ALU.subtract),
                          reads=[B_ps[2], B_acs], writes=[B_tmp[1]])
                    fw.op("pool", lambda e: e.tensor_tensor(out=v3(tmp_sb[0:64, 1, :], 8, 64), in0=v3(tmp_sb[0:64, 1, :], 8, 64),
                                                            in1=bc_mid(negm[0:64, :], 8, 64), op=ALU.add),
                          reads=[B_tmp[1], B_const], writes=[B_tmp[1]])
                    fw.op("act", lambda e: e.activation(out=E_sb[0:64, :], in_=tmp_sb[0:64, 1, :], func=AF.Exp),
                          reads=[B_tmp[1]], writes=[B_E])
                    fw.op("pe", lambda e, cs=cs: e.matmul(PS[3][0:64, 0:64], xc_sb[:, 4, cs], xc_sb[:, 5, cs], start=True, stop=True),
                          reads=[B_xc[4], B_xc[5]], writes=[B_ps[3]])
                    fw.op("dve", lambda e: e.tensor_tensor(out=v3(Mt_sb[0:64, :], 8, 64), in0=v3(E_sb[0:64, :], 8, 64),
                                                           in1=bc_mid(PS[3][0:64, 0:64], 8, 64), op=ALU.mult),
                          reads=[B_E, B_ps[3]], writes=[B_Mt])
                    for hh in range(8):
                        fw.op("pe", lambda e, hh=hh: e.matmul(PS[4][0:64, hh * 64:(hh + 1) * 64], Mt_sb[0:64, hh * 64:(hh + 1) * 64],
                                                              xdt_sb[0:64, hh * 64:(hh + 1) * 64], start=True, stop=True),
                              reads=[B_Mt, B_xdt], writes=[B_ps[4]])
                    fw.op("pe", lambda e, cs=cs: e.matmul(PS[5][0:64, :], xc_sb[:, 5, cs], Sbf_sb[:, :], start=True, stop=True),
                          reads=[B_xc[5], B_Sbf], writes=[B_ps[5]])
                    fw.op("dve", lambda e, hs=hs: e.tensor_tensor(out=v3(yt_sb[0:64, :], 8, 64), in0=v3(PS[5][0:64, :], 8, 64),
                                                           in1=bc_last(eacs_sb[0:64, hs], 8, 64), op=ALU.mult),
                          reads=[B_ps[5], B_eacs], writes=[B_yt])
                    fw.op("dve", lambda e: e.tensor_tensor(out=yt_sb[0:64, :], in0=yt_sb[0:64, :], in1=PS[4][0:64, :], op=ALU.add),
                          reads=[B_yt, B_ps[4]], writes=[B_yt])
                    fw.op("dve", lambda e, g=g: e.tensor_tensor(out=v3(t2_sb[0:64, :], 8, 64), in0=v3(xtok_sb[0:64, 0:512], 8, 64),
                                                           in1=bc_last(P("mD%d" % l, g * 8, 8)[0:64, :], 8, 64), op=ALU.mult),
                          reads=[B_xtok, B_pv], writes=[B_t2])
                    fw.op("dve", lambda e: e.tensor_tensor(out=ybf_sb[0:64, :], in0=yt_sb[0:64, :], in1=t2_sb[0:64, :], op=ALU.add),
                          reads=[B_yt, B_t2], writes=[B_ybf])
                    for i in range(4):
                        fw.op("pe", lambda e, i=i: e.transpose(PT[:, 640 + i * 64:640 + (i + 1) * 64], ybf_sb[0:64, i * 128:(i + 1) * 128],
                                                              ident_bf[0:64, 0:64]),
                              reads=[B_ybf, B_const], writes=[B_pt[1]])
                    for i in range(4):
                        fw.op("dve", lambda e, i=i, cs=cs, g=g: e.tensor_tensor(out=y_all[:, g * 4 + i, cs], in0=PT[:, 640 + i * 64:640 + (i + 1) * 64],
                                                                     in1=sz_sb[:, i, cs], op=ALU.mult),
                              reads=[B_pt[1], B_sz[i]], writes=[B_yall[g * 4 + i]])
                    fw.op("pe", lambda e: e.matmul(PS[3][:, :], xtok_sb[0:64, 512:640], xw_sb[0:64, :], start=True, stop=True),
                          reads=[B_xtok, B_xw], writes=[B_ps[3]])
                    fw.op("dve", lambda e, hs=hs: e.tensor_tensor(out=v3(S_sb[:, :], 8, 64), in0=v3(S_sb[:, :], 8, 64),
                                                           in1=bc_last(edec_sb[:, hs], 8, 64), op=ALU.mult),
                          reads=[B_S, B_edec], writes=[B_S])
                    fw.op("dve", lambda e: e.tensor_tensor(out=S_sb[:, :], in0=S_sb[:, :], in1=PS[3][:, :], op=ALU.add),
                          reads=[B_S, B_ps[3]], writes=[B_S])
                    fw.op("pool", lambda e: e.tensor_copy(out=Sbf_sb[:, :], in_=S_sb[:, :]), reads=[B_S], writes=[B_Sbf])
                if _os.environ.get("DBG_G") and g == int(_os.environ["DBG_G"]):
                    break
                fw.dma("sp", lambda e, g=g: e.dma_start(out=sst_d[m, g, :, :], in_=S_sb[:, :]), sem_st[1],
                       reads=[B_S], writes=[B_sst[m][g]])
                for i in range(4):
                    ci = g * 4 + i
                    sidx = i % 2
                    fw.op("pool", lambda e, ci=ci, sidx=sidx: e.tensor_tensor(out=sq_sb[:, sidx, :], in0=y_all[:, ci, :], in1=y_all[:, ci, :], op=ALU.mult),
                          reads=[B_yall[ci]], writes=[B_sq[sidx]])
                    fw.op("pe", lambda e, ci=ci, sidx=sidx: e.matmul(PS[6][:], ones_bf[:], sq_sb[:, sidx, :], start=(ci == 0), stop=(ci == 31)),
                          reads=[B_sq[sidx], B_const], writes=[B_ps[6]])
                    if not _os.environ.get("DBG_DUMP"):
                        fw.op("pool", lambda e, ci=ci: e.tensor_scalar(out=y_all[:, ci, :], in0=y_all[:, ci, :], scalar1=P("mnw%d" % l, ci),
                                                                       scalar2=None, op0=ALU.mult),
                              reads=[B_yall[ci], B_pv], writes=[B_yall[ci]])
            _dd = _os.environ.get("DBG_DUMP", "")
            if _dd == "chunk":
                lst = ((xtok_sb[0:64, 0:512], B_xtok, 512), (xtok_sb[0:64, 512:640], B_xtok, 128), (xdt_sb[0:64, :], B_xdt, 512),
                       (E_sb[0:64, :], B_E, 512), (Mt_sb[0:64, :], B_Mt, 512), (ybf_sb[0:64, :], B_ybf, 512), (yt_sb[0:64, :], B_yt, 512))
                for j, (src, bb, n) in enumerate(lst):
                    fw.op("dve", lambda e, j=j, src=src, n=n: e.tensor_copy(out=x_sb[0:64, j, 0:n], in_=src), reads=[bb, B_x], writes=[B_x])
                fw.op("dve", lambda e: e.tensor_copy(out=x_sb[:, 7, :], in_=S_sb[:, :]), reads=[B_S, B_x], writes=[B_x])
                return
            if _dd == "dts":
                for j, (src, bb) in enumerate(((dt_sb, B_dt), (acs_sb, B_acs), (wend_sb, B_wend), (eacs_sb, B_eacs))):
                    fw.op("dve", lambda e, j=j, src=src: e.tensor_copy(out=x_sb[0:64, j, :], in_=src[0:64, :]), reads=[bb, B_x], writes=[B_x])
                fw.op("dve", lambda e: e.tensor_copy(out=x_sb[:, 4, :], in_=edec_sb[:, :]), reads=[B_edec, B_x], writes=[B_x])
                return
            if _dd == "szxc":
                for j in range(4):
                    fw.op("dve", lambda e, j=j: e.tensor_copy(out=x_sb[:, j, :], in_=sz_sb[:, j, :]), reads=[B_sz[j], B_x], writes=[B_x])
                for j in range(6):
                    fw.op("dve", lambda e, j=j: e.tensor_copy(out=x_sb[:, 4 + j, :], in_=xc_sb[:, j, :]), reads=[B_xc[j], B_x], writes=[B_x])
                return
            if _dd.startswith("yall"):
                hh_ = int(_dd[4:])
                for j in range(DC):
                    fw.op("dve", lambda e, j=j: e.tensor_copy(out=x_sb[:, j, :], in_=y_all[:, hh_ * 16 + j, :]),
                          reads=[B_yall[hh_ * 16 + j], B_x], writes=[B_x])
                return
            if _stop <= 3:
                return
            rstd_from(6, float(SI))
            blocks_out = woff["m%d_out" % l]
            nb = len(blocks_out) // DC
            for i in range(DC):
                pb = i % 2
                first = True
                for q in range(nb):
                    o, sz, ka, kb, e0, e1 = blocks_out[i * nb + q]
                    s = load_block(o, sz)
                    for kc in range(ka, kb):
                        last = (q == nb - 1 and kc == kb - 1)
                        fw.op("pe", lambda e, s=s, kc=kc, ka=ka, pb=pb, first=first, last=last: e.matmul(
                            PS[pb][:], ring[:, s, (kc - ka) * 128:(kc - ka + 1) * 128], y_all[:, kc, :],
                            start=first, stop=last),
                            reads=[B_ring[s], B_yall[kc]], writes=[B_ps[pb]])
                        first = False
                fw.op("dve", lambda e, i=i, pb=pb: e.tensor_tensor(out=f_sb[:, i, :], in0=PS[pb][:], in1=rstd_sb[:], op=ALU.mult),
                      reads=[B_ps[pb], B_rstd], writes=[B_f[i]])
            postnorm_residual(l, 0)


        hcar = sb("hcar", [128, 2, DC], BF16)
        omka = sb("omka", [128, 2, 32], F32)
        mask5 = sb("mask5", [128, 320], BF16)
        B_hcar = Buf("hcar")
        rkv_d = nc.dram_tensor("rkvd", [3, 32, 64, TT], F32, kind="Internal").ap()
        vf_d = nc.dram_tensor("vfd", [32, 64, TT], F32, kind="Internal").ap()
        rst_d = nc.dram_tensor("rstd", [2, 32, 64, 64], F32, kind="Internal").ap()
        B_rkvd = [[Buf("rkvd%d_%d" % (i_, h_)) for h_ in range(32)] for i_ in range(3)]
        B_vfd = [Buf("vfd%d" % h_) for h_ in range(32)]
        B_rst = [[Buf("rst%d_%d" % (r_, h_)) for h_ in range(32)] for r_ in range(2)]
        sem_r = [fw.new_dma_sem() for _ in range(6)]
        fw.op("pool", lambda e: e.memset(hcar[:], 0.0), writes=[B_hcar])
        fw.op("dve", lambda e: e.tensor_tensor(out=mask5[:, 0:64], in0=P("tri", 0, 64), in1=P("ident", 0, 64), op=ALU.subtract),
              reads=[B_pv], writes=[B_const])
        fw.op("dve", lambda e: e.tensor_scalar(out=mask5[:, 64:128], in0=P("tri", 0, 64), scalar1=-1.0, scalar2=1.0, op0=ALU.mult, op1=ALU.add),
              reads=[B_pv], writes=[B_const])
        fw.op("dve", lambda e: e.tensor_copy(out=mask5[:, 128:192], in_=mask5[:, 0:64]), reads=[B_const], writes=[B_const])
        fw.op("dve", lambda e: e.tensor_copy(out=mask5[:, 192:256], in_=P("tri", 0, 64)), reads=[B_pv], writes=[B_const])
        fw.op("dve", lambda e: e.tensor_copy(out=mask5[:, 256:320], in_=P("tri", 0, 64)), reads=[B_pv], writes=[B_const])
        if use_r(mixers):
            for l_ in range(0, nl, 2):
                fw.op("dve", lambda e, l_=l_: e.tensor_scalar(out=omka[:, l_ // 2, :], in0=P("rka_%d" % l_, 0, 32), scalar1=-1.0, scalar2=1.0,
                                                               op0=ALU.mult, op1=ALU.add), reads=[B_pv], writes=[B_const])

        xx_sb = _V3(scr_bf[:, 0:DC * TT], TT)
        xi_sb = _V3(scr_bf[:, DC * TT:2 * DC * TT], TT)
        yh_all = _V3(scr_bf[0:64, 0:32 * TT], TT)
        ro = [8192]

        def rf(n):
            v = scr[:, ro[0]:ro[0] + n]
            ro[0] += n
            return v

        def rb(n):
            v = scr_bf[:, 2 * ro[0]:2 * ro[0] + n]
            ro[0] += (n + 1) // 2
            return v
        l1w = rb(512); l1a = rb(512); l1g = _V3(rb(1024), 512); l1v = rb(512)
        stg_r = _V3(rf(1024), 512)
        r_f = rf(512); k_f = rf(512); v_f = rf(512); lw_f = rf(512); as_f = rf(512)
        c0_f = tmp_sb[:, 0, :]; c1_f = tmp_sb[:, 1, :]
        bon_f = ub_sb[:, 0, 0:TT]; bb_f = ub_sb[:, 1, 0:TT]; k2_f = ub_sb[:, 2, 0:TT]; kkn_f = ub_sb[:, 3, 0:TT]
        rt_b = cv_sb[:, 0, :]; kt_b = cv_sb[:, 1, :]; bt_b = cv_sb[:, 2, :]; at_b = cv_sb[:, 3, :]
        kd_b = sg_sb[:, 0, :]; bd_b = sg_sb[:, 1, :]
        vb_b = rb(512); g_b = rb(512); q_b = rb(512)
        assert ro[0] <= SCRF, ro[0]
        ecl_f = scr2[:, 0:512]; t1_f = scr2[:, 512:1024]; t2_f = scr2[:, 1024:1536]; ytok_f = scr2[:, 1536:2048]
        yfm_f = scr2[:, 2048:2560]
        S_r = scr2[:, 2560:2624]; Sb_r = scr2[:, 2624:2656].bitcast(BF16)
        sc_b = scr2[:, 2656:2816].bitcast(BF16)
        tok_b = scr2[:, 2816:2912].bitcast(BF16)
        PP_b = scr2[:, 2912:2976].bitcast(BF16)
        M2_b = scr2[:, 2976:3040].bitcast(BF16)
        X_b = scr2[:, 3040:3072].bitcast(BF16); U_b = scr2[:, 3072:3104].bitcast(BF16)
        st8 = scr2[:, 3104:3168]
        yn_b = scr2[:, 3168:3424].bitcast(BF16)
        BR = {n: Buf("r_" + n) for n in ("xx xi l1w l1a l1g l1v stg0 stg1 r k v lw as kkn k2 bb c0 c1 bon rt kt bt at kd bd vb g q "
                                        "ecl t1 t2 ytok yfm S Sb sc tok PP M2 X U st8 yn").split()}
        BR["c0"], BR["c1"] = B_tmp[0], B_tmp[1]
        BR["bon"], BR["bb"], BR["k2"], BR["kkn"] = B_ub[0], B_ub[1], B_ub[2], B_ub[3]
        BR["rt"], BR["kt"], BR["bt"], BR["at"] = B_cv[0], B_cv[1], B_cv[2], B_cv[3]
        BR["kd"], BR["bd"] = B_sg[0], B_sg[1]
        B_yh = [Buf("yh%d" % i) for i in range(32)]

        def v8(ap):
            return ap.rearrange("p (a b) -> p a b", a=8, b=64)

        def proj_xi(blk, pb, w, nrows=None):
            o, sz, ka, kb, e0, e1 = blk
            s = load_block(o, sz)
            for kc in range(DC):
                fw.op("pe", lambda e, s=s, kc=kc, pb=pb, w=w: e.matmul(
                    PS[pb][0:w, :], ring[:, s, kc * w:(kc + 1) * w], xi_sb[:, kc, :],
                    start=(kc == 0), stop=(kc == DC - 1)),
                    reads=[B_ring[s], BR["xi"]], writes=[B_ps[pb]])

        def mix_input(l, i):
            for j in range(DC):
                fw.op("dve", lambda e, j=j: e.scalar_tensor_tensor(out=xi_sb[:, j, :], in0=xx_sb[:, j, :], scalar=P("rmu%d_%d" % (l, i), j),
                                                                   in1=h_sb[:, j, :], op0=ALU.mult, op1=ALU.add),
                      reads=[BR["xx"], B_h, B_pv], writes=[BR["xi"]])

        def rwkv(l, it):
            ri = l // 2
            D64 = slice(0, 64)
            fw.op("dve", lambda e: e.tensor_tensor(out=scr_bf[:, 0:DC * TT].rearrange("p (j t) -> p j t", j=DC)[:, :, 1:TT],
                                                   in0=h_sb[:, :, 0:TT - 1], in1=h_sb[:, :, 1:TT], op=ALU.subtract),
                  reads=[B_h], writes=[BR["xx"]])
            fw.op("dve", lambda e: e.tensor_tensor(out=scr_bf[:, 0:DC * TT].rearrange("p (j t) -> p j t", j=DC)[:, :, 0],
                                                   in0=hcar[:, ri, :], in1=h_sb[:, :, 0], op=ALU.subtract),
                  reads=[B_h, B_hcar], writes=[BR["xx"]])
            fw.op("dve", lambda e: e.tensor_copy(out=hcar[:, ri, :], in_=h_sb[:, :, TT - 1]), reads=[B_h], writes=[B_hcar])
            mix_input(l, 3)
            proj_xi(woff["r%d_w1" % l][0], 0, 96)
            fw.op("act", lambda e: e.activation(out=l1w[0:96, :], in_=PS[0][0:96, :], func=AF.Tanh), reads=[B_ps[0]], writes=[BR["l1w"]])
            mix_input(l, 4)
            proj_xi(woff["r%d_a1" % l][0], 1, 96)
            fw.op("dve", lambda e: e.tensor_copy(out=l1a[0:96, :], in_=PS[1][0:96, :]), reads=[B_ps[1]], writes=[BR["l1a"]])
            mix_input(l, 5)
            for q in range(2):
                proj_xi(woff["r%d_g1" % l][q], q, 128)
                fw.op("act", lambda e, q=q: e.activation(out=l1g[:, q, :], in_=PS[q][:, :], func=AF.Sigmoid), reads=[B_ps[q]], writes=[BR["l1g"]])
            for (i, nm) in ((2, "wv"), (0, "wr"), (1, "wk")):
                mix_input(l, i)
                if i == 2 and l >= 2:
                    proj_xi(woff["r%d_v1" % l][0], 0, 64)
                    fw.op("dve", lambda e: e.tensor_copy(out=l1v[0:64, :], in_=PS[0][0:64, :]), reads=[B_ps[0]], writes=[BR["l1v"]])
                for h in range(32):
                    pb = h % 2
                    proj_xi(woff["r%d_%s" % (l, nm)][h], pb, 64)
                    fw.op("dve", lambda e, pb=pb: e.tensor_copy(out=stg_r[0:64, pb, :], in_=PS[pb][0:64, :]), reads=[B_ps[pb]], writes=[BR["stg%d" % pb]])
                    fw.dma("sp", lambda e, i=i, h=h, pb=pb: e.dma_start(out=rkv_d[i, h, :, :], in_=stg_r[0:64, pb, :]), sem_r[pb],
                           reads=[BR["stg%d" % pb]], writes=[B_rkvd[i][h]])
            for h in range(32):
                cfs = [(r_f, "r", 0), (k_f, "k", 1), (v_f, "v", 2)]
                for (dst, nm, i) in cfs:
                    fw.dma("sp", lambda e, dst=dst, i=i, h=h: e.dma_start(out=dst[0:64, :], in_=rkv_d[i, h, :, :]), sem_r[2 + i],
                           reads=[B_rkvd[i][h]], writes=[BR[nm]])
                o, sz = woff["r%d_l2" % l][h][0:2]
                s = load_block(o, sz)
                fw.op("pe", lambda e, s=s: e.matmul(PS[0][0:64, :], ring[0:96, s, 0:64], l1w[0:96, :], start=True, stop=True),
                      reads=[B_ring[s], BR["l1w"]], writes=[B_ps[0]])
                fw.op("act", lambda e, h=h: e.activation(out=lw_f[D64, :], in_=PS[0][0:64, :], func=AF.Sigmoid, bias=P("rw0_%d" % l, h)[0:64, :]),
                      reads=[B_ps[0], B_pv], writes=[BR["lw"]])
                fw.op("dve", lambda e: e.tensor_scalar(out=lw_f[D64, :], in0=lw_f[D64, :], scalar1=-0.6065306597126334, scalar2=None, op0=ALU.mult),
                      reads=[BR["lw"]], writes=[BR["lw"]])
                fw.op("pe", lambda e, s=s: e.matmul(PS[1][0:64, :], ring[0:96, s, 64:128], l1a[0:96, :], start=True, stop=True),
                      reads=[B_ring[s], BR["l1a"]], writes=[B_ps[1]])
                fw.op("act", lambda e, h=h: e.activation(out=as_f[D64, :], in_=PS[1][0:64, :], func=AF.Sigmoid, bias=P("ra0_%d" % l, h)[0:64, :]),
                      reads=[B_ps[1], B_pv], writes=[BR["as"]])
                fw.op("pe", lambda e, s=s: e.matmul(PS[0][0:64, :], ring[:, s, 128:192], l1g[:, 0, :], start=True, stop=False),
                      reads=[B_ring[s], BR["l1g"]], writes=[B_ps[0]])
                fw.op("pe", lambda e, s=s: e.matmul(PS[0][0:64, :], ring[:, s, 192:256], l1g[:, 1, :], start=False, stop=True),
                      reads=[B_ring[s], BR["l1g"]], writes=[B_ps[0]])
                fw.op("dve", lambda e: e.tensor_copy(out=g_b[D64, :], in_=PS[0][0:64, :]), reads=[B_ps[0]], writes=[BR["g"]])
                if l >= 2:
                    fw.op("pe", lambda e, s=s: e.matmul(PS[1][0:64, :], ring[0:64, s, 256:320], l1v[0:64, :], start=True, stop=True),
                          reads=[B_ring[s], BR["l1v"]], writes=[B_ps[1]])
                    fw.op("act", lambda e, h=h: e.activation(out=t1_f[D64, :], in_=PS[1][0:64, :], func=AF.Sigmoid, bias=P("rv0_%d" % l, h)[0:64, :]),
                          reads=[B_ps[1], B_pv], writes=[BR["t1"]])
                    fw.dma("sp", lambda e, h=h: e.dma_start(out=t2_f[0:64, :], in_=vf_d[h, :, :]), sem_r[5], reads=[B_vfd[h]], writes=[BR["t2"]])
                    fw.op("dve", lambda e: e.tensor_tensor(out=t2_f[D64, :], in0=t2_f[D64, :], in1=v_f[D64, :], op=ALU.subtract),
                          reads=[BR["t2"], BR["v"]], writes=[BR["t2"]])
                    fw.op("dve", lambda e: e.tensor_tensor(out=t2_f[D64, :], in0=t2_f[D64, :], in1=t1_f[D64, :], op=ALU.mult),
                          reads=[BR["t2"], BR["t1"]], writes=[BR["t2"]])
                    fw.op("dve", lambda e: e.tensor_tensor(out=v_f[D64, :], in0=v_f[D64, :], in1=t2_f[D64, :], op=ALU.add),
                          reads=[BR["t2"], BR["v"]], writes=[BR["v"]])
                else:
                    fw.dma("sp", lambda e, h=h: e.dma_start(out=vf_d[h, :, :], in_=v_f[0:64, :]), sem_r[5], reads=[BR["v"]], writes=[B_vfd[h]])
                fw.op("dve", lambda e: e.tensor_copy(out=vb_b[D64, :], in_=v_f[D64, :]), reads=[BR["v"]], writes=[BR["vb"]])
                fw.op("dve", lambda e, h=h: e.tensor_scalar(out=kkn_f[D64, :], in0=k_f[D64, :], scalar1=P("rkk_%d" % l, h)[0:64, :], scalar2=None, op0=ALU.mult),
                      reads=[BR["k"], B_pv], writes=[BR["kkn"]])
                fw.op("dve", lambda e: e.tensor_tensor(out=q_b[D64, :], in0=kkn_f[D64, :], in1=kkn_f[D64, :], op=ALU.mult),
                      reads=[BR["kkn"]], writes=[BR["q"]])
                fw.op("pe", lambda e: e.matmul(PS[1][0:64, :], ones_bf[0:64, 0:64], q_b[D64, :], start=True, stop=True),
                      reads=[BR["q"], B_const], writes=[B_ps[1]])
                fw.op("dve", lambda e: e.tensor_scalar(out=t1_f[D64, :], in0=PS[1][0:64, :], scalar1=1e-24, scalar2=None, op0=ALU.max),
                      reads=[B_ps[1]], writes=[BR["t1"]])
                fw.op("act", lambda e: e.activation(out=t1_f[D64, :], in_=t1_f[D64, :], func=AF.Sqrt), reads=[BR["t1"]], writes=[BR["t1"]])
                fw.op("dve", lambda e: e.reciprocal(out=t1_f[D64, :], in_=t1_f[D64, :]), reads=[BR["t1"]], writes=[BR["t1"]])
                fw.op("dve", lambda e: e.tensor_tensor(out=kkn_f[D64, :], in0=kkn_f[D64, :], in1=t1_f[D64, :], op=ALU.mult),
                      reads=[BR["kkn"], BR["t1"]], writes=[BR["kkn"]])
                fw.op("dve", lambda e, h=h: e.tensor_scalar(out=t1_f[D64, :], in0=as_f[D64, :], scalar1=P("rka_%d" % l, h)[0:64, :],
                                                            scalar2=omka[0:64, ri, h:h + 1], op0=ALU.mult, op1=ALU.add),
                      reads=[BR["as"], B_pv, B_const], writes=[BR["t1"]])
                fw.op("dve", lambda e: e.tensor_tensor(out=k2_f[D64, :], in0=k_f[D64, :], in1=t1_f[D64, :], op=ALU.mult),
                      reads=[BR["k"], BR["t1"]], writes=[BR["k2"]])
                fw.op("dve", lambda e: e.tensor_tensor(out=bb_f[D64, :], in0=kkn_f[D64, :], in1=as_f[D64, :], op=ALU.mult),
                      reads=[BR["kkn"], BR["as"]], writes=[BR["bb"]])
                fw.op("dve", lambda e: e.tensor_tensor(out=t1_f[D64, :], in0=r_f[D64, :], in1=k2_f[D64, :], op=ALU.mult),
                      reads=[BR["r"], BR["k2"]], writes=[BR["t1"]])
                fw.op("dve", lambda e, h=h: e.tensor_scalar(out=q_b[D64, :], in0=t1_f[D64, :], scalar1=P("rrk_%d" % l, h)[0:64, :], scalar2=None, op0=ALU.mult),
                      reads=[BR["t1"], B_pv], writes=[BR["q"]])
                fw.op("pe", lambda e: e.matmul(PS[0][0:64, :], ones_bf[0:64, 0:64], q_b[D64, :], start=True, stop=True),
                      reads=[BR["q"], B_const], writes=[B_ps[0]])
                fw.op("dve", lambda e: e.tensor_tensor(out=bon_f[D64, :], in0=PS[0][0:64, :], in1=v_f[D64, :], op=ALU.mult),
                      reads=[B_ps[0], BR["v"]], writes=[BR["bon"]])
                src, srcn = lw_f, "lw"
                bufs = [(c0_f, "c0"), (c1_f, "c1")]
                for si, sft in enumerate((1, 2, 4, 8, 16, 32)):
                    dst, dstn = bufs[si % 2]
                    fw.op("dve", lambda e, src=src, dst=dst, sft=sft: e.tensor_tensor(out=v8(dst[D64, :])[:, :, sft:64], in0=v8(src[D64, :])[:, :, sft:64],
                                                                                 in1=v8(src[D64, :])[:, :, 0:64 - sft], op=ALU.add),
                          reads=[BR[srcn]], writes=[BR[dstn]])
                    fw.op("dve", lambda e, src=src, dst=dst, sft=sft: e.tensor_copy(out=v8(dst[D64, :])[:, :, 0:sft], in_=v8(src[D64, :])[:, :, 0:sft]),
                          reads=[BR[srcn]], writes=[BR[dstn]])
                    src, srcn = dst, dstn
                cl, cln = src, srcn
                fw.op("act", lambda e, cl=cl: e.activation(out=ecl_f[D64, :], in_=cl[D64, :], func=AF.Exp), reads=[BR[cln]], writes=[BR["ecl"]])
                fw.op("dve", lambda e: e.tensor_tensor(out=rt_b[D64, :], in0=r_f[D64, :], in1=ecl_f[D64, :], op=ALU.mult),
                      reads=[BR["r"], BR["ecl"]], writes=[BR["rt"]])
                fw.op("act", lambda e, cl=cl: e.activation(out=t1_f[D64, :], in_=cl[D64, :], func=AF.Exp, scale=-1.0), reads=[BR[cln]], writes=[BR["t1"]])
                fw.op("dve", lambda e: e.tensor_tensor(out=kt_b[D64, :], in0=k2_f[D64, :], in1=t1_f[D64, :], op=ALU.mult),
                      reads=[BR["k2"], BR["t1"]], writes=[BR["kt"]])
                fw.op("dve", lambda e: e.tensor_tensor(out=bt_b[D64, :], in0=bb_f[D64, :], in1=t1_f[D64, :], op=ALU.mult),
                      reads=[BR["bb"], BR["t1"]], writes=[BR["bt"]])
                fw.op("dve", lambda e, cl=cl: e.tensor_tensor(out=t2_f[D64, :], in0=cl[D64, :], in1=lw_f[D64, :], op=ALU.subtract),
                      reads=[BR[cln], BR["lw"]], writes=[BR["t2"]])
                fw.op("act", lambda e: e.activation(out=t2_f[D64, :], in_=t2_f[D64, :], func=AF.Exp), reads=[BR["t2"]], writes=[BR["t2"]])
                fw.op("dve", lambda e: e.scalar_tensor_tensor(out=at_b[D64, :], in0=kkn_f[D64, :], scalar=-1.0, in1=t2_f[D64, :], op0=ALU.mult, op1=ALU.mult),
                      reads=[BR["kkn"], BR["t2"]], writes=[BR["at"]])
                fw.op("dve", lambda e, cl=cl: e.tensor_tensor(out=v8(t1_f[D64, :]), in0=v8(cl[D64, :])[:, :, 63:64].to_broadcast([64, 8, 64]),
                                                              in1=v8(cl[D64, :]), op=ALU.subtract),
                      reads=[BR[cln]], writes=[BR["t1"]])
                fw.op("act", lambda e: e.activation(out=t1_f[D64, :], in_=t1_f[D64, :], func=AF.Exp), reads=[BR["t1"]], writes=[BR["t1"]])
                fw.op("dve", lambda e: e.tensor_tensor(out=kd_b[D64, :], in0=k2_f[D64, :], in1=t1_f[D64, :], op=ALU.mult),
                      reads=[BR["k2"], BR["t1"]], writes=[BR["kd"]])
                fw.op("dve", lambda e: e.tensor_tensor(out=bd_b[D64, :], in0=bb_f[D64, :], in1=t1_f[D64, :], op=ALU.mult),
                      reads=[BR["bb"], BR["t1"]], writes=[BR["bd"]])
                if it == 0:
                    fw.op("pool", lambda e: e.memset(S_r[D64, :], 0.0), writes=[BR["S"]])
                else:
                    fw.dma("sp", lambda e, h=h: e.dma_start(out=S_r[0:64, :], in_=rst_d[ri, h, :, :]), sem_r[5],
                           reads=[B_rst[ri][h], B_h], writes=[BR["S"]])
                fw.op("pool", lambda e: e.tensor_copy(out=Sb_r[D64, :], in_=S_r[D64, :]), reads=[BR["S"]], writes=[BR["Sb"]])
                for c in range(8):
                    cs = slice(c * 64, (c + 1) * 64)
                    for ti, (src_b, srcn2) in enumerate(((vb_b, "vb"), (kd_b, "kd"), (bd_b, "bd"))):
                        fw.op("pe", lambda e, ti=ti, src_b=src_b, cs=cs: e.transpose(PT[0:64, ti * 64:(ti + 1) * 64], src_b[D64, cs], ident_bf[0:64, 0:64]),
                              reads=[BR[srcn2], B_const], writes=[B_pt[0]])
                    fw.op("dve", lambda e: e.tensor_copy(out=tok_b[D64, :], in_=PT[0:64, 0:192]), reads=[B_pt[0]], writes=[BR["tok"]])
                    prs = ((bt_b, "bt", at_b, "at"), (at_b, "at", bt_b, "bt"), (kt_b, "kt", at_b, "at"), (bt_b, "bt", rt_b, "rt"), (kt_b, "kt", rt_b, "rt"))
                    for pi, (la, lan, rr, rrn) in enumerate(prs):
                        fw.op("pe", lambda e, pi=pi, la=la, rr=rr, cs=cs: e.matmul(PS[2][0:64, pi * 64:(pi + 1) * 64], la[D64, cs], rr[D64, cs], start=True, stop=True),
                              reads=[BR[lan], BR[rrn]], writes=[B_ps[2]])
                    fw.op("dve", lambda e: e.tensor_tensor(out=sc_b[D64, :], in0=PS[2][0:64, 0:320], in1=mask5[0:64, :], op=ALU.mult),
                          reads=[B_ps[2], B_const], writes=[BR["sc"]])
                    fw.op("pe", lambda e, cs=cs: e.matmul(PS[3][0:64, 0:64], at_b[D64, cs], Sb_r[D64, :], start=True, stop=False),
                          reads=[BR["at"], BR["Sb"]], writes=[B_ps[3]])
                    fw.op("pe", lambda e: e.matmul(PS[3][0:64, 0:64], sc_b[D64, 128:192], tok_b[D64, 0:64], start=False, stop=True),
                          reads=[BR["sc"], BR["tok"]], writes=[B_ps[3]])
                    fw.op("dve", lambda e: e.tensor_copy(out=X_b[D64, :], in_=PS[3][0:64, 0:64]), reads=[B_ps[3]], writes=[BR["X"]])
                    fw.op("dve", lambda e: e.tensor_tensor(out=PP_b[D64, 0:64], in0=sc_b[D64, 0:64], in1=ident_bf[0:64, 0:64], op=ALU.add),
                          reads=[BR["sc"], B_const], writes=[BR["PP"]])
                    fw.op("dve", lambda e: e.tensor_tensor(out=PP_b[D64, 64:128], in0=sc_b[D64, 64:128], in1=ident_bf[0:64, 0:64], op=ALU.add),
                          reads=[BR["sc"], B_const], writes=[BR["PP"]])
                    Mm, MT, Mn = sc_b[D64, 0:64], sc_b[D64, 64:128], "sc"
                    for lev in range(5):
                        fw.op("pe", lambda e, Mm=Mm, MT=MT: e.matmul(PS[4][0:64, 0:64], MT, Mm, start=True, stop=True), reads=[BR[Mn]], writes=[B_ps[4]])
                        fw.op("pe", lambda e, Mm=Mm, MT=MT: e.matmul(PS[4][0:64, 64:128], Mm, MT, start=True, stop=True), reads=[BR[Mn]], writes=[B_ps[4]])
                        fw.op("dve", lambda e: e.tensor_copy(out=M2_b[D64, :], in_=PS[4][0:64, 0:128]), reads=[B_ps[4]], writes=[BR["M2"]])
                        fw.op("pe", lambda e: e.matmul(PS[4][0:64, 128:192], PP_b[D64, 64:128], M2_b[D64, 0:64], start=True, stop=True),
                              reads=[BR["PP"], BR["M2"]], writes=[B_ps[4]])
                        fw.op("pe", lambda e: e.matmul(PS[4][0:64, 192:256], M2_b[D64, 0:64], PP_b[D64, 64:128], start=True, stop=True),
                              reads=[BR["PP"], BR["M2"]], writes=[B_ps[4]])
                        fw.op("dve", lambda e: e.tensor_tensor(out=PP_b[D64, :], in0=PP_b[D64, :], in1=PS[4][0:64, 128:256], op=ALU.add),
                              reads=[BR["PP"], B_ps[4]], writes=[BR["PP"]])
                        if lev < 4:
                            fw.op("dve", lambda e: e.tensor_copy(out=sc_b[D64, 0:128], in_=M2_b[D64, :]), reads=[BR["M2"]], writes=[BR["sc"]])
                    fw.op("pe", lambda e: e.matmul(PS[3][0:64, 64:128], PP_b[D64, 0:64], X_b[D64, :], start=True, stop=True),
                          reads=[BR["PP"], BR["X"]], writes=[B_ps[3]])
                    fw.op("dve", lambda e: e.tensor_copy(out=U_b[D64, :], in_=PS[3][0:64, 64:128]), reads=[B_ps[3]], writes=[BR["U"]])
                    fw.op("pe", lambda e, cs=cs: e.matmul(PS[5][0:64, 0:64], rt_b[D64, cs], Sb_r[D64, :], start=True, stop=False),
                          reads=[BR["rt"], BR["Sb"]], writes=[B_ps[5]])
                    fw.op("pe", lambda e: e.matmul(PS[5][0:64, 0:64], sc_b[D64, 192:256], U_b[D64, :], start=False, stop=False),
                          reads=[BR["sc"], BR["U"]], writes=[B_ps[5]])
                    fw.op("pe", lambda e: e.matmul(PS[5][0:64, 0:64], sc_b[D64, 256:320], tok_b[D64, 0:64], start=False, stop=True),
                          reads=[BR["sc"], BR["tok"]], writes=[B_ps[5]])
                    fw.op("dve", lambda e, cs=cs: e.tensor_copy(out=ytok_f[D64, cs], in_=PS[5][0:64, 0:64]), reads=[B_ps[5]], writes=[BR["ytok"]])
                    fw.op("pe", lambda e: e.matmul(PS[3][0:64, 128:192], tok_b[D64, 128:192], U_b[D64, :], start=True, stop=False),
                          reads=[BR["tok"], BR["U"]], writes=[B_ps[3]])
                    fw.op("pe", lambda e: e.matmul(PS[3][0:64, 128:192], tok_b[D64, 64:128], tok_b[D64, 0:64], start=False, stop=True),
                          reads=[BR["tok"]], writes=[B_ps[3]])
                    fw.op("dve", lambda e, c=c: e.scalar_tensor_tensor(out=S_r[D64, :], in0=S_r[D64, :], scalar=ecl_f[0:64, c * 64 + 63:c * 64 + 64],
                                                                   in1=PS[3][0:64, 128:192], op0=ALU.mult, op1=ALU.add),
                          reads=[BR["S"], BR["ecl"], B_ps[3]], writes=[BR["S"]])
                    fw.op("pool", lambda e: e.tensor_copy(out=Sb_r[D64, :], in_=S_r[D64, :]), reads=[BR["S"]], writes=[BR["Sb"]])
                fw.dma("sp", lambda e, h=h: e.dma_start(out=rst_d[ri, h, :, :], in_=S_r[0:64, :]), sem_r[4], reads=[BR["S"]], writes=[B_rst[ri][h]])
                fw.op("dve", lambda e: e.tensor_reduce(out=st8[D64, 0:8], in_=v8(ytok_f[D64, :]), axis=AX.X, op=ALU.add), reads=[BR["ytok"]], writes=[BR["st8"]])
                fw.op("dve", lambda e: e.tensor_tensor(out=t1_f[D64, :], in0=ytok_f[D64, :], in1=ytok_f[D64, :], op=ALU.mult), reads=[BR["ytok"]], writes=[BR["t1"]])
                fw.op("dve", lambda e: e.tensor_reduce(out=st8[D64, 8:16], in_=v8(t1_f[D64, :]), axis=AX.X, op=ALU.add), reads=[BR["t1"]], writes=[BR["st8"]])
                fw.op("dve", lambda e: e.tensor_scalar(out=st8[D64, 0:16], in0=st8[D64, 0:16], scalar1=1.0 / 64, scalar2=None, op0=ALU.mult), reads=[BR["st8"]], writes=[BR["st8"]])
                fw.op("dve", lambda e: e.tensor_tensor(out=st8[D64, 16:24], in0=st8[D64, 0:8], in1=st8[D64, 0:8], op=ALU.mult), reads=[BR["st8"]], writes=[BR["st8"]])
                fw.op("dve", lambda e: e.tensor_tensor(out=st8[D64, 8:16], in0=st8[D64, 8:16], in1=st8[D64, 16:24], op=ALU.subtract), reads=[BR["st8"]], writes=[BR["st8"]])
                fw.op("dve", lambda e: e.tensor_scalar(out=st8[D64, 8:16], in0=st8[D64, 8:16], scalar1=64e-5, scalar2=None, op0=ALU.add), reads=[BR["st8"]], writes=[BR["st8"]])
                fw.op("act", lambda e: e.activation(out=st8[D64, 8:16], in_=st8[D64, 8:16], func=AF.Sqrt), reads=[BR["st8"]], writes=[BR["st8"]])
                fw.op("dve", lambda e: e.reciprocal(out=st8[D64, 8:16], in_=st8[D64, 8:16]), reads=[BR["st8"]], writes=[BR["st8"]])
                fw.op("dve", lambda e: e.tensor_tensor(out=v8(t1_f[D64, :]), in0=v8(ytok_f[D64, :]), in1=st8[D64, 0:8].unsqueeze(2).to_broadcast([64, 8, 64]), op=ALU.subtract),
                      reads=[BR["ytok"], BR["st8"]], writes=[BR["t1"]])
                fw.op("dve", lambda e: e.tensor_tensor(out=v8(yn_b[D64, :]), in0=v8(t1_f[D64, :]), in1=st8[D64, 8:16].unsqueeze(2).to_broadcast([64, 8, 64]), op=ALU.mult),
                      reads=[BR["t1"], BR["st8"]], writes=[BR["yn"]])
                for c in range(8):
                    fw.op("pe", lambda e, c=c: e.transpose(PT[0:64, 256 + c * 64:256 + (c + 1) * 64], yn_b[D64, c * 64:(c + 1) * 64], ident_bf[0:64, 0:64]),
                          reads=[BR["yn"], B_const], writes=[B_pt[0]])
                fw.op("dve", lambda e, h=h: e.tensor_scalar(out=yfm_f[D64, :], in0=PT[0:64, 256:768], scalar1=P("rlw_%d" % l, h)[0:64, :],
                                                            scalar2=P("rlb_%d" % l, h)[0:64, :], op0=ALU.mult, op1=ALU.add),
                      reads=[B_pt[0], B_pv], writes=[BR["yfm"]])
                fw.op("dve", lambda e: e.tensor_tensor(out=yfm_f[D64, :], in0=yfm_f[D64, :], in1=bon_f[D64, :], op=ALU.add),
                      reads=[BR["yfm"], BR["bon"]], writes=[BR["yfm"]])
                fw.op("dve", lambda e, h=h: e.tensor_tensor(out=yh_all[:, h, :], in0=yfm_f[D64, :], in1=g_b[D64, :], op=ALU.mult),
                      reads=[BR["yfm"], BR["g"]], writes=[B_yh[h]])
            blocks_out = woff["r%d_wo" % l]
            for i in range(DC):
                pb = i % 2
                for q in range(2):
                    o, sz = blocks_out[i * 2 + q][0:2]
                    s = load_block(o, sz)
                    for hh in range(16):
                        hd = q * 16 + hh
                        fw.op("pe", lambda e, s=s, hh=hh, hd=hd, pb=pb, q=q: e.matmul(
                            PS[pb][:], ring[0:64, s, hh * 128:(hh + 1) * 128], yh_all[:, hd, :],
                            start=(q == 0 and hh == 0), stop=(q == 1 and hh == 15)),
                            reads=[B_ring[s], B_yh[hd]], writes=[B_ps[pb]])
                fw.op("dve", lambda e, i=i, pb=pb: e.tensor_copy(out=f_sb[:, i, :], in_=PS[pb][:]), reads=[B_ps[pb]], writes=[B_f[i]])
            postnorm_residual(l, 0)

        xv = x_d.rearrange("(j p) t -> p j t", p=128)
        yv = y_d.rearrange("(j p) t -> p j t", p=128)
        ytok = None
        for it in range(NT):
            t0 = it * TT
            fw.dma("sp", lambda e, t0=t0: e.dma_start(out=x_sb[:], in_=xv[:, :, t0:t0 + TT]), sem_x, writes=[B_x])
            for l in range(nl):
                if use_r(mixers) and l % 2 == 0:
                    prenorm(l, 0)
                    rwkv(l, it)
                if use_m(mixers) and l % 2 == 1:
                    prenorm(l, 0)
                    mamba(l, it)
                import os as _os4
                if _os4.environ.get("DBG_DUMP") and l >= 1:
                    continue
                prenorm(l, 1)
                ffn(l)
            import os as _os2
            if _os2.environ.get("DBG_CLAMP"):
                for j in range(DC):
                    fw.op("dve", lambda e, j=j: e.tensor_scalar(out=x_sb[:, j, :], in0=x_sb[:, j, :], scalar1=7777.0, scalar2=-7777.0,
                                                                 op0=ALU.min, op1=ALU.max), reads=[B_x], writes=[B_x])
            ytok = fw.dma("sp", lambda e, t0=t0: e.dma_start(out=yv[:, :, t0:t0 + TT], in_=x_sb[:]), sem_y, reads=[B_x])
        fw.emit([ytok])
    return nc


_CACHE = {}


def run(inp, T, nl, mixers=True):
    Bn = inp["x"].shape[0]
    pvs = [pack_host(inp, b, nl, mixers) for b in range(Bn)]
    wl = pack_weights(inp, nl, mixers)
    wall = wl.build()
    ada = pack_ada(inp)
    nc = build_nc(T, nl, pvs[0].off, pvs[0].n, mixers)
    in_maps = []
    for b in range(Bn):
        in_maps.append({"x": np.ascontiguousarray(inp["x"][b].T), "pv": pvs[b].build(), "wall": wall, "adaw": ada})
    res = run_bass_kernel_spmd(nc, in_maps, core_ids=list(range(Bn)))
    out = np.stack([np.ascontiguousarray(res.results[b]["y"].T) for b in range(Bn)], axis=0)
    return out


MIXERS = True


def kernel(**inputs):
    inp = {k: np.asarray(v) for k, v in inputs.items()}
    return run(inp, inp["x"].shape[1], NL_DEFAULT, mixers=MIXERS).astype(np.float32)
```

```python
import numpy as np
import concourse.bass as bass
import concourse.mybir as mybir
from concourse.bass_utils import run_bass_kernel_spmd
from contextlib import ExitStack

F32 = mybir.dt.float32
BF16 = mybir.dt.bfloat16
AF = mybir.ActivationFunctionType
ALU = mybir.AluOpType
AX = mybir.AxisListType

D = 2048
DC = D // 128
TT = 512
FH = 5504
FHC = FH // 128
SLOT = 3072
NSLOT = 6
ENGS = ("pe", "act", "dve", "pool", "sp")


class Buf:
    __slots__ = ("name", "w", "r", "excl")

    def __init__(self, name="", excl=False):
        self.name = name
        self.w = None
        self.r = {}
        self.excl = excl


class Op:
    __slots__ = ("fn", "deps", "inc", "dma_sem", "dma_val")

    def __init__(self, fn):
        self.fn = fn
        self.deps = []
        self.inc = False
        self.dma_sem = None
        self.dma_val = 0


class FW:
    def __init__(self, nc):
        self.nc = nc
        self.ops = {e: [] for e in ENGS}
        self.dma_sems = []

    def new_dma_sem(self):
        self.dma_sems.append(0)
        return len(self.dma_sems) - 1

    def _add(self, eng, fn, reads, writes, extra=()):
        writes = list(writes) + [b for b in reads if b.excl]
        reads = [b for b in reads if not b.excl]
        op = Op(fn)
        idx = len(self.ops[eng])
        deps = list(extra)
        for b in reads:
            if b.w is not None:
                deps.append(b.w)
        for b in writes:
            if b.w is not None:
                deps.append(b.w)
            deps.extend(b.r.values())
        out = []
        seen = set()
        for t in deps:
            if t in seen:
                continue
            seen.add(t)
            if t[0] == "e" and t[1] == eng and eng in ("pe", "sp"):
                continue
            out.append(t)
        op.deps = out
        for t in out:
            if t[0] == "e":
                self.ops[t[1]][t[2]].inc = True
        self.ops[eng].append(op)
        return op, idx

    def op(self, eng, fn, reads=(), writes=()):
        op, idx = self._add(eng, fn, reads, writes)
        writes = list(writes) + [b for b in reads if b.excl]
        reads = [b for b in reads if not b.excl]
        tok = ("e", eng, idx)
        for b in reads:
            b.r[("e", eng)] = tok
        for b in writes:
            b.w = tok
            b.r = {}
        return tok

    def dma(self, eng, fn, semidx, reads=(), writes=()):
        extra = []
        if self.dma_sems[semidx] > 0:
            extra.append(("d", semidx, self.dma_sems[semidx]))
        op, idx = self._add(eng, fn, reads, writes, extra)
        self.dma_sems[semidx] += 16
        op.dma_sem = semidx
        op.dma_val = self.dma_sems[semidx]
        tok = ("d", semidx, op.dma_val)
        for b in reads:
            b.r[("d", semidx)] = tok
        for b in writes:
            b.w = tok
            b.r = {}
        return tok

    def emit(self, final_wait_tokens=()):
        nc = self.nc
        with ExitStack() as st:
            esem = {e: st.enter_context(nc.semaphore("s_" + e)) for e in ENGS}
            dsem = [st.enter_context(nc.semaphore("d%d" % i)) for i in range(len(self.dma_sems))]
            val = {}
            for e in ENGS:
                c = 0
                v = []
                for op in self.ops[e]:
                    if op.inc:
                        c += 1
                    v.append(c)
                val[e] = v

            def run(e, engine):
                waited = {}
                for op in self.ops[e]:
                    for t in op.deps:
                        if t[0] == "e":
                            key = ("e", t[1])
                            need = val[t[1]][t[2]]
                            sem = esem[t[1]]
                        else:
                            key = ("d", t[1])
                            need = t[2]
                            sem = dsem[t[1]]
                        if waited.get(key, 0) >= need:
                            continue
                        waited[key] = need
                        engine.wait_ge(sem, need)
                    ins = op.fn(engine)
                    if op.dma_sem is not None:
                        ins.then_inc(dsem[op.dma_sem], 16)
                    elif op.inc:
                        ins.then_inc(esem[e], 1)
                if e == "sp":
                    for t in final_wait_tokens:
                        if t[0] == "d":
                            engine.wait_ge(dsem[t[1]], t[2])
                        else:
                            engine.wait_ge(esem[t[1]], val[t[1]][t[2]])

            with nc.Block() as block:
                @block.tensor
                def _(eng):
                    run("pe", eng)

                @block.scalar
                def _(eng):
                    run("act", eng)

                @block.vector
                def _(eng):
                    run("dve", eng)

                @block.gpsimd
                def _(eng):
                    run("pool", eng)

                @block.sync
                def _(eng):
                    run("sp", eng)


class PV:
    def __init__(self):
        self.cols = []
        self.off = {}
        self.n = 0

    def add(self, name, vec):
        vec = np.asarray(vec, np.float32).reshape(-1)
        assert vec.size % 128 == 0, (name, vec.size)
        m = vec.size // 128
        self.cols.append(vec.reshape(m, 128).T)
        self.off[name] = (self.n, m)
        self.n += m

    def addraw(self, name, arr):
        arr = np.asarray(arr, np.float32)
        self.cols.append(arr)
        self.off[name] = (self.n, arr.shape[1])
        self.n += arr.shape[1]

    def build(self):
        return np.ascontiguousarray(np.concatenate(self.cols, axis=1))


class WL:
    def __init__(self):
        self.blocks = []
        self.off = {}
        self.n = 0

    def add_raw(self, name, arrs):
        lst = []
        for arr in arrs:
            arr = np.asarray(arr, np.float32)
            assert arr.shape[0] == 128
            lst.append((self.n, arr.shape[1], 0, 0, 0, 0))
            self.blocks.append(arr)
            self.n += arr.shape[1]
        self.off[name] = lst

    def add_mat(self, name, W, eb=128, kmax=24):
        Kd, E = W.shape
        assert Kd % 128 == 0
        KC = Kd // 128
        nk = (KC + kmax - 1) // kmax
        ksz = (KC + nk - 1) // nk
        kr = [(a, min(a + ksz, KC)) for a in range(0, KC, ksz)]
        Wr = W.reshape(KC, 128, E)
        lst = []
        for e0 in range(0, E, eb):
            e1 = min(e0 + eb, E)
            for (a, b) in kr:
                blk = Wr[a:b, :, e0:e1].transpose(1, 0, 2).reshape(128, (b - a) * (e1 - e0))
                lst.append((self.n, blk.shape[1], a, b, e0, e1))
                self.blocks.append(blk)
                self.n += blk.shape[1]
        self.off[name] = lst

    def build(self):
        return np.ascontiguousarray(np.concatenate(self.blocks, axis=1))


def wl_layout(name_shapes, eb_map):
    off = {}
    n = 0
    for name, (Kd, E) in name_shapes:
        if Kd == "raw":
            lst = []
            for sz in E:
                lst.append((n, sz, 0, 0, 0, 0))
                n += sz
            off[name] = lst
            continue
        eb = eb_map.get(name, 128)
        kmax = 24
        KC = Kd // 128
        nk = (KC + kmax - 1) // kmax
        ksz = (KC + nk - 1) // nk
        kr = [(a, min(a + ksz, KC)) for a in range(0, KC, ksz)]
        lst = []
        for e0 in range(0, E, eb):
            e1 = min(e0 + eb, E)
            for (a, b) in kr:
                sz = (b - a) * (e1 - e0)
                lst.append((n, sz, a, b, e0, e1))
                n += sz
        off[name] = lst
    return off, n


NL_DEFAULT = 4


SI = 4096


def use_m(mixers):
    return mixers is True or (isinstance(mixers, str) and "m" in mixers)


def use_r(mixers):
    return mixers is True or (isinstance(mixers, str) and "r" in mixers)


def weight_names(nl, mixers=True):
    names = []
    for l in range(nl):
        if use_r(mixers) and l % 2 == 0:
            names.append(("r%d_w1" % l, (D, 96)))
            names.append(("r%d_a1" % l, (D, 96)))
            names.append(("r%d_g1" % l, (D, 256)))
            if l >= 2:
                names.append(("r%d_v1" % l, (D, 64)))
            names.append(("r%d_wv" % l, (D, D)))
            names.append(("r%d_wr" % l, (D, D)))
            names.append(("r%d_wk" % l, (D, D)))
            names.append(("r%d_l2" % l, ("raw", [320] * 32)))
            names.append(("r%d_wo" % l, ("raw", [2048] * 32)))
        if use_m(mixers) and l % 2 == 1:
            names.append(("m%d_dt" % l, (D, 64)))
            for g in range(8):
                names.append(("m%d_z%d" % (l, g), (D, 512)))
                names.append(("m%d_x%d" % (l, g), (D, 512)))
                names.append(("m%d_B%d" % (l, g), (D, 128)))
                names.append(("m%d_C%d" % (l, g), (D, 128)))
            names.append(("m%d_out" % l, (SI, D)))
        names.append(("ffn_in%d" % l, (D, 2 * FH)))
        names.append(("ffn_out%d" % l, (FH, D)))
    return names


def pack_host(inp, b, nl, mixers=True):
    pv = PV()
    pv.add("c", inp["c"][b])
    pv.add("ada_b", inp["ada_b"])
    for l in range(nl):
        pv.add("tab%d" % l, inp["ada_table"][l])
        pv.add("nmpre%d" % l, inp["norm_mix_pre"][l])
        pv.add("nmpost%d" % l, inp["norm_mix_post"][l])
        pv.add("nfpre%d" % l, inp["norm_ffn_pre"][l])
        pv.add("nfpost%d" % l, inp["norm_ffn_post"][l])
        cw = inp["ffn_conv_w"][l]
        for j in range(3):
            pv.add("fcw%d_%d" % (l, j), cw[j])
        pv.add("fcb%d" % l, inp["ffn_conv_b"][l])
        if use_m(mixers) and l % 2 == 1:
            m = l // 2
            for j in range(4):
                pv.add("mcw%d_%d" % (l, j), inp["ssm_conv_w"][m][j])
            pv.add("mcb%d" % l, inp["ssm_conv_b"][m])
            pv.add("mnw%d" % l, inp["ssm_norm"][m])
            pv.addraw("mD%d" % l, np.tile(inp["ssm_d"][m][None, :], (128, 1)))
            pv.addraw("mdtb%d" % l, np.tile(inp["ssm_dt_bias"][m][None, :], (128, 1)))
            pv.addraw("malog%d" % l, np.tile(inp["ssm_a_log"][m][None, :], (128, 1)))
        if use_r(mixers) and l % 2 == 0:
            ri = l // 2

            def a64(name, vec):
                arr = np.asarray(vec, np.float32).reshape(32, 64).T
                pv.addraw(name, np.concatenate([arr, arr], axis=0))
            for i in range(6):
                pv.add("rmu%d_%d" % (l, i), inp["rwkv_mu"][ri][i])
            a64("rw0_%d" % l, inp["rwkv_w0"][ri])
            a64("ra0_%d" % l, inp["rwkv_a0"][ri])
            a64("rkk_%d" % l, inp["rwkv_k_k"][ri])
            a64("rka_%d" % l, inp["rwkv_k_a"][ri])
            a64("rrk_%d" % l, inp["rwkv_r_k"][ri])
            a64("rlw_%d" % l, inp["rwkv_ln_w"][ri])
            a64("rlb_%d" % l, inp["rwkv_ln_b"][ri])
            if l >= 2:
                a64("rv0_%d" % l, inp["rwkv_v0"][ri - 1])
    tri = np.triu(np.ones((64, 64), np.float32))
    pv.addraw("tri", np.concatenate([tri, tri], axis=0))
    pv.addraw("ident", np.eye(128, dtype=np.float32))
    return pv


def pack_weights(inp, nl, mixers=True):
    wl = WL()
    for l in range(nl):
        if use_r(mixers) and l % 2 == 0:
            ri = l // 2
            wl.add_mat("r%d_w1" % l, inp["rwkv_w1"][ri], eb=96)
            wl.add_mat("r%d_a1" % l, inp["rwkv_a1"][ri], eb=96)
            wl.add_mat("r%d_g1" % l, inp["rwkv_g1"][ri], eb=128)
            if l >= 2:
                wl.add_mat("r%d_v1" % l, inp["rwkv_v1"][ri - 1], eb=64)
            wl.add_mat("r%d_wv" % l, inp["rwkv_w_rkv"][ri][2], eb=64)
            wl.add_mat("r%d_wr" % l, inp["rwkv_w_rkv"][ri][0], eb=64)
            wl.add_mat("r%d_wk" % l, inp["rwkv_w_rkv"][ri][1], eb=64)
            blks = []
            for h in range(32):
                hs_ = slice(h * 64, (h + 1) * 64)
                b_ = np.zeros((128, 320), np.float32)
                b_[0:96, 0:64] = inp["rwkv_w2"][ri][:, hs_]
                b_[0:96, 64:128] = inp["rwkv_a2"][ri][:, hs_]
                b_[:, 128:192] = inp["rwkv_g2"][ri][0:128, hs_]
                b_[:, 192:256] = inp["rwkv_g2"][ri][128:256, hs_]
                if l >= 2:
                    b_[0:64, 256:320] = inp["rwkv_v2"][ri - 1][:, hs_]
                blks.append(b_)
            wl.add_raw("r%d_l2" % l, blks)
            Wo = inp["rwkv_w_o"][ri].reshape(32, 64, 16, 128)
            blks = []
            for i in range(16):
                for q in range(2):
                    b_ = np.zeros((128, 2048), np.float32)
                    b_[0:64, :] = Wo[q * 16:(q + 1) * 16, :, i, :].transpose(1, 0, 2).reshape(64, 2048)
                    blks.append(b_)
            wl.add_raw("r%d_wo" % l, blks)
        if use_m(mixers) and l % 2 == 1:
            m = l // 2
            W = inp["ssm_w_in"][m]
            wl.add_mat("m%d_dt" % l, W[:, 2 * SI + 2048:], eb=64)
            for g in range(8):
                wl.add_mat("m%d_z%d" % (l, g), W[:, g * 512:(g + 1) * 512])
                wl.add_mat("m%d_x%d" % (l, g), W[:, SI + g * 512:SI + (g + 1) * 512])
                wl.add_mat("m%d_B%d" % (l, g), W[:, 2 * SI + g * 128:2 * SI + (g + 1) * 128])
                wl.add_mat("m%d_C%d" % (l, g), W[:, 2 * SI + 1024 + g * 128:2 * SI + 1024 + (g + 1) * 128])
            wl.add_mat("m%d_out" % l, inp["ssm_w_out"][m])
        wl.add_mat("ffn_in%d" % l, inp["ffn_w_in"][l])
        wl.add_mat("ffn_out%d" % l, inp["ffn_w_out"][l])
    return wl


def pack_ada(inp):
    W = inp["ada_w"].reshape(16, 128, 96, 128)
    A = W.transpose(2, 0, 1, 3)
    A = A.reshape(96, 2, 8, 128, 128).transpose(3, 0, 1, 2, 4)
    return np.ascontiguousarray(A.reshape(128, 96 * 2 * 8 * 128))


def build_nc(T, nl, pvoff, npv, mixers=True):
    NT = T // TT
    ebm = {("m%d_dt" % l): 64 for l in range(nl)}
    for l in range(nl):
        ebm["r%d_w1" % l] = 96
        ebm["r%d_a1" % l] = 96
        ebm["r%d_v1" % l] = 64
        for nm in ("wr", "wk", "wv"):
            ebm["r%d_%s" % (l, nm)] = 64
    woff, WTOT = wl_layout(weight_names(nl, mixers), ebm)
    nc = bass.Bass("TRN2", target_bir_lowering=False)
    x_d = nc.dram_tensor("x", [D, T], F32, kind="ExternalInput").ap()
    pv_d = nc.dram_tensor("pv", [128, npv], F32, kind="ExternalInput").ap()
    wall_d = nc.dram_tensor("wall", [128, WTOT], F32, kind="ExternalInput").ap()
    ada_d = nc.dram_tensor("adaw", [128, 96 * 2048], F32, kind="ExternalInput").ap()
    y_d = nc.dram_tensor("y", [D, T], F32, kind="ExternalOutput").ap()
    _wn = weight_names(nl, mixers)
    _first = {}
    for _name, _ in _wn:
        _l = int(_name.split("_")[0][1:]) if _name[0] in "mr" else int(_name[-1])
        if _l not in _first:
            _first[_l] = woff[_name][0][0]
    seg_starts = sorted(_first.values())
    seg_ends = seg_starts[1:] + [WTOT]
    wbf_segs = [nc.dram_tensor("wbf%d" % i, [128, b - a], BF16, kind="Internal").ap()
                for i, (a, b) in enumerate(zip(seg_starts, seg_ends))]

    def wbf_ap(o, sz):
        for (a, b, t) in zip(seg_starts, seg_ends, wbf_segs):
            if a <= o and o + sz <= b:
                return t[:, o - a:o - a + sz]
        raise AssertionError((o, sz))
    fw = FW(nc)
    st = ExitStack()

    def sb(name, shape, dt):
        return st.enter_context(nc.sbuf_tensor(name, shape, dt))

    def ps(name, shape, dt=F32):
        return st.enter_context(nc.psum_tensor(name, shape, dt))

    with st:
        pvs = sb("pvs", [128, npv], F32)
        x_sb = sb("x_sb", [128, DC, TT], F32)
        h_sb = sb("h_sb", [128, DC, TT], BF16)
        SCRF = 15872
        scr = sb("scr", [128, SCRF], F32)
        scr_bf = scr[:].bitcast(BF16)
        scr2 = sb("scr2", [128, 4096], F32)
        f_flat = scr2[:].bitcast(BF16)
        lm_sb = scr2[:, 0:nl * 96].rearrange("p (l n) -> p l n", l=nl)
        a_flat = scr_bf[:, 0:FHC * TT]

        class _V3:
            def __init__(self, flat, n):
                self.flat, self.n = flat, n

            def __getitem__(self, key):
                p, j, t = key
                assert isinstance(j, int)
                return self.flat[p, j * self.n:(j + 1) * self.n][:, t]

        f_sb = _V3(f_flat, TT)
        a_sb = _V3(a_flat, TT)
        sq_sb = sb("sq_sb", [128, 2, TT], BF16)
        rstd_sb = sb("rstd_sb", [128, TT], F32)
        tmp_sb = sb("tmp_sb", [128, 2, TT], F32)
        ub_sb = sb("ub_sb", [128, 4, TT + 3], F32)
        cv_sb = sb("cv_sb", [128, 4, TT], BF16)
        sg_sb = sb("sg_sb", [128, 2, TT], BF16)
        ring = sb("ring", [128, NSLOT, SLOT], BF16)
        ones_bf = sb("ones_bf", [128, 128], BF16)
        mod_sb = sb("mod_sb", [128, 96], F32)
        coef = sb("coef", [128, nl, 6, DC], F32)
        sc_sb = sb("sc_sb", [128, DC], F32)
        fcar = sb("fcar", [128, nl, 2 * FHC, 2], F32)
        PS = [ps("ps%d" % i, [128, 512]) for i in range(7)]
        PT = ps("pt", [128, 1024], BF16)

        B_pv = Buf("pv")
        B_x = Buf("x")
        B_h = Buf("h")
        B_f = [Buf("f%d" % i) for i in range(DC)]
        B_a = [Buf("a%d" % i) for i in range(FHC)]
        B_sq = [Buf("sq0"), Buf("sq1")]
        B_rstd = Buf("rstd")
        B_tmp = [Buf("tmp%d" % i) for i in range(2)]
        B_ub = [Buf("ub%d" % i) for i in range(4)]
        B_cv = [Buf("cv%d" % i) for i in range(4)]
        B_sg = [Buf("sg%d" % i) for i in range(2)]
        B_ring = [Buf("ring%d" % i) for i in range(NSLOT)]
        B_ps = [Buf("ps%d" % i, excl=True) for i in range(7)]
        _bpt = Buf("pt", excl=True)
        B_pt = [_bpt, _bpt]
        B_const = Buf("const")
        B_mod = Buf("mod")
        B_coef = Buf("coef")
        B_fcar = Buf("fcar")
        ring_sem = [fw.new_dma_sem() for _ in range(NSLOT)]
        sem_misc = fw.new_dma_sem()
        sem_x = fw.new_dma_sem()
        sem_y = fw.new_dma_sem()

        def P(name, j=0, n=1):
            o, m = pvoff[name]
            return pvs[:, o + j:o + j + n]

        fw.dma("sp", lambda e: e.dma_start(out=pvs[:], in_=pv_d[:, :]), sem_misc, writes=[B_pv])
        fw.op("pool", lambda e: e.memset(ones_bf[:], 1.0), writes=[B_const])
        fw.op("pool", lambda e: e.memset(fcar[:], 0.0), writes=[B_fcar])

        CH = 32768
        pre_chunks = []
        pre_sems = [fw.new_dma_sem() for _ in range(4)]
        ci = 0
        for (sa, sbnd) in zip(seg_starts, seg_ends):
            for c0 in range(sa, sbnd, CH):
                c1 = min(c0 + CH, sbnd)
                bb = Buf("pre%d" % ci)
                fw.dma("pool", lambda e, c0=c0, c1=c1: e.dma_start(out=wbf_ap(c0, c1 - c0), in_=wall_d[:, c0:c1]),
                       pre_sems[ci % 4], writes=[bb])
                pre_chunks.append((c0, c1, bb))
                ci += 1

        def pre_bufs(o, sz):
            return [bb for (c0, c1, bb) in pre_chunks if c0 < o + sz and c1 > o]

        ring_state = {"i": 0}

        def load_block(o, sz):
            s = ring_state["i"] % NSLOT
            ring_state["i"] += 1
            fw.dma("sp", lambda e, s=s, o=o, sz=sz: e.dma_start(out=ring[:, s, 0:sz], in_=wbf_ap(o, sz)),
                   ring_sem[s], reads=pre_bufs(o, sz), writes=[B_ring[s]])
            return s

        fw.op("act", lambda e: e.activation(out=sc_sb[:], in_=P("c", 0, DC), func=AF.Silu),
              reads=[B_pv], writes=[B_mod])
        stg = scr[:, 0:2048].rearrange("p (s n) -> p s n", s=2)
        B_stg = [Buf("stg0"), Buf("stg1")]
        stg_sem = [fw.new_dma_sem(), fw.new_dma_sem()]
        k = 0
        for j in range(96):
            for half in range(2):
                s = k % 2
                o = (j * 2 + half) * 1024
                fw.dma("sp", lambda e, s=s, o=o: e.dma_start(out=stg[:, s, :], in_=ada_d[:, o:o + 1024]),
                       stg_sem[s], writes=[B_stg[s]])
                for kc in range(8):
                    kk = half * 8 + kc
                    fw.op("pe", lambda e, s=s, kc=kc, kk=kk, j=j: e.matmul(
                        PS[0][:, j:j + 1], stg[:, s, kc * 128:(kc + 1) * 128], sc_sb[:, kk:kk + 1],
                        start=(kk == 0), stop=(kk == 15)),
                        reads=[B_stg[s], B_mod], writes=[B_ps[0]])
                k += 1
        fw.op("dve", lambda e: e.tensor_tensor(out=mod_sb[:], in0=PS[0][:, 0:96], in1=P("ada_b", 0, 96), op=ALU.add),
              reads=[B_ps[0], B_pv], writes=[B_mod])
        for l in range(nl):
            fw.op("dve", lambda e, l=l: e.tensor_tensor(out=lm_sb[:, l, :], in0=mod_sb[:], in1=P("tab%d" % l, 0, 96), op=ALU.add),
                  reads=[B_mod, B_pv], writes=[B_coef])
            for (half, pre, post) in ((0, "nmpre", "nmpost"), (1, "nfpre", "nfpost")):
                base = half * 48
                fw.op("dve", lambda e, l=l, base=base, half=half, pre=pre: e.scalar_tensor_tensor(
                    out=coef[:, l, half * 3 + 0, :], in0=lm_sb[:, l, base + 16:base + 32], scalar=1.0,
                    in1=P("%s%d" % (pre, l), 0, DC), op0=ALU.add, op1=ALU.mult),
                    reads=[B_coef, B_pv], writes=[B_coef])
                fw.op("dve", lambda e, l=l, base=base, half=half: e.tensor_copy(
                    out=coef[:, l, half * 3 + 1, :], in_=lm_sb[:, l, base:base + 16]),
                    reads=[B_coef], writes=[B_coef])
                fw.op("dve", lambda e, l=l, base=base, half=half, post=post: e.tensor_tensor(
                    out=coef[:, l, half * 3 + 2, :], in0=lm_sb[:, l, base + 32:base + 48],
                    in1=P("%s%d" % (post, l), 0, DC), op=ALU.mult),
                    reads=[B_coef, B_pv], writes=[B_coef])

        def rstd_from(psb, n=float(D)):
            fw.op("dve", lambda e: e.tensor_scalar(out=rstd_sb[:], in0=PS[psb][:], scalar1=1.0 / n, scalar2=1e-6,
                                                   op0=ALU.mult, op1=ALU.add),
                  reads=[B_ps[psb]], writes=[B_rstd])
            fw.op("act", lambda e: e.activation(out=rstd_sb[:], in_=rstd_sb[:], func=AF.Sqrt),
                  reads=[B_rstd], writes=[B_rstd])
            fw.op("dve", lambda e: e.reciprocal(out=rstd_sb[:], in_=rstd_sb[:]),
                  reads=[B_rstd], writes=[B_rstd])

        def prenorm(l, half):
            for j in range(DC):
                s = j % 2
                fw.op("pool", lambda e, j=j, s=s: e.tensor_tensor(out=sq_sb[:, s, :], in0=x_sb[:, j, :], in1=x_sb[:, j, :], op=ALU.mult),
                      reads=[B_x], writes=[B_sq[s]])
                fw.op("pe", lambda e, j=j, s=s: e.matmul(PS[6][:], ones_bf[:], sq_sb[:, s, :], start=(j == 0), stop=(j == DC - 1)),
                      reads=[B_sq[s], B_const], writes=[B_ps[6]])
            rstd_from(6)
            for j in range(DC):
                s = j % 2
                fw.op("pool", lambda e, j=j, s=s: e.tensor_tensor(out=tmp_sb[:, s, :], in0=x_sb[:, j, :], in1=rstd_sb[:], op=ALU.mult),
                      reads=[B_x, B_rstd], writes=[B_tmp[s]])
                fw.op("dve", lambda e, j=j, s=s: e.tensor_scalar(
                    out=h_sb[:, j, :], in0=tmp_sb[:, s, :], scalar1=coef[:, l, half * 3 + 0, j:j + 1],
                    scalar2=coef[:, l, half * 3 + 1, j:j + 1], op0=ALU.mult, op1=ALU.add),
                    reads=[B_tmp[s], B_coef], writes=[B_h])

        def postnorm_residual(l, half):
            for j in range(DC):
                s = j % 2
                fw.op("pool", lambda e, j=j, s=s: e.tensor_tensor(out=sq_sb[:, s, :], in0=f_sb[:, j, :], in1=f_sb[:, j, :], op=ALU.mult),
                      reads=[B_f[j]], writes=[B_sq[s]])
                fw.op("pe", lambda e, j=j, s=s: e.matmul(PS[6][:], ones_bf[:], sq_sb[:, s, :], start=(j == 0), stop=(j == DC - 1)),
                      reads=[B_sq[s], B_const], writes=[B_ps[6]])
            rstd_from(6)
            for j in range(DC):
                s = j % 2
                fw.op("pool", lambda e, j=j, s=s: e.tensor_tensor(out=tmp_sb[:, s, :], in0=f_sb[:, j, :], in1=rstd_sb[:], op=ALU.mult),
                      reads=[B_f[j], B_rstd], writes=[B_tmp[s]])
                fw.op("dve", lambda e, j=j, s=s: e.scalar_tensor_tensor(
                    out=x_sb[:, j, :], in0=tmp_sb[:, s, :], scalar=coef[:, l, half * 3 + 2, j:j + 1],
                    in1=x_sb[:, j, :], op0=ALU.mult, op1=ALU.add),
                    reads=[B_tmp[s], B_coef, B_x], writes=[B_x])

        def ffn(l):
            blocks_in = woff["ffn_in%d" % l]
            blocks_out = woff["ffn_out%d" % l]
            for jp in range(FHC):
                for gv in range(2):
                    bi = gv * FHC + jp
                    o, sz, ka, kb, e0, e1 = blocks_in[bi]
                    s = load_block(o, sz)
                    pb = (jp * 2 + gv) % 4
                    for kc in range(DC):
                        fw.op("pe", lambda e, s=s, kc=kc, pb=pb: e.matmul(
                            PS[pb][:], ring[:, s, kc * 128:(kc + 1) * 128], h_sb[:, kc, :],
                            start=(kc == 0), stop=(kc == DC - 1)),
                            reads=[B_ring[s], B_h], writes=[B_ps[pb]])
                    u = pb
                    fw.op("pool", lambda e, u=u, bi=bi: e.tensor_copy(out=ub_sb[:, u, 0:2], in_=fcar[:, l, bi, :]),
                          reads=[B_fcar], writes=[B_ub[u]])
                    fw.op("dve", lambda e, u=u, pb=pb: e.tensor_copy(out=ub_sb[:, u, 2:TT + 2], in_=PS[pb][:]),
                          reads=[B_ps[pb]], writes=[B_ub[u]])
                    fw.op("pool", lambda e, u=u, bi=bi: e.tensor_copy(out=fcar[:, l, bi, :], in_=ub_sb[:, u, TT:TT + 2]),
                          reads=[B_ub[u]], writes=[B_fcar])
                    eng1 = "dve"
                    fw.op(eng1, lambda e, u=u, bi=bi: e.tensor_scalar(
                        out=cv_sb[:, u, :], in0=ub_sb[:, u, 2:TT + 2], scalar1=P("fcw%d_2" % l, bi), scalar2=P("fcb%d" % l, bi),
                        op0=ALU.mult, op1=ALU.add), reads=[B_ub[u], B_pv], writes=[B_cv[u]])
                    fw.op(eng1, lambda e, u=u, bi=bi: e.scalar_tensor_tensor(
                        out=cv_sb[:, u, :], in0=ub_sb[:, u, 1:TT + 1], scalar=P("fcw%d_1" % l, bi), in1=cv_sb[:, u, :],
                        op0=ALU.mult, op1=ALU.add), reads=[B_ub[u], B_pv, B_cv[u]], writes=[B_cv[u]])
                    fw.op(eng1, lambda e, u=u, bi=bi: e.scalar_tensor_tensor(
                        out=cv_sb[:, u, :], in0=ub_sb[:, u, 0:TT], scalar=P("fcw%d_0" % l, bi), in1=cv_sb[:, u, :],
                        op0=ALU.mult, op1=ALU.add), reads=[B_ub[u], B_pv, B_cv[u]], writes=[B_cv[u]])
                ug = (jp * 2) % 4
                uv = ug + 1
                sgi = jp % 2
                fw.op("act", lambda e, ug=ug, sgi=sgi: e.activation(out=sg_sb[:, sgi, :], in_=cv_sb[:, ug, :], func=AF.Silu),
                      reads=[B_cv[ug]], writes=[B_sg[sgi]])
                fw.op("pool", lambda e, uv=uv, sgi=sgi, jp=jp: e.tensor_tensor(
                    out=a_sb[:, jp, :], in0=sg_sb[:, sgi, :], in1=cv_sb[:, uv, :], op=ALU.mult),
                    reads=[B_sg[sgi], B_cv[uv]], writes=[B_a[jp]])
            nb = len(blocks_out) // DC
            for i in range(DC):
                pb = 4 + (i % 2)
                first = True
                for q in range(nb):
                    o, sz, ka, kb, e0, e1 = blocks_out[i * nb + q]
                    s = load_block(o, sz)
                    for kc in range(ka, kb):
                        last = (q == nb - 1 and kc == kb - 1)
                        fw.op("pe", lambda e, s=s, kc=kc, ka=ka, pb=pb, first=first, last=last: e.matmul(
                            PS[pb][:], ring[:, s, (kc - ka) * 128:(kc - ka + 1) * 128], a_sb[:, kc, :],
                            start=first, stop=last),
                            reads=[B_ring[s], B_a[kc]], writes=[B_ps[pb]])
                        first = False
                fw.op("dve", lambda e, i=i, pb=pb: e.tensor_copy(out=f_sb[:, i, :], in_=PS[pb][:]),
                      reads=[B_ps[pb]], writes=[B_f[i]])
            postnorm_residual(l, 1)


        ident_bf = sb("ident_bf", [128, 128], BF16)
        negm = sb("negm", [128, 64], F32)
        abc = sb("abc", [128, 2, 64], F32)
        mcar = sb("mcar", [128, 2, 48, 3], F32)
        B_mcar = Buf("mcar")
        sst_d = nc.dram_tensor("sst", [2, 8, 128, 512], F32, kind="Internal").ap()
        sem_st = [fw.new_dma_sem(), fw.new_dma_sem()]
        B_sst = [[Buf("sst%d_%d" % (m_, g_)) for g_ in range(8)] for m_ in range(2)]
        fw.op("pool", lambda e: e.memset(negm[:], 0.0), writes=[B_const])
        import os as _os3
        for _k in range(int(_os3.environ.get("DBG_PAD", "0"))):
            fw.op("dve", lambda e: e.memset(negm[:], 0.0), writes=[B_const])
        fw.op("pool", lambda e: e.memset(mcar[:], 0.0), writes=[B_mcar])
        fw.op("dve", lambda e: e.tensor_copy(out=ident_bf[:], in_=P("ident", 0, 128)), reads=[B_pv], writes=[B_const])
        fw.op("dve", lambda e: e.tensor_scalar(out=negm[:], in0=P("tri", 0, 64), scalar1=1e30, scalar2=-1e30,
                                               op0=ALU.mult, op1=ALU.add), reads=[B_pv], writes=[B_const])
        fw.op("dve", lambda e: e.tensor_copy(out=tri_bf[:], in_=P("tri", 0, 64)), reads=[B_pv], writes=[B_const])
        if use_m(mixers):
            for l_ in range(1, nl, 2):
                fw.op("act", lambda e, l_=l_: e.activation(out=abc[:, l_ // 2, :], in_=P("malog%d" % l_, 0, 64), func=AF.Exp),
                      reads=[B_pv], writes=[B_const])
                fw.op("dve", lambda e, l_=l_: e.tensor_scalar(out=abc[:, l_ // 2, :], in0=abc[:, l_ // 2, :], scalar1=-1.0,
                                                               scalar2=None, op0=ALU.mult), reads=[B_const], writes=[B_const])

        y_all = _V3(scr_bf[:, 0:32 * TT], TT)
        B_yall = [Buf("yall%d" % i) for i in range(32)]
        o_ = [8192]

        def carve_f(n):
            v = scr[:, o_[0]:o_[0] + n]
            o_[0] += n
            return v

        def carve_b(n):
            v = scr_bf[:, 2 * o_[0]:2 * o_[0] + n]
            o_[0] += (n + 1) // 2
            return v

        sz_sb = _V3(carve_b(4 * TT), TT)
        xc_sb = _V3(carve_b(6 * TT), TT)
        S_sb = carve_f(512)
        Sbf_sb = carve_b(512)
        E_sb = carve_b(512)
        Mt_sb = carve_b(512)
        xtok_sb = carve_b(640)
        xdt_sb = carve_b(512)
        xw_sb = carve_b(512)
        ybf_sb = carve_b(512)
        ahi_sb = carve_b(512)
        alo_sb = carve_b(512)
        r1h_sb = carve_b(512)
        r1l_sb = carve_b(512)
        tri_bf = sb("tri_bf", [128, 64], BF16)
        B_ahl = Buf("ahl")
        B_r1 = Buf("r1")
        assert o_[0] <= SCRF, o_[0]
        dt_sb = scr2[:, 0:512]
        adt_sb = scr2[:, 512:1024]
        acs_sb = scr2[:, 1024:1536]
        edec_sb = scr2[:, 1536:2048]
        wend_sb = scr2[:, 2048:2560]
        eacs_sb = scr2[:, 2560:3072]
        yt_sb = scr2[:, 3072:3584]
        t2_sb = scr2[:, 3584:4096]
        B_sz = [Buf("sz%d" % i) for i in range(4)]
        B_xc = [Buf("xc%d" % i) for i in range(6)]
        B_S, B_Sbf, B_E, B_Mt, B_xtok, B_xdt, B_xw, B_ybf = [Buf(n) for n in ("S", "Sbf", "E", "Mt", "xtok", "xdt", "xw", "ybf")]
        B_dt, B_adt, B_acs, B_edec, B_wend, B_eacs, B_yt, B_t2 = [Buf(n) for n in ("dt", "adt", "acs", "edec", "wend", "eacs", "yt", "t2")]

        def v3(ap, a, b):
            return ap.rearrange("p (a b) -> p a b", a=a, b=b)

        def bc_last(ap2, a, b):
            return ap2.unsqueeze(2).to_broadcast([ap2.shape[0], a, b])

        def bc_mid(ap2, a, b):
            return ap2.unsqueeze(1).to_broadcast([ap2.shape[0], a, b])

        def proj16(blk, pb, rhs_sb=None):
            o, sz, ka, kb, e0, e1 = blk
            s = load_block(o, sz)
            w = e1 - e0
            for kc in range(DC):
                fw.op("pe", lambda e, s=s, kc=kc, pb=pb, w=w: e.matmul(
                    PS[pb][0:w, :], ring[:, s, kc * w:(kc + 1) * w], h_sb[:, kc, :],
                    start=(kc == 0), stop=(kc == DC - 1)),
                    reads=[B_ring[s], B_h], writes=[B_ps[pb]])

        def mamba(l, it):
            m = l // 2
            import os as _os
            _st0 = float(_os.environ.get("MAMBA_STOP", "9"))
            if _st0 <= 0:
                return
            o, sz, ka, kb, e0, e1 = woff["m%d_dt" % l][0]
            s = load_block(o, sz)
            for c in range(8):
                for kc in range(DC):
                    fw.op("pe", lambda e, s=s, kc=kc, c=c: e.matmul(
                        PS[4][0:64, c * 64:(c + 1) * 64], h_sb[:, kc, c * 64:(c + 1) * 64], ring[:, s, kc * 64:(kc + 1) * 64],
                        start=(kc == 0), stop=(kc == DC - 1)),
                        reads=[B_ring[s], B_h], writes=[B_ps[4]])
            fw.op("dve", lambda e: e.tensor_tensor(out=v3(dt_sb[0:64, :], 8, 64), in0=v3(PS[4][0:64, :], 8, 64),
                                                   in1=bc_mid(P("mdtb%d" % l, 0, 64)[0:64, :], 8, 64), op=ALU.add),
                  reads=[B_ps[4], B_pv], writes=[B_dt])
            fw.op("act", lambda e: e.activation(out=dt_sb[0:64, :], in_=dt_sb[0:64, :], func=AF.Exp), reads=[B_dt], writes=[B_dt])
            fw.op("dve", lambda e: e.tensor_scalar(out=dt_sb[0:64, :], in0=dt_sb[0:64, :], scalar1=1.0, scalar2=None, op0=ALU.add),
                  reads=[B_dt], writes=[B_dt])
            fw.op("act", lambda e: e.activation(out=dt_sb[0:64, :], in_=dt_sb[0:64, :], func=AF.Ln), reads=[B_dt], writes=[B_dt])
            fw.op("dve", lambda e: e.tensor_tensor(out=v3(adt_sb[0:64, :], 8, 64), in0=v3(dt_sb[0:64, :], 8, 64),
                                                   in1=bc_mid(abc[0:64, m, :], 8, 64), op=ALU.mult),
                  reads=[B_dt, B_const], writes=[B_adt])
            if _st0 <= 0.5:
                return
            fw.op("dve", lambda e: e.tensor_copy(out=ahi_sb[0:64, :], in_=adt_sb[0:64, :]), reads=[B_adt], writes=[B_ahl])
            fw.op("dve", lambda e: e.tensor_tensor(out=alo_sb[0:64, :], in0=adt_sb[0:64, :], in1=ahi_sb[0:64, :], op=ALU.subtract),
                  reads=[B_adt, B_ahl], writes=[B_ahl])
            fw.op("pe", lambda e: e.matmul(PS[5][0:64, :], tri_bf[0:64, :], ahi_sb[0:64, :], start=True, stop=False),
                  reads=[B_ahl, B_const], writes=[B_ps[5]])
            fw.op("pe", lambda e: e.matmul(PS[5][0:64, :], tri_bf[0:64, :], alo_sb[0:64, :], start=False, stop=True),
                  reads=[B_ahl, B_const], writes=[B_ps[5]])
            fw.op("dve", lambda e: e.tensor_copy(out=acs_sb[0:64, :], in_=PS[5][0:64, :]), reads=[B_ps[5]], writes=[B_acs])
            if _st0 <= 0.6:
                return
            fw.op("pe", lambda e: e.matmul(PS[4][:, :], ones_bf[0:64, :], ahi_sb[0:64, :], start=True, stop=False),
                  reads=[B_ahl, B_const], writes=[B_ps[4]])
            fw.op("pe", lambda e: e.matmul(PS[4][:, :], ones_bf[0:64, :], alo_sb[0:64, :], start=False, stop=True),
                  reads=[B_ahl, B_const], writes=[B_ps[4]])
            fw.op("dve", lambda e: e.tensor_copy(out=tmp_sb[:, 0, :], in_=PS[4][:, :]), reads=[B_ps[4]], writes=[B_tmp[0]])
            fw.op("act", lambda e: e.activation(out=edec_sb[:, :], in_=tmp_sb[:, 0, :], func=AF.Exp), reads=[B_tmp[0]], writes=[B_edec])
            if _st0 <= 0.7:
                return
            _var = _os.environ.get("MAMBA_VAR", "")
            if "nosub" not in _var:
                fw.op("dve", lambda e: e.tensor_tensor(out=wend_sb[0:64, :], in0=tmp_sb[0:64, 0, :], in1=acs_sb[0:64, :], op=ALU.subtract),
                      reads=[B_tmp[0], B_acs], writes=[B_wend])
            if "nowend" not in _var:
                fw.op("act", lambda e: e.activation(out=wend_sb[0:64, :], in_=wend_sb[0:64, :], func=AF.Exp), reads=[B_wend], writes=[B_wend])
            if "noeacs" not in _var:
                fw.op("act", lambda e: e.activation(out=eacs_sb[0:64, :], in_=acs_sb[0:64, :], func=AF.Exp), reads=[B_acs], writes=[B_eacs])

            import os as _os
            _stop = float(_os.environ.get("MAMBA_STOP", "9"))
            if _stop <= 1:
                return
            for g in range(8):
                if it == 0 or _os.environ.get("DBG_ZSTATE"):
                    fw.op("pool", lambda e: e.memset(S_sb[:, :], 0.0), writes=[B_S])
                else:
                    fw.dma("sp", lambda e, g=g: e.dma_start(out=S_sb[:, :], in_=sst_d[m, g, :, :]), sem_st[0],
                           reads=[B_sst[m][g], B_h], writes=[B_S])
                fw.op("pool", lambda e: e.tensor_copy(out=Sbf_sb[:, :], in_=S_sb[:, :]), reads=[B_S], writes=[B_Sbf])
                for i in range(4):
                    pb = i % 2
                    proj16(woff["m%d_z%d" % (l, g)][i], pb)
                    fw.op("act", lambda e, i=i, pb=pb: e.activation(out=sz_sb[:, i, :], in_=PS[pb][:, :], func=AF.Silu),
                          reads=[B_ps[pb]], writes=[B_sz[i]])
                for i in range(6):
                    pb = i % 2
                    u = i % 4
                    if i < 4:
                        blk = woff["m%d_x%d" % (l, g)][i]
                        ci = g * 4 + i
                    elif i == 4:
                        blk = woff["m%d_B%d" % (l, g)][0]
                        ci = 32 + g
                    else:
                        blk = woff["m%d_C%d" % (l, g)][0]
                        ci = 40 + g
                    proj16(blk, pb)
                    fw.op("pool", lambda e, u=u, ci=ci: e.tensor_copy(out=ub_sb[:, u, 0:3], in_=mcar[:, m, ci, :]),
                          reads=[B_mcar], writes=[B_ub[u]])
                    fw.op("dve", lambda e, u=u, pb=pb: e.tensor_copy(out=ub_sb[:, u, 3:TT + 3], in_=PS[pb][:, :]),
                          reads=[B_ps[pb]], writes=[B_ub[u]])
                    fw.op("pool", lambda e, u=u, ci=ci: e.tensor_copy(out=mcar[:, m, ci, :], in_=ub_sb[:, u, TT:TT + 3]),
                          reads=[B_ub[u]], writes=[B_mcar])
                    fw.op("dve", lambda e, u=u, ci=ci: e.tensor_scalar(
                        out=cv_sb[:, u, :], in0=ub_sb[:, u, 3:TT + 3], scalar1=P("mcw%d_3" % l, ci), scalar2=P("mcb%d" % l, ci),
                        op0=ALU.mult, op1=ALU.add), reads=[B_ub[u], B_pv], writes=[B_cv[u]])
                    for j in (2, 1, 0):
                        fw.op("dve", lambda e, u=u, ci=ci, j=j: e.scalar_tensor_tensor(
                            out=cv_sb[:, u, :], in0=ub_sb[:, u, j:j + TT], scalar=P("mcw%d_%d" % (l, j), ci), in1=cv_sb[:, u, :],
                            op0=ALU.mult, op1=ALU.add), reads=[B_ub[u], B_pv, B_cv[u]], writes=[B_cv[u]])
                    fw.op("act", lambda e, u=u, i=i: e.activation(out=xc_sb[:, i, :], in_=cv_sb[:, u, :], func=AF.Silu),
                          reads=[B_cv[u]], writes=[B_xc[i]])
                if _stop <= 2:
                    continue
                for c in range(8):
                    cs = slice(c * 64, (c + 1) * 64)
                    hs = slice(c * 64 + g * 8, c * 64 + g * 8 + 8)
                    for i in range(5):
                        fw.op("pe", lambda e, i=i, cs=cs: e.transpose(PT[0:64, i * 128:(i + 1) * 128], xc_sb[:, i, cs], ident_bf[:, :]),
                              reads=[B_xc[i], B_const], writes=[B_pt[0]])
                    fw.op("dve", lambda e: e.tensor_copy(out=xtok_sb[0:64, :], in_=PT[0:64, 0:640]), reads=[B_pt[0]], writes=[B_xtok])
                    fw.op("dve", lambda e, hs=hs: e.tensor_tensor(out=v3(xdt_sb[0:64, :], 8, 64), in0=v3(xtok_sb[0:64, 0:512], 8, 64),
                                                           in1=bc_last(dt_sb[0:64, hs], 8, 64), op=ALU.mult),
                          reads=[B_xtok, B_dt], writes=[B_xdt])
                    fw.op("dve", lambda e, hs=hs: e.tensor_tensor(out=v3(xw_sb[0:64, :], 8, 64), in0=v3(xdt_sb[0:64, :], 8, 64),
                                                           in1=bc_last(wend_sb[0:64, hs], 8, 64), op=ALU.mult),
                          reads=[B_xdt, B_wend], writes=[B_xw])
                    fw.op("dve", lambda e, hs=hs: e.tensor_tensor(out=v3(r1h_sb[0:64, :], 8, 64), in0=bc_last(ahi_sb[0:64, hs], 8, 64),
                                                           in1=bc_mid(tri_bf[0:64, :], 8, 64), op=ALU.mult),
                          reads=[B_ahl, B_const], writes=[B_r1])
                    fw.op("dve", lambda e, hs=hs: e.tensor_tensor(out=v3(r1l_sb[0:64, :], 8, 64), in0=bc_last(alo_sb[0:64, hs], 8, 64),
                                                           in1=bc_mid(tri_bf[0:64, :], 8, 64), op=ALU.mult),
                          reads=[B_ahl, B_const], writes=[B_r1])
                    fw.op("pe", lambda e: e.matmul(PS[2][0:64, :], ones_bf[0:64, 0:64], r1h_sb[0:64, :], start=True, stop=False),
                          reads=[B_r1, B_const], writes=[B_ps[2]])
                    fw.op("pe", lambda e: e.matmul(PS[2][0:64, :], ones_bf[0:64, 0:64], r1l_sb[0:64, :], start=False, stop=True),
                          reads=[B_r1, B_const], writes=[B_ps[2]])
                    fw.op("dve", lambda e, hs=hs: e.tensor_tensor(out=v3(tmp_sb[0:64, 1, :], 8, 64), in0=v3(PS[2][0:64, :], 8, 64),
                                                           in1=bc_last(acs_sb[0:64, hs], 8, 64), op=ALU.subtract),
                          reads=[B_ps[2], B_acs], writes=[B_tmp[1]])
                    fw.op("pool", lambda e: e.tensor_tensor(out=v3(tmp_sb[0:64, 1, :], 8, 64), in0=v3(tmp_sb[0:64, 1, :], 8, 64),
                                                            in1=bc_mid(negm[0:64, :], 8, 64), op=ALU.add),
                          reads=[B_tmp[1], B_const], writes=[B_tmp[1]])
                    fw.op("act", lambda e: e.activation(out=E_sb[0:64, :], in_=tmp_sb[0:64, 1, :], func=AF.Exp),
                          reads=[B_tmp[1]], writes=[B_E])
                    fw.op("pe", lambda e, cs=cs: e.matmul(PS[3][0:64, 0:64], xc_sb[:, 4, cs], xc_sb[:, 5, cs], start=True, stop=True),
                          reads=[B_xc[4], B_xc[5]], writes=[B_ps[3]])
                    fw.op("dve", lambda e: e.tensor_tensor(out=v3(Mt_sb[0:64, :], 8, 64), in0=v3(E_sb[0:64, :], 8, 64),
                                                           in1=bc_mid(PS[3][0:64, 0:64], 8, 64), op=ALU.mult),
                          reads=[B_E, B_ps[3]], writes=[B_Mt])
                    for hh in range(8):
                        fw.op("pe", lambda e, hh=hh: e.matmul(PS[4][0:64, hh * 64:(hh + 1) * 64], Mt_sb[0:64, hh * 64:(hh + 1) * 64],
                                                              xdt_sb[0:64, hh * 64:(hh + 1) * 64], start=True, stop=True),
                              reads=[B_Mt, B_xdt], writes=[B_ps[4]])
                    fw.op("pe", lambda e, cs=cs: e.matmul(PS[5][0:64, :], xc_sb[:, 5, cs], Sbf_sb[:, :], start=True, stop=True),
                          reads=[B_xc[5], B_Sbf], writes=[B_ps[5]])
                    fw.op("dve", lambda e, hs=hs: e.tensor_tensor(out=v3(yt_sb[0:64, :], 8, 64), in0=v3(PS[5][0:64, :], 8, 64),
                                                           in1=bc_last(eacs_sb[0:64, hs], 8, 64), op=ALU.mult),
                          reads=[B_ps[5], B_eacs], writes=[B_yt])
                    fw.op("dve", lambda e: e.tensor_tensor(out=yt_sb[0:64, :], in0=yt_sb[0:64, :], in1=PS[4][0:64, :], op=ALU.add),
                          reads=[B_yt, B_ps[4]], writes=[B_yt])
                    fw.op("dve", lambda e, g=g: e.tensor_tensor(out=v3(t2_sb[0:64, :], 8, 64), in0=v3(xtok_sb[0:64, 0:512], 8, 64),
                                                           in1=bc_last(P("mD%d" % l, g * 8, 8)[0:64, :], 8, 64), op=ALU.mult),
                          reads=[B_xtok, B_pv], writes=[B_t2])
                    fw.op("dve", lambda e: e.tensor_tensor(out=ybf_sb[0:64, :], in0=yt_sb[0:64, :], in1=t2_sb[0:64, :], op=ALU.add),
                          reads=[B_yt, B_t2], writes=[B_ybf])
                    for i in range(4):
                        fw.op("pe", lambda e, i=i: e.transpose(PT[:, 640 + i * 64:640 + (i + 1) * 64], ybf_sb[0:64, i * 128:(i + 1) * 128],
                                                              ident_bf[0:64, 0:64]),
                              reads=[B_ybf, B_const], writes=[B_pt[1]])
                    for i in range(4):
                        fw.op("dve", lambda e, i=i, cs=cs, g=g: e.tensor_tensor(out=y_all[:, g * 4 + i, cs], in0=PT[:, 640 + i * 64:640 + (i + 1) * 64],
                                                                     in1=sz_sb[:, i, cs], op=ALU.mult),
                              reads=[B_pt[1], B_sz[i]], writes=[B_yall[g * 4 + i]])
                    fw.op("pe", lambda e: e.matmul(PS[3][:, :], xtok_sb[0:64, 512:640], xw_sb[0:64, :], start=True, stop=True),
                          reads=[B_xtok, B_xw], writes=[B_ps[3]])
                    fw.op("dve", lambda e, hs=hs: e.tensor_tensor(out=v3(S_sb[:, :], 8, 64), in0=v3(S_sb[:, :], 8, 64),
                                                           in1=bc_last(edec_sb[:, hs], 8, 64), op=ALU.mult),
                          reads=[B_S, B_edec], writes=[B_S])
                    fw.op("dve", lambda e: e.tensor_tensor(out=S_sb[:, :], in0=S_sb[:, :], in1=PS[3][:, :], op=ALU.add),
                          reads=[B_S, B_ps[3]], writes=[B_S])
                    fw.op("pool", lambda e: e.tensor_copy(out=Sbf_sb[:, :], in_=S_sb[:, :]), reads=[B_S], writes=[B_Sbf])
                if _os.environ.get("DBG_G") and g == int(_os.environ["DBG_G"]):
                    break
                fw.dma("sp", lambda e, g=g: e.dma_start(out=sst_d[m, g, :, :], in_=S_sb[:, :]), sem_st[1],
                       reads=[B_S], writes=[B_sst[m][g]])
                for i in range(4):
                    ci = g * 4 + i
                    sidx = i % 2
                    fw.op("pool", lambda e, ci=ci, sidx=sidx: e.tensor_tensor(out=sq_sb[:, sidx, :], in0=y_all[:, ci, :], in1=y_all[:, ci, :], op=ALU.mult),
                          reads=[B_yall[ci]], writes=[B_sq[sidx]])
                    fw.op("pe", lambda e, ci=ci, sidx=sidx: e.matmul(PS[6][:], ones_bf[:], sq_sb[:, sidx, :], start=(ci == 0), stop=(ci == 31)),
                          reads=[B_sq[sidx], B_const], writes=[B_ps[6]])
                    if not _os.environ.get("DBG_DUMP"):
                        fw.op("pool", lambda e, ci=ci: e.tensor_scalar(out=y_all[:, ci, :], in0=y_all[:, ci, :], scalar1=P("mnw%d" % l, ci),
                                                                       scalar2=None, op0=ALU.mult),
                              reads=[B_yall[ci], B_pv], writes=[B_yall[ci]])
            _dd = _os.environ.get("DBG_DUMP", "")
            if _dd == "chunk":
                lst = ((xtok_sb[0:64, 0:512], B_xtok, 512), (xtok_sb[0:64, 512:640], B_xtok, 128), (xdt_sb[0:64, :], B_xdt, 512),
                       (E_sb[0:64, :], B_E, 512), (Mt_sb[0:64, :], B_Mt, 512), (ybf_sb[0:64, :], B_ybf, 512), (yt_sb[0:64, :], B_yt, 512))
                for j, (src, bb, n) in enumerate(lst):
                    fw.op("dve", lambda e, j=j, src=src, n=n: e.tensor_copy(out=x_sb[0:64, j, 0:n], in_=src), reads=[bb, B_x], writes=[B_x])
                fw.op("dve", lambda e: e.tensor_copy(out=x_sb[:, 7, :], in_=S_sb[:, :]), reads=[B_S, B_x], writes=[B_x])
                return
            if _dd == "dts":
                for j, (src, bb) in enumerate(((dt_sb, B_dt), (acs_sb, B_acs), (wend_sb, B_wend), (eacs_sb, B_eacs))):
                    fw.op("dve", lambda e, j=j, src=src: e.tensor_copy(out=x_sb[0:64, j, :], in_=src[0:64, :]), reads=[bb, B_x], writes=[B_x])
                fw.op("dve", lambda e: e.tensor_copy(out=x_sb[:, 4, :], in_=edec_sb[:, :]), reads=[B_edec, B_x], writes=[B_x])
                return
            if _dd == "szxc":
                for j in range(4):
                    fw.op("dve", lambda e, j=j: e.tensor_copy(out=x_sb[:, j, :], in_=sz_sb[:, j, :]), reads=[B_sz[j], B_x], writes=[B_x])
                for j in range(6):
                    fw.op("dve", lambda e, j=j: e.tensor_copy(out=x_sb[:, 4 + j, :], in_=xc_sb[:, j, :]), reads=[B_xc[j], B_x], writes=[B_x])
                return
            if _dd.startswith("yall"):
                hh_ = int(_dd[4:])
                for j in range(DC):
                    fw.op("dve", lambda e, j=j: e.tensor_copy(out=x_sb[:, j, :], in_=y_all[:, hh_ * 16 + j, :]),
                          reads=[B_yall[hh_ * 16 + j], B_x], writes=[B_x])
                return
            if _stop <= 3:
                return
            rstd_from(6, float(SI))
            blocks_out = woff["m%d_out" % l]
            nb = len(blocks_out) // DC
            for i in range(DC):
                pb = i % 2
                first = True
                for q in range(nb):
                    o, sz, ka, kb, e0, e1 = blocks_out[i * nb + q]
                    s = load_block(o, sz)
                    for kc in range(ka, kb):
                        last = (q == nb - 1 and kc == kb - 1)
                        fw.op("pe", lambda e, s=s, kc=kc, ka=ka, pb=pb, first=first, last=last: e.matmul(
                            PS[pb][:], ring[:, s, (kc - ka) * 128:(kc - ka + 1) * 128], y_all[:, kc, :],
                            start=first, stop=last),
                            reads=[B_ring[s], B_yall[kc]], writes=[B_ps[pb]])
                        first = False
                fw.op("dve", lambda e, i=i, pb=pb: e.tensor_tensor(out=f_sb[:, i, :], in0=PS[pb][:], in1=rstd_sb[:], op=ALU.mult),
                      reads=[B_ps[pb], B_rstd], writes=[B_f[i]])
            postnorm_residual(l, 0)


        hcar = sb("hcar", [128, 2, DC], BF16)
        omka = sb("omka", [128, 2, 32], F32)
        mask5 = sb("mask5", [128, 448], BF16)
        B_hcar = Buf("hcar")
        rkv_d = nc.dram_tensor("rkvd", [3, 32, 64, TT], F32, kind="Internal").ap()
        vf_d = nc.dram_tensor("vfd", [32, 64, TT], F32, kind="Internal").ap()
        rst_d = nc.dram_tensor("rstd", [2, 32, 64, 64], F32, kind="Internal").ap()
        B_rkvd = [[Buf("rkvd%d_%d" % (i_, h_)) for h_ in range(32)] for i_ in range(3)]
        B_vfd = [Buf("vfd%d" % h_) for h_ in range(32)]
        B_rst = [[Buf("rst%d_%d" % (r_, h_)) for h_ in range(32)] for r_ in range(2)]
        sem_r = [fw.new_dma_sem() for _ in range(6)]
        fw.op("pool", lambda e: e.memset(hcar[:], 0.0), writes=[B_hcar])
        fw.op("dve", lambda e: e.tensor_tensor(out=mask5[:, 0:64], in0=P("tri", 0, 64), in1=P("ident", 0, 64), op=ALU.subtract),
              reads=[B_pv], writes=[B_const])
        fw.op("dve", lambda e: e.tensor_scalar(out=mask5[:, 64:128], in0=P("tri", 0, 64), scalar1=-1.0, scalar2=1.0, op0=ALU.mult, op1=ALU.add),
              reads=[B_pv], writes=[B_const])
        fw.op("dve", lambda e: e.tensor_copy(out=mask5[:, 128:192], in_=mask5[:, 0:64]), reads=[B_const], writes=[B_const])
        fw.op("dve", lambda e: e.tensor_copy(out=mask5[:, 192:256], in_=P("tri", 0, 64)), reads=[B_pv], writes=[B_const])
        fw.op("dve", lambda e: e.tensor_copy(out=mask5[:, 256:320], in_=P("tri", 0, 64)), reads=[B_pv], writes=[B_const])
        fw.op("dve", lambda e: e.tensor_copy(out=mask5[:, 320:384], in_=P("ident", 0, 64)), reads=[B_pv], writes=[B_const])
        fw.op("dve", lambda e: e.tensor_copy(out=mask5[:, 384:448], in_=P("ident", 0, 64)), reads=[B_pv], writes=[B_const])
        if use_r(mixers):
            for l_ in range(0, nl, 2):
                fw.op("dve", lambda e, l_=l_: e.tensor_scalar(out=omka[:, l_ // 2, :], in0=P("rka_%d" % l_, 0, 32), scalar1=-1.0, scalar2=1.0,
                                                               op0=ALU.mult, op1=ALU.add), reads=[B_pv], writes=[B_const])

        xx_sb = _V3(scr_bf[:, 0:DC * TT], TT)
        xi_sb = _V3(scr_bf[:, DC * TT:2 * DC * TT], TT)
        yh_all = _V3(scr_bf[0:64, 0:32 * TT], TT)
        ro = [8192]

        def rf(n):
            v = scr[:, ro[0]:ro[0] + n]
            ro[0] += n
            return v

        def rb(n):
            v = scr_bf[:, 2 * ro[0]:2 * ro[0] + n]
            ro[0] += (n + 1) // 2
            return v
        l1w = rb(512); l1a = rb(512); l1g = _V3(rb(1024), 512); l1v = rb(512)
        stg_r = _V3(rf(1024), 512)
        r_f = rf(512); k_f = rf(512); v_f = rf(512); lw_f = rf(512); as_f = rf(512)
        c0_f = tmp_sb[:, 0, :]; c1_f = tmp_sb[:, 1, :]
        bon_f = ub_sb[:, 0, 0:TT]; bb_f = ub_sb[:, 1, 0:TT]; k2_f = ub_sb[:, 2, 0:TT]; kkn_f = ub_sb[:, 3, 0:TT]
        rt_b = cv_sb[:, 0, :]; kt_b = cv_sb[:, 1, :]; bt_b = cv_sb[:, 2, :]; at_b = cv_sb[:, 3, :]
        kd_b = sg_sb[:, 0, :]; bd_b = sg_sb[:, 1, :]
        vb_b = rb(512); g_b = rb(512); q_b = rb(512)
        assert ro[0] <= SCRF, ro[0]
        ecl_f = scr2[:, 0:512]; t1_f = scr2[:, 512:1024]; t2_f = scr2[:, 1024:1536]; ytok_f = scr2[:, 1536:2048]
        yfm_f = scr2[:, 2048:2560]
        S_r = scr2[:, 2560:2624]; Sb_r = scr2[:, 2624:2656].bitcast(BF16)
        sc_b = scr2[:, 3424:3648].bitcast(BF16)
        tok_b = scr2[:, 2816:2912].bitcast(BF16)
        PP_b = sc_b[:, 320:448]
        M2_b = scr2[:, 2976:3040].bitcast(BF16)
        X_b = scr2[:, 3040:3072].bitcast(BF16); U_b = scr2[:, 3072:3104].bitcast(BF16)
        st8 = scr2[:, 3104:3168]
        yn_b = scr2[:, 3168:3424].bitcast(BF16)
        BR = {n: Buf("r_" + n) for n in ("xx xi l1w l1a l1g l1v stg0 stg1 r k v lw as kkn k2 bb c0 c1 bon rt kt bt at kd bd vb g q "
                                        "ecl t1 t2 ytok yfm S Sb sc scM tok PP M2 X U st8 yn").split()}
        BR["c0"], BR["c1"] = B_tmp[0], B_tmp[1]
        BR["bon"], BR["bb"], BR["k2"], BR["kkn"] = B_ub[0], B_ub[1], B_ub[2], B_ub[3]
        BR["rt"], BR["kt"], BR["bt"], BR["at"] = B_cv[0], B_cv[1], B_cv[2], B_cv[3]
        BR["kd"], BR["bd"] = B_sg[0], B_sg[1]
        B_yh = [Buf("yh%d" % i) for i in range(32)]

        def v8(ap):
            return ap.rearrange("p (a b) -> p a b", a=8, b=64)

        def proj_xi(blk, pb, w, nrows=None):
            o, sz, ka, kb, e0, e1 = blk
            s = load_block(o, sz)
            for kc in range(DC):
                fw.op("pe", lambda e, s=s, kc=kc, pb=pb, w=w: e.matmul(
                    PS[pb][0:w, :], ring[:, s, kc * w:(kc + 1) * w], xi_sb[:, kc, :],
                    start=(kc == 0), stop=(kc == DC - 1)),
                    reads=[B_ring[s], BR["xi"]], writes=[B_ps[pb]])

        def mix_input(l, i):
            for j in range(DC):
                fw.op("dve", lambda e, j=j: e.scalar_tensor_tensor(out=xi_sb[:, j, :], in0=xx_sb[:, j, :], scalar=P("rmu%d_%d" % (l, i), j),
                                                                   in1=h_sb[:, j, :], op0=ALU.mult, op1=ALU.add),
                      reads=[BR["xx"], B_h, B_pv], writes=[BR["xi"]])

        def rwkv(l, it):
            ri = l // 2
            D64 = slice(0, 64)
            fw.op("dve", lambda e: e.tensor_tensor(out=scr_bf[:, 0:DC * TT].rearrange("p (j t) -> p j t", j=DC)[:, :, 1:TT],
                                                   in0=h_sb[:, :, 0:TT - 1], in1=h_sb[:, :, 1:TT], op=ALU.subtract),
                  reads=[B_h], writes=[BR["xx"]])
            fw.op("dve", lambda e: e.tensor_tensor(out=scr_bf[:, 0:DC * TT].rearrange("p (j t) -> p j t", j=DC)[:, :, 0],
                                                   in0=hcar[:, ri, :], in1=h_sb[:, :, 0], op=ALU.subtract),
                  reads=[B_h, B_hcar], writes=[BR["xx"]])
            fw.op("dve", lambda e: e.tensor_copy(out=hcar[:, ri, :], in_=h_sb[:, :, TT - 1]), reads=[B_h], writes=[B_hcar])
            mix_input(l, 3)
            proj_xi(woff["r%d_w1" % l][0], 0, 96)
            fw.op("act", lambda e: e.activation(out=l1w[0:96, :], in_=PS[0][0:96, :], func=AF.Tanh), reads=[B_ps[0]], writes=[BR["l1w"]])
            mix_input(l, 4)
            proj_xi(woff["r%d_a1" % l][0], 1, 96)
            fw.op("dve", lambda e: e.tensor_copy(out=l1a[0:96, :], in_=PS[1][0:96, :]), reads=[B_ps[1]], writes=[BR["l1a"]])
            mix_input(l, 5)
            for q in range(2):
                proj_xi(woff["r%d_g1" % l][q], q, 128)
                fw.op("act", lambda e, q=q: e.activation(out=l1g[:, q, :], in_=PS[q][:, :], func=AF.Sigmoid), reads=[B_ps[q]], writes=[BR["l1g"]])
            for (i, nm) in ((2, "wv"), (0, "wr"), (1, "wk")):
                mix_input(l, i)
                if i == 2 and l >= 2:
                    proj_xi(woff["r%d_v1" % l][0], 0, 64)
                    fw.op("dve", lambda e: e.tensor_copy(out=l1v[0:64, :], in_=PS[0][0:64, :]), reads=[B_ps[0]], writes=[BR["l1v"]])
                for h in range(32):
                    pb = h % 2
                    proj_xi(woff["r%d_%s" % (l, nm)][h], pb, 64)
                    fw.op("dve", lambda e, pb=pb: e.tensor_copy(out=stg_r[0:64, pb, :], in_=PS[pb][0:64, :]), reads=[B_ps[pb]], writes=[BR["stg%d" % pb]])
                    fw.dma("sp", lambda e, i=i, h=h, pb=pb: e.dma_start(out=rkv_d[i, h, :, :], in_=stg_r[0:64, pb, :]), sem_r[pb],
                           reads=[BR["stg%d" % pb]], writes=[B_rkvd[i][h]])
            for h in range(32):
                cfs = [(r_f, "r", 0), (k_f, "k", 1), (v_f, "v", 2)]
                for (dst, nm, i) in cfs:
                    fw.dma("sp", lambda e, dst=dst, i=i, h=h: e.dma_start(out=dst[0:64, :], in_=rkv_d[i, h, :, :]), sem_r[2 + i],
                           reads=[B_rkvd[i][h]], writes=[BR[nm]])
                o, sz = woff["r%d_l2" % l][h][0:2]
                s = load_block(o, sz)
                fw.op("pe", lambda e, s=s: e.matmul(PS[0][0:64, :], ring[0:96, s, 0:64], l1w[0:96, :], start=True, stop=True),
                      reads=[B_ring[s], BR["l1w"]], writes=[B_ps[0]])
                fw.op("act", lambda e, h=h: e.activation(out=lw_f[D64, :], in_=PS[0][0:64, :], func=AF.Sigmoid, bias=P("rw0_%d" % l, h)[0:64, :]),
                      reads=[B_ps[0], B_pv], writes=[BR["lw"]])
                fw.op("dve", lambda e: e.tensor_scalar(out=lw_f[D64, :], in0=lw_f[D64, :], scalar1=-0.6065306597126334, scalar2=None, op0=ALU.mult),
                      reads=[BR["lw"]], writes=[BR["lw"]])
                fw.op("pe", lambda e, s=s: e.matmul(PS[1][0:64, :], ring[0:96, s, 64:128], l1a[0:96, :], start=True, stop=True),
                      reads=[B_ring[s], BR["l1a"]], writes=[B_ps[1]])
                fw.op("act", lambda e, h=h: e.activation(out=as_f[D64, :], in_=PS[1][0:64, :], func=AF.Sigmoid, bias=P("ra0_%d" % l, h)[0:64, :]),
                      reads=[B_ps[1], B_pv], writes=[BR["as"]])
                fw.op("pe", lambda e, s=s: e.matmul(PS[0][0:64, :], ring[:, s, 128:192], l1g[:, 0, :], start=True, stop=False),
                      reads=[B_ring[s], BR["l1g"]], writes=[B_ps[0]])
                fw.op("pe", lambda e, s=s: e.matmul(PS[0][0:64, :], ring[:, s, 192:256], l1g[:, 1, :], start=False, stop=True),
                      reads=[B_ring[s], BR["l1g"]], writes=[B_ps[0]])
                fw.op("dve", lambda e: e.tensor_copy(out=g_b[D64, :], in_=PS[0][0:64, :]), reads=[B_ps[0]], writes=[BR["g"]])
                if l >= 2:
                    fw.op("pe", lambda e, s=s: e.matmul(PS[1][0:64, :], ring[0:64, s, 256:320], l1v[0:64, :], start=True, stop=True),
                          reads=[B_ring[s], BR["l1v"]], writes=[B_ps[1]])
                    fw.op("act", lambda e, h=h: e.activation(out=t1_f[D64, :], in_=PS[1][0:64, :], func=AF.Sigmoid, bias=P("rv0_%d" % l, h)[0:64, :]),
                          reads=[B_ps[1], B_pv], writes=[BR["t1"]])
                    fw.dma("sp", lambda e, h=h: e.dma_start(out=t2_f[0:64, :], in_=vf_d[h, :, :]), sem_r[5], reads=[B_vfd[h]], writes=[BR["t2"]])
                    fw.op("dve", lambda e: e.tensor_tensor(out=t2_f[D64, :], in0=t2_f[D64, :], in1=v_f[D64, :], op=ALU.subtract),
                          reads=[BR["t2"], BR["v"]], writes=[BR["t2"]])
                    fw.op("dve", lambda e: e.tensor_tensor(out=t2_f[D64, :], in0=t2_f[D64, :], in1=t1_f[D64, :], op=ALU.mult),
                          reads=[BR["t2"], BR["t1"]], writes=[BR["t2"]])
                    fw.op("dve", lambda e: e.tensor_tensor(out=v_f[D64, :], in0=v_f[D64, :], in1=t2_f[D64, :], op=ALU.add),
                          reads=[BR["t2"], BR["v"]], writes=[BR["v"]])
                else:
                    fw.dma("sp", lambda e, h=h: e.dma_start(out=vf_d[h, :, :], in_=v_f[0:64, :]), sem_r[5], reads=[BR["v"]], writes=[B_vfd[h]])
                fw.op("dve", lambda e: e.tensor_copy(out=vb_b[D64, :], in_=v_f[D64, :]), reads=[BR["v"]], writes=[BR["vb"]])
                fw.op("dve", lambda e, h=h: e.tensor_scalar(out=kkn_f[D64, :], in0=k_f[D64, :], scalar1=P("rkk_%d" % l, h)[0:64, :], scalar2=None, op0=ALU.mult),
                      reads=[BR["k"], B_pv], writes=[BR["kkn"]])
                fw.op("dve", lambda e: e.tensor_tensor(out=q_b[D64, :], in0=kkn_f[D64, :], in1=kkn_f[D64, :], op=ALU.mult),
                      reads=[BR["kkn"]], writes=[BR["q"]])
                fw.op("pe", lambda e: e.matmul(PS[1][0:64, :], ones_bf[0:64, 0:64], q_b[D64, :], start=True, stop=True),
                      reads=[BR["q"], B_const], writes=[B_ps[1]])
                fw.op("dve", lambda e: e.tensor_scalar(out=t1_f[D64, :], in0=PS[1][0:64, :], scalar1=1e-24, scalar2=None, op0=ALU.max),
                      reads=[B_ps[1]], writes=[BR["t1"]])
                fw.op("act", lambda e: e.activation(out=t1_f[D64, :], in_=t1_f[D64, :], func=AF.Sqrt), reads=[BR["t1"]], writes=[BR["t1"]])
                fw.op("dve", lambda e: e.reciprocal(out=t1_f[D64, :], in_=t1_f[D64, :]), reads=[BR["t1"]], writes=[BR["t1"]])
                fw.op("dve", lambda e: e.tensor_tensor(out=kkn_f[D64, :], in0=kkn_f[D64, :], in1=t1_f[D64, :], op=ALU.mult),
                      reads=[BR["kkn"], BR["t1"]], writes=[BR["kkn"]])
                fw.op("dve", lambda e, h=h: e.tensor_scalar(out=t1_f[D64, :], in0=as_f[D64, :], scalar1=P("rka_%d" % l, h)[0:64, :],
                                                            scalar2=omka[0:64, ri, h:h + 1], op0=ALU.mult, op1=ALU.add),
                      reads=[BR["as"], B_pv, B_const], writes=[BR["t1"]])
                fw.op("dve", lambda e: e.tensor_tensor(out=k2_f[D64, :], in0=k_f[D64, :], in1=t1_f[D64, :], op=ALU.mult),
                      reads=[BR["k"], BR["t1"]], writes=[BR["k2"]])
                fw.op("dve", lambda e: e.tensor_tensor(out=bb_f[D64, :], in0=kkn_f[D64, :], in1=as_f[D64, :], op=ALU.mult),
                      reads=[BR["kkn"], BR["as"]], writes=[BR["bb"]])
                fw.op("dve", lambda e: e.tensor_tensor(out=t1_f[D64, :], in0=r_f[D64, :], in1=k2_f[D64, :], op=ALU.mult),
                      reads=[BR["r"], BR["k2"]], writes=[BR["t1"]])
                fw.op("dve", lambda e, h=h: e.tensor_scalar(out=q_b[D64, :], in0=t1_f[D64, :], scalar1=P("rrk_%d" % l, h)[0:64, :], scalar2=None, op0=ALU.mult),
                      reads=[BR["t1"], B_pv], writes=[BR["q"]])
                fw.op("pe", lambda e: e.matmul(PS[0][0:64, :], ones_bf[0:64, 0:64], q_b[D64, :], start=True, stop=True),
                      reads=[BR["q"], B_const], writes=[B_ps[0]])
                fw.op("dve", lambda e: e.tensor_tensor(out=bon_f[D64, :], in0=PS[0][0:64, :], in1=v_f[D64, :], op=ALU.mult),
                      reads=[B_ps[0], BR["v"]], writes=[BR["bon"]])
                src, srcn = lw_f, "lw"
                bufs = [(c0_f, "c0"), (c1_f, "c1")]
                for si, sft in enumerate((1, 2, 4, 8, 16, 32)):
                    dst, dstn = bufs[si % 2]
                    fw.op("dve", lambda e, src=src, dst=dst, sft=sft: e.tensor_tensor(out=v8(dst[D64, :])[:, :, sft:64], in0=v8(src[D64, :])[:, :, sft:64],
                                                                                 in1=v8(src[D64, :])[:, :, 0:64 - sft], op=ALU.add),
                          reads=[BR[srcn]], writes=[BR[dstn]])
                    fw.op("dve", lambda e, src=src, dst=dst, sft=sft: e.tensor_copy(out=v8(dst[D64, :])[:, :, 0:sft], in_=v8(src[D64, :])[:, :, 0:sft]),
                          reads=[BR[srcn]], writes=[BR[dstn]])
                    src, srcn = dst, dstn
                cl, cln = src, srcn
                fw.op("act", lambda e, cl=cl: e.activation(out=ecl_f[D64, :], in_=cl[D64, :], func=AF.Exp), reads=[BR[cln]], writes=[BR["ecl"]])
                fw.op("dve", lambda e: e.tensor_tensor(out=rt_b[D64, :], in0=r_f[D64, :], in1=ecl_f[D64, :], op=ALU.mult),
                      reads=[BR["r"], BR["ecl"]], writes=[BR["rt"]])
                fw.op("act", lambda e, cl=cl: e.activation(out=t1_f[D64, :], in_=cl[D64, :], func=AF.Exp, scale=-1.0), reads=[BR[cln]], writes=[BR["t1"]])
                fw.op("dve", lambda e: e.tensor_tensor(out=kt_b[D64, :], in0=k2_f[D64, :], in1=t1_f[D64, :], op=ALU.mult),
                      reads=[BR["k2"], BR["t1"]], writes=[BR["kt"]])
                fw.op("dve", lambda e: e.tensor_tensor(out=bt_b[D64, :], in0=bb_f[D64, :], in1=t1_f[D64, :], op=ALU.mult),
                      reads=[BR["bb"], BR["t1"]], writes=[BR["bt"]])
                fw.op("dve", lambda e, cl=cl: e.tensor_tensor(out=t2_f[D64, :], in0=cl[D64, :], in1=lw_f[D64, :], op=ALU.subtract),
                      reads=[BR[cln], BR["lw"]], writes=[BR["t2"]])
                fw.op("act", lambda e: e.activation(out=t2_f[D64, :], in_=t2_f[D64, :], func=AF.Exp), reads=[BR["t2"]], writes=[BR["t2"]])
                fw.op("dve", lambda e: e.scalar_tensor_tensor(out=at_b[D64, :], in0=kkn_f[D64, :], scalar=-1.0, in1=t2_f[D64, :], op0=ALU.mult, op1=ALU.mult),
                      reads=[BR["kkn"], BR["t2"]], writes=[BR["at"]])
                fw.op("dve", lambda e, cl=cl: e.tensor_tensor(out=v8(t1_f[D64, :]), in0=v8(cl[D64, :])[:, :, 63:64].to_broadcast([64, 8, 64]),
                                                              in1=v8(cl[D64, :]), op=ALU.subtract),
                      reads=[BR[cln]], writes=[BR["t1"]])
                fw.op("act", lambda e: e.activation(out=t1_f[D64, :], in_=t1_f[D64, :], func=AF.Exp), reads=[BR["t1"]], writes=[BR["t1"]])
                fw.op("dve", lambda e: e.tensor_tensor(out=kd_b[D64, :], in0=k2_f[D64, :], in1=t1_f[D64, :], op=ALU.mult),
                      reads=[BR["k2"], BR["t1"]], writes=[BR["kd"]])
                fw.op("dve", lambda e: e.tensor_tensor(out=bd_b[D64, :], in0=bb_f[D64, :], in1=t1_f[D64, :], op=ALU.mult),
                      reads=[BR["bb"], BR["t1"]], writes=[BR["bd"]])
                if it == 0:
                    fw.op("pool", lambda e: e.memset(S_r[D64, :], 0.0), writes=[BR["S"]])
                else:
                    fw.dma("sp", lambda e, h=h: e.dma_start(out=S_r[0:64, :], in_=rst_d[ri, h, :, :]), sem_r[5],
                           reads=[B_rst[ri][h], B_h], writes=[BR["S"]])
                fw.op("pool", lambda e: e.tensor_copy(out=Sb_r[D64, :], in_=S_r[D64, :]), reads=[BR["S"]], writes=[BR["Sb"]])
                for c in range(8):
                    cs = slice(c * 64, (c + 1) * 64)
                    for ti, (src_b, srcn2) in enumerate(((vb_b, "vb"), (kd_b, "kd"), (bd_b, "bd"))):
                        fw.op("pe", lambda e, ti=ti, src_b=src_b, cs=cs: e.transpose(PT[0:64, ti * 64:(ti + 1) * 64], src_b[D64, cs], ident_bf[0:64, 0:64]),
                              reads=[BR[srcn2], B_const], writes=[B_pt[0]])
                    fw.op("dve", lambda e: e.tensor_copy(out=tok_b[D64, :], in_=PT[0:64, 0:192]), reads=[B_pt[0]], writes=[BR["tok"]])
                    prs = ((bt_b, "bt", at_b, "at"), (at_b, "at", bt_b, "bt"), (kt_b, "kt", at_b, "at"), (bt_b, "bt", rt_b, "rt"), (kt_b, "kt", rt_b, "rt"))
                    for pi, (la, lan, rr, rrn) in enumerate(prs):
                        fw.op("pe", lambda e, pi=pi, la=la, rr=rr, cs=cs: e.matmul(PS[2][0:64, pi * 64:(pi + 1) * 64], la[D64, cs], rr[D64, cs], start=True, stop=True),
                              reads=[BR[lan], BR[rrn]], writes=[B_ps[2]])
                    fw.op("dve", lambda e: e.tensor_tensor(out=sc_b[D64, 0:320], in0=PS[2][0:64, 0:320], in1=mask5[0:64, 0:320], op=ALU.mult),
                          reads=[B_ps[2], B_const], writes=[BR["sc"], BR["scM"]])
                    fw.op("dve", lambda e: e.tensor_tensor(out=PP_b[D64, :], in0=sc_b[D64, 0:128], in1=mask5[0:64, 320:448], op=ALU.add),
                          reads=[BR["scM"], B_const], writes=[BR["PP"]])
                    fw.op("pe", lambda e, cs=cs: e.matmul(PS[3][0:64, 0:64], at_b[D64, cs], Sb_r[D64, :], start=True, stop=False),
                          reads=[BR["at"], BR["Sb"]], writes=[B_ps[3]])
                    fw.op("pe", lambda e: e.matmul(PS[3][0:64, 0:64], sc_b[D64, 128:192], tok_b[D64, 0:64], start=False, stop=True),
                          reads=[BR["sc"], BR["tok"]], writes=[B_ps[3]])
                    fw.op("dve", lambda e: e.tensor_copy(out=X_b[D64, :], in_=PS[3][0:64, 0:64]), reads=[B_ps[3]], writes=[BR["X"]])
                    bufA, bufAn, bufB, bufBn = sc_b[D64, 0:128], "scM", M2_b[D64, :], "M2"
                    cur, curn, nxt, nxtn = bufA, bufAn, bufB, bufBn
                    fw.op("pe", lambda e, cur=cur: e.matmul(PS[4][0:64, 0:64], cur[:, 64:128], cur[:, 0:64], start=True, stop=True), reads=[BR[curn]], writes=[B_ps[4]])
                    fw.op("pe", lambda e, cur=cur: e.matmul(PS[4][0:64, 64:128], cur[:, 0:64], cur[:, 64:128], start=True, stop=True), reads=[BR[curn]], writes=[B_ps[4]])
                    fw.op("dve", lambda e, nxt=nxt: e.tensor_copy(out=nxt, in_=PS[4][0:64, 0:128]), reads=[B_ps[4]], writes=[BR[nxtn]])
                    for lev in range(5):
                        cur, curn, nxt, nxtn = nxt, nxtn, cur, curn
                        fw.op("pe", lambda e, cur=cur: e.matmul(PS[4][0:64, 128:192], PP_b[D64, 64:128], cur[:, 0:64], start=True, stop=True),
                              reads=[BR["PP"], BR[curn]], writes=[B_ps[4]])
                        fw.op("pe", lambda e, cur=cur: e.matmul(PS[4][0:64, 192:256], cur[:, 0:64], PP_b[D64, 64:128], start=True, stop=True),
                              reads=[BR["PP"], BR[curn]], writes=[B_ps[4]])
                        if lev < 4:
                            fw.op("pe", lambda e, cur=cur: e.matmul(PS[4][0:64, 0:64], cur[:, 64:128], cur[:, 0:64], start=True, stop=True), reads=[BR[curn]], writes=[B_ps[4]])
                            fw.op("pe", lambda e, cur=cur: e.matmul(PS[4][0:64, 64:128], cur[:, 0:64], cur[:, 64:128], start=True, stop=True), reads=[BR[curn]], writes=[B_ps[4]])
                        fw.op("dve", lambda e: e.tensor_tensor(out=PP_b[D64, :], in0=PP_b[D64, :], in1=PS[4][0:64, 128:256], op=ALU.add),
                              reads=[BR["PP"], B_ps[4]], writes=[BR["PP"]])
                        if lev < 4:
                            fw.op("dve", lambda e, nxt=nxt: e.tensor_copy(out=nxt, in_=PS[4][0:64, 0:128]), reads=[B_ps[4]], writes=[BR[nxtn]])
                    fw.op("pe", lambda e: e.matmul(PS[3][0:64, 64:128], PP_b[D64, 0:64], X_b[D64, :], start=True, stop=True),
                          reads=[BR["PP"], BR["X"]], writes=[B_ps[3]])
                    fw.op("dve", lambda e: e.tensor_copy(out=U_b[D64, :], in_=PS[3][0:64, 64:128]), reads=[B_ps[3]], writes=[BR["U"]])
                    fw.op("pe", lambda e, cs=cs: e.matmul(PS[5][0:64, 0:64], rt_b[D64, cs], Sb_r[D64, :], start=True, stop=False),
                          reads=[BR["rt"], BR["Sb"]], writes=[B_ps[5]])
                    fw.op("pe", lambda e: e.matmul(PS[5][0:64, 0:64], sc_b[D64, 192:256], U_b[D64, :], start=False, stop=False),
                          reads=[BR["sc"], BR["U"]], writes=[B_ps[5]])
                    fw.op("pe", lambda e: e.matmul(PS[5][0:64, 0:64], sc_b[D64, 256:320], tok_b[D64, 0:64], start=False, stop=True),
                          reads=[BR["sc"], BR["tok"]], writes=[B_ps[5]])
                    fw.op("dve", lambda e, cs=cs: e.tensor_copy(out=ytok_f[D64, cs], in_=PS[5][0:64, 0:64]), reads=[B_ps[5]], writes=[BR["ytok"]])
                    fw.op("pe", lambda e: e.matmul(PS[3][0:64, 128:192], tok_b[D64, 128:192], U_b[D64, :], start=True, stop=False),
                          reads=[BR["tok"], BR["U"]], writes=[B_ps[3]])
                    fw.op("pe", lambda e: e.matmul(PS[3][0:64, 128:192], tok_b[D64, 64:128], tok_b[D64, 0:64], start=False, stop=True),
                          reads=[BR["tok"]], writes=[B_ps[3]])
                    fw.op("dve", lambda e, c=c: e.scalar_tensor_tensor(out=S_r[D64, :], in0=S_r[D64, :], scalar=ecl_f[0:64, c * 64 + 63:c * 64 + 64],
                                                                   in1=PS[3][0:64, 128:192], op0=ALU.mult, op1=ALU.add),
                          reads=[BR["S"], BR["ecl"], B_ps[3]], writes=[BR["S"]])
                    fw.op("pool", lambda e: e.tensor_copy(out=Sb_r[D64, :], in_=S_r[D64, :]), reads=[BR["S"]], writes=[BR["Sb"]])
                fw.dma("sp", lambda e, h=h: e.dma_start(out=rst_d[ri, h, :, :], in_=S_r[0:64, :]), sem_r[4], reads=[BR["S"]], writes=[B_rst[ri][h]])
                fw.op("dve", lambda e: e.tensor_reduce(out=st8[D64, 0:8], in_=v8(ytok_f[D64, :]), axis=AX.X, op=ALU.add), reads=[BR["ytok"]], writes=[BR["st8"]])
                fw.op("dve", lambda e: e.tensor_tensor(out=t1_f[D64, :], in0=ytok_f[D64, :], in1=ytok_f[D64, :], op=ALU.mult), reads=[BR["ytok"]], writes=[BR["t1"]])
                fw.op("dve", lambda e: e.tensor_reduce(out=st8[D64, 8:16], in_=v8(t1_f[D64, :]), axis=AX.X, op=ALU.add), reads=[BR["t1"]], writes=[BR["st8"]])
                fw.op("dve", lambda e: e.tensor_scalar(out=st8[D64, 0:16], in0=st8[D64, 0:16], scalar1=1.0 / 64, scalar2=None, op0=ALU.mult), reads=[BR["st8"]], writes=[BR["st8"]])
                fw.op("dve", lambda e: e.tensor_tensor(out=st8[D64, 16:24], in0=st8[D64, 0:8], in1=st8[D64, 0:8], op=ALU.mult), reads=[BR["st8"]], writes=[BR["st8"]])
                fw.op("dve", lambda e: e.tensor_tensor(out=st8[D64, 8:16], in0=st8[D64, 8:16], in1=st8[D64, 16:24], op=ALU.subtract), reads=[BR["st8"]], writes=[BR["st8"]])
                fw.op("dve", lambda e: e.tensor_scalar(out=st8[D64, 8:16], in0=st8[D64, 8:16], scalar1=64e-5, scalar2=None, op0=ALU.add), reads=[BR["st8"]], writes=[BR["st8"]])
                fw.op("act", lambda e: e.activation(out=st8[D64, 8:16], in_=st8[D64, 8:16], func=AF.Sqrt), reads=[BR["st8"]], writes=[BR["st8"]])
                fw.op("dve", lambda e: e.reciprocal(out=st8[D64, 8:16], in_=st8[D64, 8:16]), reads=[BR["st8"]], writes=[BR["st8"]])
                fw.op("dve", lambda e: e.tensor_tensor(out=v8(t1_f[D64, :]), in0=v8(ytok_f[D64, :]), in1=st8[D64, 0:8].unsqueeze(2).to_broadcast([64, 8, 64]), op=ALU.subtract),
                      reads=[BR["ytok"], BR["st8"]], writes=[BR["t1"]])
                fw.op("dve", lambda e: e.tensor_tensor(out=v8(yn_b[D64, :]), in0=v8(t1_f[D64, :]), in1=st8[D64, 8:16].unsqueeze(2).to_broadcast([64, 8, 64]), op=ALU.mult),
                      reads=[BR["t1"], BR["st8"]], writes=[BR["yn"]])
                for c in range(8):
                    fw.op("pe", lambda e, c=c: e.transpose(PT[0:64, 256 + c * 64:256 + (c + 1) * 64], yn_b[D64, c * 64:(c + 1) * 64], ident_bf[0:64, 0:64]),
                          reads=[BR["yn"], B_const], writes=[B_pt[0]])
                fw.op("dve", lambda e, h=h: e.tensor_scalar(out=yfm_f[D64, :], in0=PT[0:64, 256:768], scalar1=P("rlw_%d" % l, h)[0:64, :],
                                                            scalar2=P("rlb_%d" % l, h)[0:64, :], op0=ALU.mult, op1=ALU.add),
                      reads=[B_pt[0], B_pv], writes=[BR["yfm"]])
                fw.op("dve", lambda e: e.tensor_tensor(out=yfm_f[D64, :], in0=yfm_f[D64, :], in1=bon_f[D64, :], op=ALU.add),
                      reads=[BR["yfm"], BR["bon"]], writes=[BR["yfm"]])
                fw.op("dve", lambda e, h=h: e.tensor_tensor(out=yh_all[:, h, :], in0=yfm_f[D64, :], in1=g_b[D64, :], op=ALU.mult),
                      reads=[BR["yfm"], BR["g"]], writes=[B_yh[h]])
            blocks_out = woff["r%d_wo" % l]
            for i in range(DC):
                pb = i % 2
                for q in range(2):
                    o, sz = blocks_out[i * 2 + q][0:2]
                    s = load_block(o, sz)
                    for hh in range(16):
                        hd = q * 16 + hh
                        fw.op("pe", lambda e, s=s, hh=hh, hd=hd, pb=pb, q=q: e.matmul(
                            PS[pb][:], ring[0:64, s, hh * 128:(hh + 1) * 128], yh_all[:, hd, :],
                            start=(q == 0 and hh == 0), stop=(q == 1 and hh == 15)),
                            reads=[B_ring[s], B_yh[hd]], writes=[B_ps[pb]])
                fw.op("dve", lambda e, i=i, pb=pb: e.tensor_copy(out=f_sb[:, i, :], in_=PS[pb][:]), reads=[B_ps[pb]], writes=[B_f[i]])
            postnorm_residual(l, 0)

        xv = x_d.rearrange("(j p) t -> p j t", p=128)
        yv = y_d.rearrange("(j p) t -> p j t", p=128)
        ytok = None
        for it in range(NT):
            t0 = it * TT
            fw.dma("sp", lambda e, t0=t0: e.dma_start(out=x_sb[:], in_=xv[:, :, t0:t0 + TT]), sem_x, writes=[B_x])
            for l in range(nl):
                if use_r(mixers) and l % 2 == 0:
                    prenorm(l, 0)
                    rwkv(l, it)
                if use_m(mixers) and l % 2 == 1:
                    prenorm(l, 0)
                    mamba(l, it)
                import os as _os4
                if _os4.environ.get("DBG_DUMP") and l >= 1:
                    continue
                prenorm(l, 1)
                ffn(l)
            import os as _os2
            if _os2.environ.get("DBG_CLAMP"):
                for j in range(DC):
                    fw.op("dve", lambda e, j=j: e.tensor_scalar(out=x_sb[:, j, :], in0=x_sb[:, j, :], scalar1=7777.0, scalar2=-7777.0,
                                                                 op0=ALU.min, op1=ALU.max), reads=[B_x], writes=[B_x])
            ytok = fw.dma("sp", lambda e, t0=t0: e.dma_start(out=yv[:, :, t0:t0 + TT], in_=x_sb[:]), sem_y, reads=[B_x])
        fw.emit([ytok])
    return nc


_CACHE = {}


def run(inp, T, nl, mixers=True):
    Bn = inp["x"].shape[0]
    pvs = [pack_host(inp, b, nl, mixers) for b in range(Bn)]
    wl = pack_weights(inp, nl, mixers)
    wall = wl.build()
    ada = pack_ada(inp)
    nc = build_nc(T, nl, pvs[0].off, pvs[0].n, mixers)
    in_maps = []
    for b in range(Bn):
        in_maps.append({"x": np.ascontiguousarray(inp["x"][b].T), "pv": pvs[b].build(), "wall": wall, "adaw": ada})
    res = run_bass_kernel_spmd(nc, in_maps, core_ids=list(range(Bn)))
    out = np.stack([np.ascontiguousarray(res.results[b]["y"].T) for b in range(Bn)], axis=0)
    return out


MIXERS = True


def kernel(**inputs):
    inp = {k: np.asarray(v) for k, v in inputs.items()}
    return run(inp, inp["x"].shape[1], NL_DEFAULT, mixers=MIXERS).astype(np.float32)
```

```python
import numpy as np
import concourse.bass as bass
import concourse.mybir as mybir
from concourse.bass_utils import run_bass_kernel_spmd
from contextlib import ExitStack

F32 = mybir.dt.float32
BF16 = mybir.dt.bfloat16
AF = mybir.ActivationFunctionType
ALU = mybir.AluOpType
AX = mybir.AxisListType

D = 2048
DC = D // 128
TT = 512
FH = 5504
FHC = FH // 128
SLOT = 3072
NSLOT = 6
ENGS = ("pe", "act", "dve", "pool", "sp")


class Buf:
    __slots__ = ("name", "w", "r", "excl")

    def __init__(self, name="", excl=False):
        self.name = name
        self.w = None
        self.r = {}
        self.excl = excl


class Op:
    __slots__ = ("fn", "deps", "inc", "dma_sem", "dma_val")

    def __init__(self, fn):
        self.fn = fn
        self.deps = []
        self.inc = False
        self.dma_sem = None
        self.dma_val = 0


class FW:
    def __init__(self, nc):
        self.nc = nc
        self.ops = {e: [] for e in ENGS}
        self.dma_sems = []

    def new_dma_sem(self):
        self.dma_sems.append(0)
        return len(self.dma_sems) - 1

    def _add(self, eng, fn, reads, writes, extra=()):
        writes = list(writes) + [b for b in reads if b.excl]
        reads = [b for b in reads if not b.excl]
        op = Op(fn)
        idx = len(self.ops[eng])
        deps = list(extra)
        for b in reads:
            if b.w is not None:
                deps.append(b.w)
        for b in writes:
            if b.w is not None:
                deps.append(b.w)
            deps.extend(b.r.values())
        out = []
        seen = set()
        for t in deps:
            if t in seen:
                continue
            seen.add(t)
            if t[0] == "e" and t[1] == eng and eng in ("pe", "sp"):
                continue
            out.append(t)
        op.deps = out
        for t in out:
            if t[0] == "e":
                self.ops[t[1]][t[2]].inc = True
        self.ops[eng].append(op)
        return op, idx

    def op(self, eng, fn, reads=(), writes=()):
        op, idx = self._add(eng, fn, reads, writes)
        writes = list(writes) + [b for b in reads if b.excl]
        reads = [b for b in reads if not b.excl]
        tok = ("e", eng, idx)
        for b in reads:
            b.r[("e", eng)] = tok
        for b in writes:
            b.w = tok
            b.r = {}
        return tok

    def dma(self, eng, fn, semidx, reads=(), writes=()):
        extra = []
        if self.dma_sems[semidx] > 0:
            extra.append(("d", semidx, self.dma_sems[semidx]))
        op, idx = self._add(eng, fn, reads, writes, extra)
        self.dma_sems[semidx] += 16
        op.dma_sem = semidx
        op.dma_val = self.dma_sems[semidx]
        tok = ("d", semidx, op.dma_val)
        for b in reads:
            b.r[("d", semidx)] = tok
        for b in writes:
            b.w = tok
            b.r = {}
        return tok

    def emit(self, final_wait_tokens=()):
        nc = self.nc
        with ExitStack() as st:
            esem = {e: st.enter_context(nc.semaphore("s_" + e)) for e in ENGS}
            dsem = [st.enter_context(nc.semaphore("d%d" % i)) for i in range(len(self.dma_sems))]
            val = {}
            for e in ENGS:
                c = 0
                v = []
                for op in self.ops[e]:
                    if op.inc:
                        c += 1
                    v.append(c)
                val[e] = v

            def run(e, engine):
                waited = {}
                for op in self.ops[e]:
                    for t in op.deps:
                        if t[0] == "e":
                            key = ("e", t[1])
                            need = val[t[1]][t[2]]
                            sem = esem[t[1]]
                        else:
                            key = ("d", t[1])
                            need = t[2]
                            sem = dsem[t[1]]
                        if waited.get(key, 0) >= need:
                            continue
                        waited[key] = need
                        engine.wait_ge(sem, need)
                    ins = op.fn(engine)
                    if op.dma_sem is not None:
                        ins.then_inc(dsem[op.dma_sem], 16)
                    elif op.inc:
                        ins.then_inc(esem[e], 1)
                if e == "sp":
                    for t in final_wait_tokens:
                        if t[0] == "d":
                            engine.wait_ge(dsem[t[1]], t[2])
                        else:
                            engine.wait_ge(esem[t[1]], val[t[1]][t[2]])

            with nc.Block() as block:
                @block.tensor
                def _(eng):
                    run("pe", eng)

                @block.scalar
                def _(eng):
                    run("act", eng)

                @block.vector
                def _(eng):
                    run("dve", eng)

                @block.gpsimd
                def _(eng):
                    run("pool", eng)

                @block.sync
                def _(eng):
                    run("sp", eng)


class PV:
    def __init__(self):
        self.cols = []
        self.off = {}
        self.n = 0

    def add(self, name, vec):
        vec = np.asarray(vec, np.float32).reshape(-1)
        assert vec.size % 128 == 0, (name, vec.size)
        m = vec.size // 128
        self.cols.append(vec.reshape(m, 128).T)
        self.off[name] = (self.n, m)
        self.n += m

    def addraw(self, name, arr):
        arr = np.asarray(arr, np.float32)
        self.cols.append(arr)
        self.off[name] = (self.n, arr.shape[1])
        self.n += arr.shape[1]

    def build(self):
        return np.ascontiguousarray(np.concatenate(self.cols, axis=1))


class WL:
    def __init__(self):
        self.blocks = []
        self.off = {}
        self.n = 0

    def add_raw(self, name, arrs):
        lst = []
        for arr in arrs:
            arr = np.asarray(arr, np.float32)
            assert arr.shape[0] == 128
            lst.append((self.n, arr.shape[1], 0, 0, 0, 0))
            self.blocks.append(arr)
            self.n += arr.shape[1]
        self.off[name] = lst

    def add_mat(self, name, W, eb=128, kmax=24):
        Kd, E = W.shape
        assert Kd % 128 == 0
        KC = Kd // 128
        nk = (KC + kmax - 1) // kmax
        ksz = (KC + nk - 1) // nk
        kr = [(a, min(a + ksz, KC)) for a in range(0, KC, ksz)]
        Wr = W.reshape(KC, 128, E)
        lst = []
        for e0 in range(0, E, eb):
            e1 = min(e0 + eb, E)
            for (a, b) in kr:
                blk = Wr[a:b, :, e0:e1].transpose(1, 0, 2).reshape(128, (b - a) * (e1 - e0))
                lst.append((self.n, blk.shape[1], a, b, e0, e1))
                self.blocks.append(blk)
                self.n += blk.shape[1]
        self.off[name] = lst

    def build(self):
        return np.ascontiguousarray(np.concatenate(self.blocks, axis=1))


def wl_layout(name_shapes, eb_map):
    off = {}
    n = 0
    for name, (Kd, E) in name_shapes:
        if Kd == "raw":
            lst = []
            for sz in E:
                lst.append((n, sz, 0, 0, 0, 0))
                n += sz
            off[name] = lst
            continue
        eb = eb_map.get(name, 128)
        kmax = 24
        KC = Kd // 128
        nk = (KC + kmax - 1) // kmax
        ksz = (KC + nk - 1) // nk
        kr = [(a, min(a + ksz, KC)) for a in range(0, KC, ksz)]
        lst = []
        for e0 in range(0, E, eb):
            e1 = min(e0 + eb, E)
            for (a, b) in kr:
                sz = (b - a) * (e1 - e0)
                lst.append((n, sz, a, b, e0, e1))
                n += sz
        off[name] = lst
    return off, n


NL_DEFAULT = 4


SI = 4096


def use_m(mixers):
    return mixers is True or (isinstance(mixers, str) and "m" in mixers)


def use_r(mixers):
    return mixers is True or (isinstance(mixers, str) and "r" in mixers)


def weight_names(nl, mixers=True):
    names = []
    for l in range(nl):
        if use_r(mixers) and l % 2 == 0:
            names.append(("r%d_w1" % l, (D, 96)))
            names.append(("r%d_a1" % l, (D, 96)))
            names.append(("r%d_g1" % l, (D, 256)))
            if l >= 2:
                names.append(("r%d_v1" % l, (D, 64)))
            names.append(("r%d_wv" % l, (D, D)))
            names.append(("r%d_wr" % l, (D, D)))
            names.append(("r%d_wk" % l, (D, D)))
            names.append(("r%d_l2" % l, ("raw", [320] * 32)))
            names.append(("r%d_wo" % l, ("raw", [2048] * 32)))
        if use_m(mixers) and l % 2 == 1:
            names.append(("m%d_dt" % l, (D, 64)))
            for g in range(8):
                names.append(("m%d_z%d" % (l, g), (D, 512)))
                names.append(("m%d_x%d" % (l, g), (D, 512)))
                names.append(("m%d_B%d" % (l, g), (D, 128)))
                names.append(("m%d_C%d" % (l, g), (D, 128)))
            names.append(("m%d_out" % l, (SI, D)))
        names.append(("ffn_in%d" % l, (D, 2 * FH)))
        names.append(("ffn_out%d" % l, (FH, D)))
    return names


def pack_host(inp, b, nl, mixers=True):
    pv = PV()
    pv.add("c", inp["c"][b])
    pv.add("ada_b", inp["ada_b"])
    for l in range(nl):
        pv.add("tab%d" % l, inp["ada_table"][l])
        pv.add("nmpre%d" % l, inp["norm_mix_pre"][l])
        pv.add("nmpost%d" % l, inp["norm_mix_post"][l])
        pv.add("nfpre%d" % l, inp["norm_ffn_pre"][l])
        pv.add("nfpost%d" % l, inp["norm_ffn_post"][l])
        cw = inp["ffn_conv_w"][l]
        for j in range(3):
            pv.add("fcw%d_%d" % (l, j), cw[j])
        pv.add("fcb%d" % l, inp["ffn_conv_b"][l])
        if use_m(mixers) and l % 2 == 1:
            m = l // 2
            for j in range(4):
                pv.add("mcw%d_%d" % (l, j), inp["ssm_conv_w"][m][j])
            pv.add("mcb%d" % l, inp["ssm_conv_b"][m])
            pv.add("mnw%d" % l, inp["ssm_norm"][m])
            pv.addraw("mD%d" % l, np.tile(inp["ssm_d"][m][None, :], (128, 1)))
            pv.addraw("mdtb%d" % l, np.tile(inp["ssm_dt_bias"][m][None, :], (128, 1)))
            pv.addraw("malog%d" % l, np.tile(inp["ssm_a_log"][m][None, :], (128, 1)))
        if use_r(mixers) and l % 2 == 0:
            ri = l // 2

            def a64(name, vec):
                arr = np.asarray(vec, np.float32).reshape(32, 64).T
                pv.addraw(name, np.concatenate([arr, arr], axis=0))
            for i in range(6):
                pv.add("rmu%d_%d" % (l, i), inp["rwkv_mu"][ri][i])
            a64("rw0_%d" % l, inp["rwkv_w0"][ri])
            a64("ra0_%d" % l, inp["rwkv_a0"][ri])
            a64("rkk_%d" % l, inp["rwkv_k_k"][ri])
            a64("rka_%d" % l, inp["rwkv_k_a"][ri])
            a64("rrk_%d" % l, inp["rwkv_r_k"][ri])
            a64("rlw_%d" % l, inp["rwkv_ln_w"][ri])
            a64("rlb_%d" % l, inp["rwkv_ln_b"][ri])
            if l >= 2:
                a64("rv0_%d" % l, inp["rwkv_v0"][ri - 1])
    tri = np.triu(np.ones((64, 64), np.float32))
    pv.addraw("tri", np.concatenate([tri, tri], axis=0))
    pv.addraw("ident", np.eye(128, dtype=np.float32))
    return pv


def pack_weights(inp, nl, mixers=True):
    wl = WL()
    for l in range(nl):
        if use_r(mixers) and l % 2 == 0:
            ri = l // 2
            wl.add_mat("r%d_w1" % l, inp["rwkv_w1"][ri], eb=96)
            wl.add_mat("r%d_a1" % l, inp["rwkv_a1"][ri], eb=96)
            wl.add_mat("r%d_g1" % l, inp["rwkv_g1"][ri], eb=128)
            if l >= 2:
                wl.add_mat("r%d_v1" % l, inp["rwkv_v1"][ri - 1], eb=64)
            wl.add_mat("r%d_wv" % l, inp["rwkv_w_rkv"][ri][2], eb=64)
            wl.add_mat("r%d_wr" % l, inp["rwkv_w_rkv"][ri][0], eb=64)
            wl.add_mat("r%d_wk" % l, inp["rwkv_w_rkv"][ri][1], eb=64)
            blks = []
            for h in range(32):
                hs_ = slice(h * 64, (h + 1) * 64)
                b_ = np.zeros((128, 320), np.float32)
                b_[0:96, 0:64] = inp["rwkv_w2"][ri][:, hs_]
                b_[0:96, 64:128] = inp["rwkv_a2"][ri][:, hs_]
                b_[:, 128:192] = inp["rwkv_g2"][ri][0:128, hs_]
                b_[:, 192:256] = inp["rwkv_g2"][ri][128:256, hs_]
                if l >= 2:
                    b_[0:64, 256:320] = inp["rwkv_v2"][ri - 1][:, hs_]
                blks.append(b_)
            wl.add_raw("r%d_l2" % l, blks)
            Wo = inp["rwkv_w_o"][ri].reshape(32, 64, 16, 128)
            blks = []
            for i in range(16):
                for q in range(2):
                    b_ = np.zeros((128, 2048), np.float32)
                    b_[0:64, :] = Wo[q * 16:(q + 1) * 16, :, i, :].transpose(1, 0, 2).reshape(64, 2048)
                    blks.append(b_)
            wl.add_raw("r%d_wo" % l, blks)
        if use_m(mixers) and l % 2 == 1:
            m = l // 2
            W = inp["ssm_w_in"][m]
            wl.add_mat("m%d_dt" % l, W[:, 2 * SI + 2048:], eb=64)
            for g in range(8):
                wl.add_mat("m%d_z%d" % (l, g), W[:, g * 512:(g + 1) * 512])
                wl.add_mat("m%d_x%d" % (l, g), W[:, SI + g * 512:SI + (g + 1) * 512])
                wl.add_mat("m%d_B%d" % (l, g), W[:, 2 * SI + g * 128:2 * SI + (g + 1) * 128])
                wl.add_mat("m%d_C%d" % (l, g), W[:, 2 * SI + 1024 + g * 128:2 * SI + 1024 + (g + 1) * 128])
            wl.add_mat("m%d_out" % l, inp["ssm_w_out"][m])
        wl.add_mat("ffn_in%d" % l, inp["ffn_w_in"][l])
        wl.add_mat("ffn_out%d" % l, inp["ffn_w_out"][l])
    return wl


def pack_ada(inp):
    W = inp["ada_w"].reshape(16, 128, 96, 128)
    A = W.transpose(2, 0, 1, 3)
    A = A.reshape(96, 2, 8, 128, 128).transpose(3, 0, 1, 2, 4)
    return np.ascontiguousarray(A.reshape(128, 96 * 2 * 8 * 128))


def build_nc(T, nl, pvoff, npv, mixers=True):
    NT = T // TT
    ebm = {("m%d_dt" % l): 64 for l in range(nl)}
    for l in range(nl):
        ebm["r%d_w1" % l] = 96
        ebm["r%d_a1" % l] = 96
        ebm["r%d_v1" % l] = 64
        for nm in ("wr", "wk", "wv"):
            ebm["r%d_%s" % (l, nm)] = 64
    woff, WTOT = wl_layout(weight_names(nl, mixers), ebm)
    nc = bass.Bass("TRN2", target_bir_lowering=False)
    x_d = nc.dram_tensor("x", [D, T], F32, kind="ExternalInput").ap()
    pv_d = nc.dram_tensor("pv", [128, npv], F32, kind="ExternalInput").ap()
    wall_d = nc.dram_tensor("wall", [128, WTOT], F32, kind="ExternalInput").ap()
    ada_d = nc.dram_tensor("adaw", [128, 96 * 2048], F32, kind="ExternalInput").ap()
    y_d = nc.dram_tensor("y", [D, T], F32, kind="ExternalOutput").ap()
    _wn = weight_names(nl, mixers)
    _first = {}
    for _name, _ in _wn:
        _l = int(_name.split("_")[0][1:]) if _name[0] in "mr" else int(_name[-1])
        if _l not in _first:
            _first[_l] = woff[_name][0][0]
    seg_starts = sorted(_first.values())
    seg_ends = seg_starts[1:] + [WTOT]
    wbf_segs = [nc.dram_tensor("wbf%d" % i, [128, b - a], BF16, kind="Internal").ap()
                for i, (a, b) in enumerate(zip(seg_starts, seg_ends))]

    def wbf_ap(o, sz):
        for (a, b, t) in zip(seg_starts, seg_ends, wbf_segs):
            if a <= o and o + sz <= b:
                return t[:, o - a:o - a + sz]
        raise AssertionError((o, sz))
    fw = FW(nc)
    st = ExitStack()

    def sb(name, shape, dt):
        return st.enter_context(nc.sbuf_tensor(name, shape, dt))

    def ps(name, shape, dt=F32):
        return st.enter_context(nc.psum_tensor(name, shape, dt))

    with st:
        pvs = sb("pvs", [128, npv], F32)
        x_sb = sb("x_sb", [128, DC, TT], F32)
        h_sb = sb("h_sb", [128, DC, TT], BF16)
        SCRF = 15872
        scr = sb("scr", [128, SCRF], F32)
        scr_bf = scr[:].bitcast(BF16)
        scr2 = sb("scr2", [128, 4096], F32)
        f_flat = scr2[:].bitcast(BF16)
        lm_sb = scr2[:, 0:nl * 96].rearrange("p (l n) -> p l n", l=nl)
        a_flat = scr_bf[:, 0:FHC * TT]

        class _V3:
            def __init__(self, flat, n):
                self.flat, self.n = flat, n

            def __getitem__(self, key):
                p, j, t = key
                assert isinstance(j, int)
                return self.flat[p, j * self.n:(j + 1) * self.n][:, t]

        f_sb = _V3(f_flat, TT)
        a_sb = _V3(a_flat, TT)
        sq_sb = sb("sq_sb", [128, 2, TT], BF16)
        rstd_sb = sb("rstd_sb", [128, TT], F32)
        tmp_sb = sb("tmp_sb", [128, 2, TT], F32)
        ub_sb = sb("ub_sb", [128, 4, TT + 3], F32)
        cv_sb = sb("cv_sb", [128, 4, TT], BF16)
        sg_sb = sb("sg_sb", [128, 2, TT], BF16)
        ring = sb("ring", [128, NSLOT, SLOT], BF16)
        ones_bf = sb("ones_bf", [128, 128], BF16)
        mod_sb = sb("mod_sb", [128, 96], F32)
        coef = sb("coef", [128, nl, 6, DC], F32)
        sc_sb = sb("sc_sb", [128, DC], F32)
        fcar = sb("fcar", [128, nl, 2 * FHC, 2], F32)
        PS = [ps("ps%d" % i, [128, 512]) for i in range(7)]
        PT = ps("pt", [128, 1024], BF16)

        B_pv = Buf("pv")
        B_x = Buf("x")
        B_h = Buf("h")
        B_f = [Buf("f%d" % i) for i in range(DC)]
        B_a = [Buf("a%d" % i) for i in range(FHC)]
        B_sq = [Buf("sq0"), Buf("sq1")]
        B_rstd = Buf("rstd")
        B_tmp = [Buf("tmp%d" % i) for i in range(2)]
        B_ub = [Buf("ub%d" % i) for i in range(4)]
        B_cv = [Buf("cv%d" % i) for i in range(4)]
        B_sg = [Buf("sg%d" % i) for i in range(2)]
        B_ring = [Buf("ring%d" % i) for i in range(NSLOT)]
        B_ps = [Buf("ps%d" % i, excl=True) for i in range(7)]
        _bpt = Buf("pt", excl=True)
        B_pt = [_bpt, _bpt]
        B_const = Buf("const")
        B_mod = Buf("mod")
        B_coef = Buf("coef")
        B_fcar = Buf("fcar")
        ring_sem = [fw.new_dma_sem() for _ in range(NSLOT)]
        sem_misc = fw.new_dma_sem()
        sem_x = fw.new_dma_sem()
        sem_y = fw.new_dma_sem()

        def P(name, j=0, n=1):
            o, m = pvoff[name]
            return pvs[:, o + j:o + j + n]

        fw.dma("sp", lambda e: e.dma_start(out=pvs[:], in_=pv_d[:, :]), sem_misc, writes=[B_pv])
        fw.op("pool", lambda e: e.memset(ones_bf[:], 1.0), writes=[B_const])
        fw.op("pool", lambda e: e.memset(fcar[:], 0.0), writes=[B_fcar])

        CH = 32768
        pre_chunks = []
        pre_sems = [fw.new_dma_sem() for _ in range(4)]
        ci = 0
        for (sa, sbnd) in zip(seg_starts, seg_ends):
            for c0 in range(sa, sbnd, CH):
                c1 = min(c0 + CH, sbnd)
                bb = Buf("pre%d" % ci)
                fw.dma("pool", lambda e, c0=c0, c1=c1: e.dma_start(out=wbf_ap(c0, c1 - c0), in_=wall_d[:, c0:c1]),
                       pre_sems[ci % 4], writes=[bb])
                pre_chunks.append((c0, c1, bb))
                ci += 1

        def pre_bufs(o, sz):
            return [bb for (c0, c1, bb) in pre_chunks if c0 < o + sz and c1 > o]

        ring_state = {"i": 0}

        def load_block(o, sz):
            s = ring_state["i"] % NSLOT
            ring_state["i"] += 1
            fw.dma("sp", lambda e, s=s, o=o, sz=sz: e.dma_start(out=ring[:, s, 0:sz], in_=wbf_ap(o, sz)),
                   ring_sem[s], reads=pre_bufs(o, sz), writes=[B_ring[s]])
            return s

        fw.op("act", lambda e: e.activation(out=sc_sb[:], in_=P("c", 0, DC), func=AF.Silu),
              reads=[B_pv], writes=[B_mod])
        stg = scr[:, 0:2048].rearrange("p (s n) -> p s n", s=2)
        B_stg = [Buf("stg0"), Buf("stg1")]
        stg_sem = [fw.new_dma_sem(), fw.new_dma_sem()]
        k = 0
        for j in range(96):
            for half in range(2):
                s = k % 2
                o = (j * 2 + half) * 1024
                fw.dma("sp", lambda e, s=s, o=o: e.dma_start(out=stg[:, s, :], in_=ada_d[:, o:o + 1024]),
                       stg_sem[s], writes=[B_stg[s]])
                for kc in range(8):
                    kk = half * 8 + kc
                    fw.op("pe", lambda e, s=s, kc=kc, kk=kk, j=j: e.matmul(
                        PS[0][:, j:j + 1], stg[:, s, kc * 128:(kc + 1) * 128], sc_sb[:, kk:kk + 1],
                        start=(kk == 0), stop=(kk == 15)),
                        reads=[B_stg[s], B_mod], writes=[B_ps[0]])
                k += 1
        fw.op("dve", lambda e: e.tensor_tensor(out=mod_sb[:], in0=PS[0][:, 0:96], in1=P("ada_b", 0, 96), op=ALU.add),
              reads=[B_ps[0], B_pv], writes=[B_mod])
        for l in range(nl):
            fw.op("dve", lambda e, l=l: e.tensor_tensor(out=lm_sb[:, l, :], in0=mod_sb[:], in1=P("tab%d" % l, 0, 96), op=ALU.add),
                  reads=[B_mod, B_pv], writes=[B_coef])
            for (half, pre, post) in ((0, "nmpre", "nmpost"), (1, "nfpre", "nfpost")):
                base = half * 48
                fw.op("dve", lambda e, l=l, base=base, half=half, pre=pre: e.scalar_tensor_tensor(
                    out=coef[:, l, half * 3 + 0, :], in0=lm_sb[:, l, base + 16:base + 32], scalar=1.0,
                    in1=P("%s%d" % (pre, l), 0, DC), op0=ALU.add, op1=ALU.mult),
                    reads=[B_coef, B_pv], writes=[B_coef])
                fw.op("dve", lambda e, l=l, base=base, half=half: e.tensor_copy(
                    out=coef[:, l, half * 3 + 1, :], in_=lm_sb[:, l, base:base + 16]),
                    reads=[B_coef], writes=[B_coef])
                fw.op("dve", lambda e, l=l, base=base, half=half, post=post: e.tensor_tensor(
                    out=coef[:, l, half * 3 + 2, :], in0=lm_sb[:, l, base + 32:base + 48],
                    in1=P("%s%d" % (post, l), 0, DC), op=ALU.mult),
                    reads=[B_coef, B_pv], writes=[B_coef])

        def rstd_from(psb, n=float(D)):
            fw.op("dve", lambda e: e.tensor_scalar(out=rstd_sb[:], in0=PS[psb][:], scalar1=1.0 / n, scalar2=1e-6,
                                                   op0=ALU.mult, op1=ALU.add),
                  reads=[B_ps[psb]], writes=[B_rstd])
            fw.op("act", lambda e: e.activation(out=rstd_sb[:], in_=rstd_sb[:], func=AF.Sqrt),
                  reads=[B_rstd], writes=[B_rstd])
            fw.op("dve", lambda e: e.reciprocal(out=rstd_sb[:], in_=rstd_sb[:]),
                  reads=[B_rstd], writes=[B_rstd])

        def prenorm(l, half):
            for j in range(DC):
                s = j % 2
                fw.op("pool", lambda e, j=j, s=s: e.tensor_tensor(out=sq_sb[:, s, :], in0=x_sb[:, j, :], in1=x_sb[:, j, :], op=ALU.mult),
                      reads=[B_x], writes=[B_sq[s]])
                fw.op("pe", lambda e, j=j, s=s: e.matmul(PS[6][:], ones_bf[:], sq_sb[:, s, :], start=(j == 0), stop=(j == DC - 1)),
                      reads=[B_sq[s], B_const], writes=[B_ps[6]])
            rstd_from(6)
            for j in range(DC):
                s = j % 2
                fw.op("pool", lambda e, j=j, s=s: e.tensor_tensor(out=tmp_sb[:, s, :], in0=x_sb[:, j, :], in1=rstd_sb[:], op=ALU.mult),
                      reads=[B_x, B_rstd], writes=[B_tmp[s]])
                fw.op("dve", lambda e, j=j, s=s: e.tensor_scalar(
                    out=h_sb[:, j, :], in0=tmp_sb[:, s, :], scalar1=coef[:, l, half * 3 + 0, j:j + 1],
                    scalar2=coef[:, l, half * 3 + 1, j:j + 1], op0=ALU.mult, op1=ALU.add),
                    reads=[B_tmp[s], B_coef], writes=[B_h])

        def postnorm_residual(l, half):
            for j in range(DC):
                s = j % 2
                fw.op("pool", lambda e, j=j, s=s: e.tensor_tensor(out=sq_sb[:, s, :], in0=f_sb[:, j, :], in1=f_sb[:, j, :], op=ALU.mult),
                      reads=[B_f[j]], writes=[B_sq[s]])
                fw.op("pe", lambda e, j=j, s=s: e.matmul(PS[6][:], ones_bf[:], sq_sb[:, s, :], start=(j == 0), stop=(j == DC - 1)),
                      reads=[B_sq[s], B_const], writes=[B_ps[6]])
            rstd_from(6)
            for j in range(DC):
                s = j % 2
                fw.op("pool", lambda e, j=j, s=s: e.tensor_tensor(out=tmp_sb[:, s, :], in0=f_sb[:, j, :], in1=rstd_sb[:], op=ALU.mult),
                      reads=[B_f[j], B_rstd], writes=[B_tmp[s]])
                fw.op("dve", lambda e, j=j, s=s: e.scalar_tensor_tensor(
                    out=x_sb[:, j, :], in0=tmp_sb[:, s, :], scalar=coef[:, l, half * 3 + 2, j:j + 1],
                    in1=x_sb[:, j, :], op0=ALU.mult, op1=ALU.add),
                    reads=[B_tmp[s], B_coef, B_x], writes=[B_x])

        def ffn(l):
            blocks_in = woff["ffn_in%d" % l]
            blocks_out = woff["ffn_out%d" % l]
            for jp in range(FHC):
                for gv in range(2):
                    bi = gv * FHC + jp
                    o, sz, ka, kb, e0, e1 = blocks_in[bi]
                    s = load_block(o, sz)
                    pb = (jp * 2 + gv) % 4
                    for kc in range(DC):
                        fw.op("pe", lambda e, s=s, kc=kc, pb=pb: e.matmul(
                            PS[pb][:], ring[:, s, kc * 128:(kc + 1) * 128], h_sb[:, kc, :],
                            start=(kc == 0), stop=(kc == DC - 1)),
                            reads=[B_ring[s], B_h], writes=[B_ps[pb]])
                    u = pb
                    fw.op("pool", lambda e, u=u, bi=bi: e.tensor_copy(out=ub_sb[:, u, 0:2], in_=fcar[:, l, bi, :]),
                          reads=[B_fcar], writes=[B_ub[u]])
                    fw.op("dve", lambda e, u=u, pb=pb: e.tensor_copy(out=ub_sb[:, u, 2:TT + 2], in_=PS[pb][:]),
                          reads=[B_ps[pb]], writes=[B_ub[u]])
                    fw.op("pool", lambda e, u=u, bi=bi: e.tensor_copy(out=fcar[:, l, bi, :], in_=ub_sb[:, u, TT:TT + 2]),
                          reads=[B_ub[u]], writes=[B_fcar])
                    eng1 = "dve"
                    fw.op(eng1, lambda e, u=u, bi=bi: e.tensor_scalar(
                        out=cv_sb[:, u, :], in0=ub_sb[:, u, 2:TT + 2], scalar1=P("fcw%d_2" % l, bi), scalar2=P("fcb%d" % l, bi),
                        op0=ALU.mult, op1=ALU.add), reads=[B_ub[u], B_pv], writes=[B_cv[u]])
                    fw.op(eng1, lambda e, u=u, bi=bi: e.scalar_tensor_tensor(
                        out=cv_sb[:, u, :], in0=ub_sb[:, u, 1:TT + 1], scalar=P("fcw%d_1" % l, bi), in1=cv_sb[:, u, :],
                        op0=ALU.mult, op1=ALU.add), reads=[B_ub[u], B_pv, B_cv[u]], writes=[B_cv[u]])
                    fw.op(eng1, lambda e, u=u, bi=bi: e.scalar_tensor_tensor(
                        out=cv_sb[:, u, :], in0=ub_sb[:, u, 0:TT], scalar=P("fcw%d_0" % l, bi), in1=cv_sb[:, u, :],
                        op0=ALU.mult, op1=ALU.add), reads=[B_ub[u], B_pv, B_cv[u]], writes=[B_cv[u]])
                ug = (jp * 2) % 4
                uv = ug + 1
                sgi = jp % 2
                fw.op("act", lambda e, ug=ug, sgi=sgi: e.activation(out=sg_sb[:, sgi, :], in_=cv_sb[:, ug, :], func=AF.Silu),
                      reads=[B_cv[ug]], writes=[B_sg[sgi]])
                fw.op("pool", lambda e, uv=uv, sgi=sgi, jp=jp: e.tensor_tensor(
                    out=a_sb[:, jp, :], in0=sg_sb[:, sgi, :], in1=cv_sb[:, uv, :], op=ALU.mult),
                    reads=[B_sg[sgi], B_cv[uv]], writes=[B_a[jp]])
            nb = len(blocks_out) // DC
            for i in range(DC):
                pb = 4 + (i % 2)
                first = True
                for q in range(nb):
                    o, sz, ka, kb, e0, e1 = blocks_out[i * nb + q]
                    s = load_block(o, sz)
                    for kc in range(ka, kb):
                        last = (q == nb - 1 and kc == kb - 1)
                        fw.op("pe", lambda e, s=s, kc=kc, ka=ka, pb=pb, first=first, last=last: e.matmul(
                            PS[pb][:], ring[:, s, (kc - ka) * 128:(kc - ka + 1) * 128], a_sb[:, kc, :],
                            start=first, stop=last),
                            reads=[B_ring[s], B_a[kc]], writes=[B_ps[pb]])
                        first = False
                fw.op("dve", lambda e, i=i, pb=pb: e.tensor_copy(out=f_sb[:, i, :], in_=PS[pb][:]),
                      reads=[B_ps[pb]], writes=[B_f[i]])
            postnorm_residual(l, 1)


        ident_bf = sb("ident_bf", [128, 128], BF16)
        negm = sb("negm", [128, 64], F32)
        abc = sb("abc", [128, 2, 64], F32)
        mcar = sb("mcar", [128, 2, 48, 3], F32)
        B_mcar = Buf("mcar")
        sst_d = nc.dram_tensor("sst", [2, 8, 128, 512], F32, kind="Internal").ap()
        sem_st = [fw.new_dma_sem(), fw.new_dma_sem()]
        B_sst = [[Buf("sst%d_%d" % (m_, g_)) for g_ in range(8)] for m_ in range(2)]
        fw.op("pool", lambda e: e.memset(negm[:], 0.0), writes=[B_const])
        import os as _os3
        for _k in range(int(_os3.environ.get("DBG_PAD", "0"))):
            fw.op("dve", lambda e: e.memset(negm[:], 0.0), writes=[B_const])
        fw.op("pool", lambda e: e.memset(mcar[:], 0.0), writes=[B_mcar])
        fw.op("dve", lambda e: e.tensor_copy(out=ident_bf[:], in_=P("ident", 0, 128)), reads=[B_pv], writes=[B_const])
        fw.op("dve", lambda e: e.tensor_scalar(out=negm[:], in0=P("tri", 0, 64), scalar1=1e30, scalar2=-1e30,
                                               op0=ALU.mult, op1=ALU.add), reads=[B_pv], writes=[B_const])
        fw.op("dve", lambda e: e.tensor_copy(out=tri_bf[:], in_=P("tri", 0, 64)), reads=[B_pv], writes=[B_const])
        if use_m(mixers):
            for l_ in range(1, nl, 2):
                fw.op("act", lambda e, l_=l_: e.activation(out=abc[:, l_ // 2, :], in_=P("malog%d" % l_, 0, 64), func=AF.Exp),
                      reads=[B_pv], writes=[B_const])
                fw.op("dve", lambda e, l_=l_: e.tensor_scalar(out=abc[:, l_ // 2, :], in0=abc[:, l_ // 2, :], scalar1=-1.0,
                                                               scalar2=None, op0=ALU.mult), reads=[B_const], writes=[B_const])

        y_all = _V3(scr_bf[:, 0:32 * TT], TT)
        B_yall = [Buf("yall%d" % i) for i in range(32)]
        o_ = [8192]

        def carve_f(n):
            v = scr[:, o_[0]:o_[0] + n]
            o_[0] += n
            return v

        def carve_b(n):
            v = scr_bf[:, 2 * o_[0]:2 * o_[0] + n]
            o_[0] += (n + 1) // 2
            return v

        sz_sb = _V3(carve_b(4 * TT), TT)
        xc_sb = _V3(carve_b(6 * TT), TT)
        S_sb = carve_f(512)
        Sbf_sb = carve_b(512)
        E_sb = carve_b(512)
        Mt_sb = carve_b(512)
        xtok_sb = carve_b(640)
        xdt_sb = carve_b(512)
        xw_sb = carve_b(512)
        ybf_sb = carve_b(512)
        ahi_sb = carve_b(512)
        alo_sb = carve_b(512)
        r1h_sb = carve_b(512)
        r1l_sb = carve_b(512)
        tri_bf = sb("tri_bf", [128, 64], BF16)
        B_ahl = Buf("ahl")
        B_r1 = Buf("r1")
        assert o_[0] <= SCRF, o_[0]
        dt_sb = scr2[:, 0:512]
        adt_sb = scr2[:, 512:1024]
        acs_sb = scr2[:, 1024:1536]
        edec_sb = scr2[:, 1536:2048]
        wend_sb = scr2[:, 2048:2560]
        eacs_sb = scr2[:, 2560:3072]
        yt_sb = scr2[:, 3072:3584]
        t2_sb = scr2[:, 3584:4096]
        B_sz = [Buf("sz%d" % i) for i in range(4)]
        B_xc = [Buf("xc%d" % i) for i in range(6)]
        B_S, B_Sbf, B_E, B_Mt, B_xtok, B_xdt, B_xw, B_ybf = [Buf(n) for n in ("S", "Sbf", "E", "Mt", "xtok", "xdt", "xw", "ybf")]
        B_dt, B_adt, B_acs, B_edec, B_wend, B_eacs, B_yt, B_t2 = [Buf(n) for n in ("dt", "adt", "acs", "edec", "wend", "eacs", "yt", "t2")]

        def v3(ap, a, b):
            return ap.rearrange("p (a b) -> p a b", a=a, b=b)

        def bc_last(ap2, a, b):
            return ap2.unsqueeze(2).to_broadcast([ap2.shape[0], a, b])

        def bc_mid(ap2, a, b):
            return ap2.unsqueeze(1).to_broadcast([ap2.shape[0], a, b])

        def proj16(blk, pb, rhs_sb=None):
            o, sz, ka, kb, e0, e1 = blk
            s = load_block(o, sz)
            w = e1 - e0
            for kc in range(DC):
                fw.op("pe", lambda e, s=s, kc=kc, pb=pb, w=w: e.matmul(
                    PS[pb][0:w, :], ring[:, s, kc * w:(kc + 1) * w], h_sb[:, kc, :],
                    start=(kc == 0), stop=(kc == DC - 1)),
                    reads=[B_ring[s], B_h], writes=[B_ps[pb]])

        def mamba(l, it):
            m = l // 2
            import os as _os
            _st0 = float(_os.environ.get("MAMBA_STOP", "9"))
            if _st0 <= 0:
                return
            o, sz, ka, kb, e0, e1 = woff["m%d_dt" % l][0]
            s = load_block(o, sz)
            for c in range(8):
                for kc in range(DC):
                    fw.op("pe", lambda e, s=s, kc=kc, c=c: e.matmul(
                        PS[4][0:64, c * 64:(c + 1) * 64], h_sb[:, kc, c * 64:(c + 1) * 64], ring[:, s, kc * 64:(kc + 1) * 64],
                        start=(kc == 0), stop=(kc == DC - 1)),
                        reads=[B_ring[s], B_h], writes=[B_ps[4]])
            fw.op("dve", lambda e: e.tensor_tensor(out=v3(dt_sb[0:64, :], 8, 64), in0=v3(PS[4][0:64, :], 8, 64),
                                                   in1=bc_mid(P("mdtb%d" % l, 0, 64)[0:64, :], 8, 64), op=ALU.add),
                  reads=[B_ps[4], B_pv], writes=[B_dt])
            fw.op("act", lambda e: e.activation(out=dt_sb[0:64, :], in_=dt_sb[0:64, :], func=AF.Exp), reads=[B_dt], writes=[B_dt])
            fw.op("dve", lambda e: e.tensor_scalar(out=dt_sb[0:64, :], in0=dt_sb[0:64, :], scalar1=1.0, scalar2=None, op0=ALU.add),
                  reads=[B_dt], writes=[B_dt])
            fw.op("act", lambda e: e.activation(out=dt_sb[0:64, :], in_=dt_sb[0:64, :], func=AF.Ln), reads=[B_dt], writes=[B_dt])
            fw.op("dve", lambda e: e.tensor_tensor(out=v3(adt_sb[0:64, :], 8, 64), in0=v3(dt_sb[0:64, :], 8, 64),
                                                   in1=bc_mid(abc[0:64, m, :], 8, 64), op=ALU.mult),
                  reads=[B_dt, B_const], writes=[B_adt])
            if _st0 <= 0.5:
                return
            fw.op("dve", lambda e: e.tensor_copy(out=ahi_sb[0:64, :], in_=adt_sb[0:64, :]), reads=[B_adt], writes=[B_ahl])
            fw.op("dve", lambda e: e.tensor_tensor(out=alo_sb[0:64, :], in0=adt_sb[0:64, :], in1=ahi_sb[0:64, :], op=ALU.subtract),
                  reads=[B_adt, B_ahl], writes=[B_ahl])
            fw.op("pe", lambda e: e.matmul(PS[5][0:64, :], tri_bf[0:64, :], ahi_sb[0:64, :], start=True, stop=False),
                  reads=[B_ahl, B_const], writes=[B_ps[5]])
            fw.op("pe", lambda e: e.matmul(PS[5][0:64, :], tri_bf[0:64, :], alo_sb[0:64, :], start=False, stop=True),
                  reads=[B_ahl, B_const], writes=[B_ps[5]])
            fw.op("dve", lambda e: e.tensor_copy(out=acs_sb[0:64, :], in_=PS[5][0:64, :]), reads=[B_ps[5]], writes=[B_acs])
            if _st0 <= 0.6:
                return
            fw.op("pe", lambda e: e.matmul(PS[4][:, :], ones_bf[0:64, :], ahi_sb[0:64, :], start=True, stop=False),
                  reads=[B_ahl, B_const], writes=[B_ps[4]])
            fw.op("pe", lambda e: e.matmul(PS[4][:, :], ones_bf[0:64, :], alo_sb[0:64, :], start=False, stop=True),
                  reads=[B_ahl, B_const], writes=[B_ps[4]])
            fw.op("dve", lambda e: e.tensor_copy(out=tmp_sb[:, 0, :], in_=PS[4][:, :]), reads=[B_ps[4]], writes=[B_tmp[0]])
            fw.op("act", lambda e: e.activation(out=edec_sb[:, :], in_=tmp_sb[:, 0, :], func=AF.Exp), reads=[B_tmp[0]], writes=[B_edec])
            if _st0 <= 0.7:
                return
            _var = _os.environ.get("MAMBA_VAR", "")
            if "nosub" not in _var:
                fw.op("dve", lambda e: e.tensor_tensor(out=wend_sb[0:64, :], in0=tmp_sb[0:64, 0, :], in1=acs_sb[0:64, :], op=ALU.subtract),
                      reads=[B_tmp[0], B_acs], writes=[B_wend])
            if "nowend" not in _var:
                fw.op("act", lambda e: e.activation(out=wend_sb[0:64, :], in_=wend_sb[0:64, :], func=AF.Exp), reads=[B_wend], writes=[B_wend])
            if "noeacs" not in _var:
                fw.op("act", lambda e: e.activation(out=eacs_sb[0:64, :], in_=acs_sb[0:64, :], func=AF.Exp), reads=[B_acs], writes=[B_eacs])

            import os as _os
            _stop = float(_os.environ.get("MAMBA_STOP", "9"))
            if _stop <= 1:
                return
            for g in range(8):
                if it == 0 or _os.environ.get("DBG_ZSTATE"):
                    fw.op("pool", lambda e: e.memset(S_sb[:, :], 0.0), writes=[B_S])
                else:
                    fw.dma("sp", lambda e, g=g: e.dma_start(out=S_sb[:, :], in_=sst_d[m, g, :, :]), sem_st[0],
                           reads=[B_sst[m][g], B_h], writes=[B_S])
                fw.op("pool", lambda e: e.tensor_copy(out=Sbf_sb[:, :], in_=S_sb[:, :]), reads=[B_S], writes=[B_Sbf])
                for i in range(4):
                    pb = i % 2
                    proj16(woff["m%d_z%d" % (l, g)][i], pb)
                    fw.op("act", lambda e, i=i, pb=pb: e.activation(out=sz_sb[:, i, :], in_=PS[pb][:, :], func=AF.Silu),
                          reads=[B_ps[pb]], writes=[B_sz[i]])
                for i in range(6):
                    pb = i % 2
                    u = i % 4
                    if i < 4:
                        blk = woff["m%d_x%d" % (l, g)][i]
                        ci = g * 4 + i
                    elif i == 4:
                        blk = woff["m%d_B%d" % (l, g)][0]
                        ci = 32 + g
                    else:
                        blk = woff["m%d_C%d" % (l, g)][0]
                        ci = 40 + g
                    proj16(blk, pb)
                    fw.op("pool", lambda e, u=u, ci=ci: e.tensor_copy(out=ub_sb[:, u, 0:3], in_=mcar[:, m, ci, :]),
                          reads=[B_mcar], writes=[B_ub[u]])
                    fw.op("dve", lambda e, u=u, pb=pb: e.tensor_copy(out=ub_sb[:, u, 3:TT + 3], in_=PS[pb][:, :]),
                          reads=[B_ps[pb]], writes=[B_ub[u]])
                    fw.op("pool", lambda e, u=u, ci=ci: e.tensor_copy(out=mcar[:, m, ci, :], in_=ub_sb[:, u, TT:TT + 3]),
                          reads=[B_ub[u]], writes=[B_mcar])
                    fw.op("dve", lambda e, u=u, ci=ci: e.tensor_scalar(
                        out=cv_sb[:, u, :], in0=ub_sb[:, u, 3:TT + 3], scalar1=P("mcw%d_3" % l, ci), scalar2=P("mcb%d" % l, ci),
                        op0=ALU.mult, op1=ALU.add), reads=[B_ub[u], B_pv], writes=[B_cv[u]])
                    for j in (2, 1, 0):
                        fw.op("dve", lambda e, u=u, ci=ci, j=j: e.scalar_tensor_tensor(
                            out=cv_sb[:, u, :], in0=ub_sb[:, u, j:j + TT], scalar=P("mcw%d_%d" % (l, j), ci), in1=cv_sb[:, u, :],
                            op0=ALU.mult, op1=ALU.add), reads=[B_ub[u], B_pv, B_cv[u]], writes=[B_cv[u]])
                    fw.op("act", lambda e, u=u, i=i: e.activation(out=xc_sb[:, i, :], in_=cv_sb[:, u, :], func=AF.Silu),
                          reads=[B_cv[u]], writes=[B_xc[i]])
                if _stop <= 2:
                    continue
                for c in range(8):
                    cs = slice(c * 64, (c + 1) * 64)
                    hs = slice(c * 64 + g * 8, c * 64 + g * 8 + 8)
                    for i in range(5):
                        fw.op("pe", lambda e, i=i, cs=cs: e.transpose(PT[0:64, i * 128:(i + 1) * 128], xc_sb[:, i, cs], ident_bf[:, :]),
                              reads=[B_xc[i], B_const], writes=[B_pt[0]])
                    fw.op("dve", lambda e: e.tensor_copy(out=xtok_sb[0:64, :], in_=PT[0:64, 0:640]), reads=[B_pt[0]], writes=[B_xtok])
                    fw.op("dve", lambda e, hs=hs: e.tensor_tensor(out=v3(xdt_sb[0:64, :], 8, 64), in0=v3(xtok_sb[0:64, 0:512], 8, 64),
                                                           in1=bc_last(dt_sb[0:64, hs], 8, 64), op=ALU.mult),
                          reads=[B_xtok, B_dt], writes=[B_xdt])
                    fw.op("dve", lambda e, hs=hs: e.tensor_tensor(out=v3(xw_sb[0:64, :], 8, 64), in0=v3(xdt_sb[0:64, :], 8, 64),
                                                           in1=bc_last(wend_sb[0:64, hs], 8, 64), op=ALU.mult),
                          reads=[B_xdt, B_wend], writes=[B_xw])
                    fw.op("dve", lambda e, hs=hs: e.tensor_tensor(out=v3(r1h_sb[0:64, :], 8, 64), in0=bc_last(ahi_sb[0:64, hs], 8, 64),
                                                           in1=bc_mid(tri_bf[0:64, :], 8, 64), op=ALU.mult),
                          reads=[B_ahl, B_const], writes=[B_r1])
                    fw.op("dve", lambda e, hs=hs: e.tensor_tensor(out=v3(r1l_sb[0:64, :], 8, 64), in0=bc_last(alo_sb[0:64, hs], 8, 64),
                                                           in1=bc_mid(tri_bf[0:64, :], 8, 64), op=ALU.mult),
                          reads=[B_ahl, B_const], writes=[B_r1])
                    fw.op("pe", lambda e: e.matmul(PS[2][0:64, :], ones_bf[0:64, 0:64], r1h_sb[0:64, :], start=True, stop=False),
                          reads=[B_r1, B_const], writes=[B_ps[2]])
                    fw.op("pe", lambda e: e.matmul(PS[2][0:64, :], ones_bf[0:64, 0:64], r1l_sb[0:64, :], start=False, stop=True),
                          reads=[B_r1, B_const], writes=[B_ps[2]])
                    fw.op("dve", lambda e, hs=hs: e.tensor_tensor(out=v3(tmp_sb[0:64, 1, :], 8, 64), in0=v3(PS[2][0:64, :], 8, 64),
                                                           in1=bc_last(acs_sb[0:64, hs], 8, 64), op=ALU.subtract),
                          reads=[B_ps[2], B_acs], writes=[B_tmp[1]])
                    fw.op("pool", lambda e: e.tensor_tensor(out=v3(tmp_sb[0:64, 1, :], 8, 64), in0=v3(tmp_sb[0:64, 1, :], 8, 64),
                                                            in1=bc_mid(negm[0:64, :], 8, 64), op=ALU.add),
                          reads=[B_tmp[1], B_const], writes=[B_tmp[1]])
                    fw.op("act", lambda e: e.activation(out=E_sb[0:64, :], in_=tmp_sb[0:64, 1, :], func=AF.Exp),
                          reads=[B_tmp[1]], writes=[B_E])
                    fw.op("pe", lambda e, cs=cs: e.matmul(PS[3][0:64, 0:64], xc_sb[:, 4, cs], xc_sb[:, 5, cs], start=True, stop=True),
                          reads=[B_xc[4], B_xc[5]], writes=[B_ps[3]])
                    fw.op("dve", lambda e: e.tensor_tensor(out=v3(Mt_sb[0:64, :], 8, 64), in0=v3(E_sb[0:64, :], 8, 64),
                                                           in1=bc_mid(PS[3][0:64, 0:64], 8, 64), op=ALU.mult),
                          reads=[B_E, B_ps[3]], writes=[B_Mt])
                    for hh in range(8):
                        fw.op("pe", lambda e, hh=hh: e.matmul(PS[4][0:64, hh * 64:(hh + 1) * 64], Mt_sb[0:64, hh * 64:(hh + 1) * 64],
                                                              xdt_sb[0:64, hh * 64:(hh + 1) * 64], start=True, stop=True),
                              reads=[B_Mt, B_xdt], writes=[B_ps[4]])
                    fw.op("pe", lambda e, cs=cs: e.matmul(PS[5][0:64, :], xc_sb[:, 5, cs], Sbf_sb[:, :], start=True, stop=True),
                          reads=[B_xc[5], B_Sbf], writes=[B_ps[5]])
                    fw.op("dve", lambda e, hs=hs: e.tensor_tensor(out=v3(yt_sb[0:64, :], 8, 64), in0=v3(PS[5][0:64, :], 8, 64),
                                                           in1=bc_last(eacs_sb[0:64, hs], 8, 64), op=ALU.mult),
                          reads=[B_ps[5], B_eacs], writes=[B_yt])
                    fw.op("dve", lambda e: e.tensor_tensor(out=yt_sb[0:64, :], in0=yt_sb[0:64, :], in1=PS[4][0:64, :], op=ALU.add),
                          reads=[B_yt, B_ps[4]], writes=[B_yt])
                    fw.op("dve", lambda e, g=g: e.tensor_tensor(out=v3(t2_sb[0:64, :], 8, 64), in0=v3(xtok_sb[0:64, 0:512], 8, 64),
                                                           in1=bc_last(P("mD%d" % l, g * 8, 8)[0:64, :], 8, 64), op=ALU.mult),
                          reads=[B_xtok, B_pv], writes=[B_t2])
                    fw.op("dve", lambda e: e.tensor_tensor(out=ybf_sb[0:64, :], in0=yt_sb[0:64, :], in1=t2_sb[0:64, :], op=ALU.add),
                          reads=[B_yt, B_t2], writes=[B_ybf])
                    for i in range(4):
                        fw.op("pe", lambda e, i=i: e.transpose(PT[:, 640 + i * 64:640 + (i + 1) * 64], ybf_sb[0:64, i * 128:(i + 1) * 128],
                                                              ident_bf[0:64, 0:64]),
                              reads=[B_ybf, B_const], writes=[B_pt[1]])
                    for i in range(4):
                        fw.op("dve", lambda e, i=i, cs=cs, g=g: e.tensor_tensor(out=y_all[:, g * 4 + i, cs], in0=PT[:, 640 + i * 64:640 + (i + 1) * 64],
                                                                     in1=sz_sb[:, i, cs], op=ALU.mult),
                              reads=[B_pt[1], B_sz[i]], writes=[B_yall[g * 4 + i]])
                    fw.op("pe", lambda e: e.matmul(PS[3][:, :], xtok_sb[0:64, 512:640], xw_sb[0:64, :], start=True, stop=True),
                          reads=[B_xtok, B_xw], writes=[B_ps[3]])
                    fw.op("dve", lambda e, hs=hs: e.tensor_tensor(out=v3(S_sb[:, :], 8, 64), in0=v3(S_sb[:, :], 8, 64),
                                                           in1=bc_last(edec_sb[:, hs], 8, 64), op=ALU.mult),
                          reads=[B_S, B_edec], writes=[B_S])
                    fw.op("dve", lambda e: e.tensor_tensor(out=S_sb[:, :], in0=S_sb[:, :], in1=PS[3][:, :], op=ALU.add),
                          reads=[B_S, B_ps[3]], writes=[B_S])
                    fw.op("pool", lambda e: e.tensor_copy(out=Sbf_sb[:, :], in_=S_sb[:, :]), reads=[B_S], writes=[B_Sbf])
                if _os.environ.get("DBG_G") and g == int(_os.environ["DBG_G"]):
                    break
                fw.dma("sp", lambda e, g=g: e.dma_start(out=sst_d[m, g, :, :], in_=S_sb[:, :]), sem_st[1],
                       reads=[B_S], writes=[B_sst[m][g]])
                for i in range(4):
                    ci = g * 4 + i
                    sidx = i % 2
                    fw.op("pool", lambda e, ci=ci, sidx=sidx: e.tensor_tensor(out=sq_sb[:, sidx, :], in0=y_all[:, ci, :], in1=y_all[:, ci, :], op=ALU.mult),
                          reads=[B_yall[ci]], writes=[B_sq[sidx]])
                    fw.op("pe", lambda e, ci=ci, sidx=sidx: e.matmul(PS[6][:], ones_bf[:], sq_sb[:, sidx, :], start=(ci == 0), stop=(ci == 31)),
                          reads=[B_sq[sidx], B_const], writes=[B_ps[6]])
                    if not _os.environ.get("DBG_DUMP"):
                        fw.op("pool", lambda e, ci=ci: e.tensor_scalar(out=y_all[:, ci, :], in0=y_all[:, ci, :], scalar1=P("mnw%d" % l, ci),
                                                                       scalar2=None, op0=ALU.mult),
                              reads=[B_yall[ci], B_pv], writes=[B_yall[ci]])
            _dd = _os.environ.get("DBG_DUMP", "")
            if _dd == "chunk":
                lst = ((xtok_sb[0:64, 0:512], B_xtok, 512), (xtok_sb[0:64, 512:640], B_xtok, 128), (xdt_sb[0:64, :], B_xdt, 512),
                       (E_sb[0:64, :], B_E, 512), (Mt_sb[0:64, :], B_Mt, 512), (ybf_sb[0:64, :], B_ybf, 512), (yt_sb[0:64, :], B_yt, 512))
                for j, (src, bb, n) in enumerate(lst):
                    fw.op("dve", lambda e, j=j, src=src, n=n: e.tensor_copy(out=x_sb[0:64, j, 0:n], in_=src), reads=[bb, B_x], writes=[B_x])
                fw.op("dve", lambda e: e.tensor_copy(out=x_sb[:, 7, :], in_=S_sb[:, :]), reads=[B_S, B_x], writes=[B_x])
                return
            if _dd == "dts":
                for j, (src, bb) in enumerate(((dt_sb, B_dt), (acs_sb, B_acs), (wend_sb, B_wend), (eacs_sb, B_eacs))):
                    fw.op("dve", lambda e, j=j, src=src: e.tensor_copy(out=x_sb[0:64, j, :], in_=src[0:64, :]), reads=[bb, B_x], writes=[B_x])
                fw.op("dve", lambda e: e.tensor_copy(out=x_sb[:, 4, :], in_=edec_sb[:, :]), reads=[B_edec, B_x], writes=[B_x])
                return
            if _dd == "szxc":
                for j in range(4):
                    fw.op("dve", lambda e, j=j: e.tensor_copy(out=x_sb[:, j, :], in_=sz_sb[:, j, :]), reads=[B_sz[j], B_x], writes=[B_x])
                for j in range(6):
                    fw.op("dve", lambda e, j=j: e.tensor_copy(out=x_sb[:, 4 + j, :], in_=xc_sb[:, j, :]), reads=[B_xc[j], B_x], writes=[B_x])
                return
            if _dd.startswith("yall"):
                hh_ = int(_dd[4:])
                for j in range(DC):
                    fw.op("dve", lambda e, j=j: e.tensor_copy(out=x_sb[:, j, :], in_=y_all[:, hh_ * 16 + j, :]),
                          reads=[B_yall[hh_ * 16 + j], B_x], writes=[B_x])
                return
            if _stop <= 3:
                return
            rstd_from(6, float(SI))
            blocks_out = woff["m%d_out" % l]
            nb = len(blocks_out) // DC
            for i in range(DC):
                pb = i % 2
                first = True
                for q in range(nb):
                    o, sz, ka, kb, e0, e1 = blocks_out[i * nb + q]
                    s = load_block(o, sz)
                    for kc in range(ka, kb):
                        last = (q == nb - 1 and kc == kb - 1)
                        fw.op("pe", lambda e, s=s, kc=kc, ka=ka, pb=pb, first=first, last=last: e.matmul(
                            PS[pb][:], ring[:, s, (kc - ka) * 128:(kc - ka + 1) * 128], y_all[:, kc, :],
                            start=first, stop=last),
                            reads=[B_ring[s], B_yall[kc]], writes=[B_ps[pb]])
                        first = False
                fw.op("dve", lambda e, i=i, pb=pb: e.tensor_tensor(out=f_sb[:, i, :], in0=PS[pb][:], in1=rstd_sb[:], op=ALU.mult),
                      reads=[B_ps[pb], B_rstd], writes=[B_f[i]])
            postnorm_residual(l, 0)


        hcar = sb("hcar", [128, 2, DC], BF16)
        omka = sb("omka", [128, 2, 32], F32)
        mask5 = sb("mask5", [128, 448], BF16)
        B_hcar = Buf("hcar")
        rkv_d = nc.dram_tensor("rkvd", [3, 32, 64, TT], F32, kind="Internal").ap()
        vf_d = nc.dram_tensor("vfd", [32, 64, TT], F32, kind="Internal").ap()
        rst_d = nc.dram_tensor("rstd", [2, 32, 64, 64], F32, kind="Internal").ap()
        B_rkvd = [[Buf("rkvd%d_%d" % (i_, h_)) for h_ in range(32)] for i_ in range(3)]
        B_vfd = [Buf("vfd%d" % h_) for h_ in range(32)]
        B_rst = [[Buf("rst%d_%d" % (r_, h_)) for h_ in range(32)] for r_ in range(2)]
        sem_r = [fw.new_dma_sem() for _ in range(6)]
        fw.op("pool", lambda e: e.memset(hcar[:], 0.0), writes=[B_hcar])
        fw.op("dve", lambda e: e.tensor_tensor(out=mask5[:, 0:64], in0=P("tri", 0, 64), in1=P("ident", 0, 64), op=ALU.subtract),
              reads=[B_pv], writes=[B_const])
        fw.op("dve", lambda e: e.tensor_scalar(out=mask5[:, 64:128], in0=P("tri", 0, 64), scalar1=-1.0, scalar2=1.0, op0=ALU.mult, op1=ALU.add),
              reads=[B_pv], writes=[B_const])
        fw.op("dve", lambda e: e.tensor_copy(out=mask5[:, 128:192], in_=mask5[:, 0:64]), reads=[B_const], writes=[B_const])
        fw.op("dve", lambda e: e.tensor_copy(out=mask5[:, 192:256], in_=P("tri", 0, 64)), reads=[B_pv], writes=[B_const])
        fw.op("dve", lambda e: e.tensor_copy(out=mask5[:, 256:320], in_=P("tri", 0, 64)), reads=[B_pv], writes=[B_const])
        fw.op("dve", lambda e: e.tensor_copy(out=mask5[:, 320:384], in_=P("ident", 0, 64)), reads=[B_pv], writes=[B_const])
        fw.op("dve", lambda e: e.tensor_copy(out=mask5[:, 384:448], in_=P("ident", 0, 64)), reads=[B_pv], writes=[B_const])
        if use_r(mixers):
            for l_ in range(0, nl, 2):
                fw.op("dve", lambda e, l_=l_: e.tensor_scalar(out=omka[:, l_ // 2, :], in0=P("rka_%d" % l_, 0, 32), scalar1=-1.0, scalar2=1.0,
                                                               op0=ALU.mult, op1=ALU.add), reads=[B_pv], writes=[B_const])

        xx_sb = _V3(scr_bf[:, 0:DC * TT], TT)
        xi_sb = _V3(scr_bf[:, DC * TT:2 * DC * TT], TT)
        yh_all = _V3(scr_bf[0:64, 0:32 * TT], TT)
        ro = [8192]

        def rf(n):
            v = scr[:, ro[0]:ro[0] + n]
            ro[0] += n
            return v

        def rb(n):
            v = scr_bf[:, 2 * ro[0]:2 * ro[0] + n]
            ro[0] += (n + 1) // 2
            return v
        l1w = rb(512); l1a = rb(512); l1g = _V3(rb(1024), 512); l1v = rb(512)
        stg_r = _V3(rf(1024), 512)
        r_f = rf(512); k_f = rf(512); v_f = rf(512); lw_f = rf(512); as_f = rf(512)
        c0_f = tmp_sb[:, 0, :]; c1_f = tmp_sb[:, 1, :]
        bon_f = ub_sb[:, 0, 0:TT]; bb_f = ub_sb[:, 1, 0:TT]; k2_f = ub_sb[:, 2, 0:TT]; kkn_f = ub_sb[:, 3, 0:TT]
        rt_b = cv_sb[:, 0, :]; kt_b = cv_sb[:, 1, :]; bt_b = cv_sb[:, 2, :]; at_b = cv_sb[:, 3, :]
        kd_b = sg_sb[:, 0, :]; bd_b = sg_sb[:, 1, :]
        vb_b = rb(512); g_b = rb(512); q_b = rb(512)
        assert ro[0] <= SCRF, ro[0]
        ecl_f = scr2[:, 0:512]; t1_f = scr2[:, 512:1024]; t2_f = scr2[:, 1024:1536]; ytok_f = scr2[:, 1536:2048]
        yfm_f = scr2[:, 2048:2560]
        S_r = scr2[:, 2560:2624]; Sb_r = scr2[:, 2624:2656].bitcast(BF16)
        sc_b = scr2[:, 3424:3648].bitcast(BF16)
        tok_b = scr2[:, 2816:2912].bitcast(BF16)
        PP_b = sc_b[:, 320:448]
        M2_b = scr2[:, 2976:3040].bitcast(BF16)
        X_b = scr2[:, 3040:3072].bitcast(BF16); U_b = scr2[:, 3072:3104].bitcast(BF16)
        st8 = scr2[:, 3104:3168]
        yn_b = scr2[:, 3168:3424].bitcast(BF16)
        BR = {n: Buf("r_" + n) for n in ("xx xi l1w l1a l1g l1v stg0 stg1 r k v lw as kkn k2 bb c0 c1 bon rt kt bt at kd bd vb g q "
                                        "ecl t1 t2 ytok yfm S Sb sc scM tok PP M2 X U st8 yn").split()}
        BR["c0"], BR["c1"] = B_tmp[0], B_tmp[1]
        BR["bon"], BR["bb"], BR["k2"], BR["kkn"] = B_ub[0], B_ub[1], B_ub[2], B_ub[3]
        BR["rt"], BR["kt"], BR["bt"], BR["at"] = B_cv[0], B_cv[1], B_cv[2], B_cv[3]
        BR["kd"], BR["bd"] = B_sg[0], B_sg[1]
        B_yh = [Buf("yh%d" % i) for i in range(32)]

        def v8(ap):
            return ap.rearrange("p (a b) -> p a b", a=8, b=64)

        def proj_xi(blk, pb, w, nrows=None):
            o, sz, ka, kb, e0, e1 = blk
            s = load_block(o, sz)
            for kc in range(DC):
                fw.op("pe", lambda e, s=s, kc=kc, pb=pb, w=w: e.matmul(
                    PS[pb][0:w, :], ring[:, s, kc * w:(kc + 1) * w], xi_sb[:, kc, :],
                    start=(kc == 0), stop=(kc == DC - 1)),
                    reads=[B_ring[s], BR["xi"]], writes=[B_ps[pb]])

        def mix_input(l, i):
            for j in range(DC):
                fw.op("dve", lambda e, j=j: e.scalar_tensor_tensor(out=xi_sb[:, j, :], in0=xx_sb[:, j, :], scalar=P("rmu%d_%d" % (l, i), j),
                                                                   in1=h_sb[:, j, :], op0=ALU.mult, op1=ALU.add),
                      reads=[BR["xx"], B_h, B_pv], writes=[BR["xi"]])

        def rwkv(l, it):
            ri = l // 2
            D64 = slice(0, 64)
            fw.op("dve", lambda e: e.tensor_tensor(out=scr_bf[:, 0:DC * TT].rearrange("p (j t) -> p j t", j=DC)[:, :, 1:TT],
                                                   in0=h_sb[:, :, 0:TT - 1], in1=h_sb[:, :, 1:TT], op=ALU.subtract),
                  reads=[B_h], writes=[BR["xx"]])
            fw.op("dve", lambda e: e.tensor_tensor(out=scr_bf[:, 0:DC * TT].rearrange("p (j t) -> p j t", j=DC)[:, :, 0],
                                                   in0=hcar[:, ri, :], in1=h_sb[:, :, 0], op=ALU.subtract),
                  reads=[B_h, B_hcar], writes=[BR["xx"]])
            fw.op("dve", lambda e: e.tensor_copy(out=hcar[:, ri, :], in_=h_sb[:, :, TT - 1]), reads=[B_h], writes=[B_hcar])
            mix_input(l, 3)
            proj_xi(woff["r%d_w1" % l][0], 0, 96)
            fw.op("act", lambda e: e.activation(out=l1w[0:96, :], in_=PS[0][0:96, :], func=AF.Tanh), reads=[B_ps[0]], writes=[BR["l1w"]])
            mix_input(l, 4)
            proj_xi(woff["r%d_a1" % l][0], 1, 96)
            fw.op("dve", lambda e: e.tensor_copy(out=l1a[0:96, :], in_=PS[1][0:96, :]), reads=[B_ps[1]], writes=[BR["l1a"]])
            mix_input(l, 5)
            for q in range(2):
                proj_xi(woff["r%d_g1" % l][q], q, 128)
                fw.op("act", lambda e, q=q: e.activation(out=l1g[:, q, :], in_=PS[q][:, :], func=AF.Sigmoid), reads=[B_ps[q]], writes=[BR["l1g"]])
            for (i, nm) in ((2, "wv"), (0, "wr"), (1, "wk")):
                mix_input(l, i)
                if i == 2 and l >= 2:
                    proj_xi(woff["r%d_v1" % l][0], 0, 64)
                    fw.op("dve", lambda e: e.tensor_copy(out=l1v[0:64, :], in_=PS[0][0:64, :]), reads=[B_ps[0]], writes=[BR["l1v"]])
                for h in range(32):
                    pb = h % 2
                    proj_xi(woff["r%d_%s" % (l, nm)][h], pb, 64)
                    fw.op("dve", lambda e, pb=pb: e.tensor_copy(out=stg_r[0:64, pb, :], in_=PS[pb][0:64, :]), reads=[B_ps[pb]], writes=[BR["stg%d" % pb]])
                    fw.dma("sp", lambda e, i=i, h=h, pb=pb: e.dma_start(out=rkv_d[i, h, :, :], in_=stg_r[0:64, pb, :]), sem_r[pb],
                           reads=[BR["stg%d" % pb]], writes=[B_rkvd[i][h]])
            for h in range(32):
                cfs = [(r_f, "r", 0), (k_f, "k", 1), (v_f, "v", 2)]
                for (dst, nm, i) in cfs:
                    fw.dma("sp", lambda e, dst=dst, i=i, h=h: e.dma_start(out=dst[0:64, :], in_=rkv_d[i, h, :, :]), sem_r[2 + i],
                           reads=[B_rkvd[i][h]], writes=[BR[nm]])
                o, sz = woff["r%d_l2" % l][h][0:2]
                s = load_block(o, sz)
                fw.op("pe", lambda e, s=s: e.matmul(PS[0][0:64, :], ring[0:96, s, 0:64], l1w[0:96, :], start=True, stop=True),
                      reads=[B_ring[s], BR["l1w"]], writes=[B_ps[0]])
                fw.op("act", lambda e, h=h: e.activation(out=lw_f[D64, :], in_=PS[0][0:64, :], func=AF.Sigmoid, bias=P("rw0_%d" % l, h)[0:64, :]),
                      reads=[B_ps[0], B_pv], writes=[BR["lw"]])
                fw.op("dve", lambda e: e.tensor_scalar(out=lw_f[D64, :], in0=lw_f[D64, :], scalar1=-0.6065306597126334, scalar2=None, op0=ALU.mult),
                      reads=[BR["lw"]], writes=[BR["lw"]])
                fw.op("pe", lambda e, s=s: e.matmul(PS[1][0:64, :], ring[0:96, s, 64:128], l1a[0:96, :], start=True, stop=True),
                      reads=[B_ring[s], BR["l1a"]], writes=[B_ps[1]])
                fw.op("act", lambda e, h=h: e.activation(out=as_f[D64, :], in_=PS[1][0:64, :], func=AF.Sigmoid, bias=P("ra0_%d" % l, h)[0:64, :]),
                      reads=[B_ps[1], B_pv], writes=[BR["as"]])
                fw.op("pe", lambda e, s=s: e.matmul(PS[0][0:64, :], ring[:, s, 128:192], l1g[:, 0, :], start=True, stop=False),
                      reads=[B_ring[s], BR["l1g"]], writes=[B_ps[0]])
                fw.op("pe", lambda e, s=s: e.matmul(PS[0][0:64, :], ring[:, s, 192:256], l1g[:, 1, :], start=False, stop=True),
                      reads=[B_ring[s], BR["l1g"]], writes=[B_ps[0]])
                fw.op("dve", lambda e: e.tensor_copy(out=g_b[D64, :], in_=PS[0][0:64, :]), reads=[B_ps[0]], writes=[BR["g"]])
                if l >= 2:
                    fw.op("pe", lambda e, s=s: e.matmul(PS[1][0:64, :], ring[0:64, s, 256:320], l1v[0:64, :], start=True, stop=True),
                          reads=[B_ring[s], BR["l1v"]], writes=[B_ps[1]])
                    fw.op("act", lambda e, h=h: e.activation(out=t1_f[D64, :], in_=PS[1][0:64, :], func=AF.Sigmoid, bias=P("rv0_%d" % l, h)[0:64, :]),
                          reads=[B_ps[1], B_pv], writes=[BR["t1"]])
                    fw.dma("sp", lambda e, h=h: e.dma_start(out=t2_f[0:64, :], in_=vf_d[h, :, :]), sem_r[5], reads=[B_vfd[h]], writes=[BR["t2"]])
                    fw.op("dve", lambda e: e.tensor_tensor(out=t2_f[D64, :], in0=t2_f[D64, :], in1=v_f[D64, :], op=ALU.subtract),
                          reads=[BR["t2"], BR["v"]], writes=[BR["t2"]])
                    fw.op("dve", lambda e: e.tensor_tensor(out=t2_f[D64, :], in0=t2_f[D64, :], in1=t1_f[D64, :], op=ALU.mult),
                          reads=[BR["t2"], BR["t1"]], writes=[BR["t2"]])
                    fw.op("dve", lambda e: e.tensor_tensor(out=v_f[D64, :], in0=v_f[D64, :], in1=t2_f[D64, :], op=ALU.add),
                          reads=[BR["t2"], BR["v"]], writes=[BR["v"]])
                else:
                    fw.dma("sp", lambda e, h=h: e.dma_start(out=vf_d[h, :, :], in_=v_f[0:64, :]), sem_r[5], reads=[BR["v"]], writes=[B_vfd[h]])
                fw.op("dve", lambda e: e.tensor_copy(out=vb_b[D64, :], in_=v_f[D64, :]), reads=[BR["v"]], writes=[BR["vb"]])
                fw.op("dve", lambda e, h=h: e.tensor_scalar(out=kkn_f[D64, :], in0=k_f[D64, :], scalar1=P("rkk_%d" % l, h)[0:64, :], scalar2=None, op0=ALU.mult),
                      reads=[BR["k"], B_pv], writes=[BR["kkn"]])
                fw.op("dve", lambda e: e.tensor_tensor(out=q_b[D64, :], in0=kkn_f[D64, :], in1=kkn_f[D64, :], op=ALU.mult),
                      reads=[BR["kkn"]], writes=[BR["q"]])
                fw.op("pe", lambda e: e.matmul(PS[1][0:64, :], ones_bf[0:64, 0:64], q_b[D64, :], start=True, stop=True),
                      reads=[BR["q"], B_const], writes=[B_ps[1]])
                fw.op("dve", lambda e: e.tensor_scalar(out=t1_f[D64, :], in0=PS[1][0:64, :], scalar1=1e-24, scalar2=None, op0=ALU.max),
                      reads=[B_ps[1]], writes=[BR["t1"]])
                fw.op("act", lambda e: e.activation(out=t1_f[D64, :], in_=t1_f[D64, :], func=AF.Sqrt), reads=[BR["t1"]], writes=[BR["t1"]])
                fw.op("dve", lambda e: e.reciprocal(out=t1_f[D64, :], in_=t1_f[D64, :]), reads=[BR["t1"]], writes=[BR["t1"]])
                fw.op("dve", lambda e: e.tensor_tensor(out=kkn_f[D64, :], in0=kkn_f[D64, :], in1=t1_f[D64, :], op=ALU.mult),
                      reads=[BR["kkn"], BR["t1"]], writes=[BR["kkn"]])
                fw.op("dve", lambda e, h=h: e.tensor_scalar(out=t1_f[D64, :], in0=as_f[D64, :], scalar1=P("rka_%d" % l, h)[0:64, :],
                                                            scalar2=omka[0:64, ri, h:h + 1], op0=ALU.mult, op1=ALU.add),
                      reads=[BR["as"], B_pv, B_const], writes=[BR["t1"]])
                fw.op("dve", lambda e: e.tensor_tensor(out=k2_f[D64, :], in0=k_f[D64, :], in1=t1_f[D64, :], op=ALU.mult),
                      reads=[BR["k"], BR["t1"]], writes=[BR["k2"]])
                fw.op("dve", lambda e: e.tensor_tensor(out=bb_f[D64, :], in0=kkn_f[D64, :], in1=as_f[D64, :], op=ALU.mult),
                      reads=[BR["kkn"], BR["as"]], writes=[BR["bb"]])
                fw.op("dve", lambda e: e.tensor_tensor(out=t1_f[D64, :], in0=r_f[D64, :], in1=k2_f[D64, :], op=ALU.mult),
                      reads=[BR["r"], BR["k2"]], writes=[BR["t1"]])
                fw.op("dve", lambda e, h=h: e.tensor_scalar(out=q_b[D64, :], in0=t1_f[D64, :], scalar1=P("rrk_%d" % l, h)[0:64, :], scalar2=None, op0=ALU.mult),
                      reads=[BR["t1"], B_pv], writes=[BR["q"]])
                fw.op("pe", lambda e: e.matmul(PS[0][0:64, :], ones_bf[0:64, 0:64], q_b[D64, :], start=True, stop=True),
                      reads=[BR["q"], B_const], writes=[B_ps[0]])
                fw.op("dve", lambda e: e.tensor_tensor(out=bon_f[D64, :], in0=PS[0][0:64, :], in1=v_f[D64, :], op=ALU.mult),
                      reads=[B_ps[0], BR["v"]], writes=[BR["bon"]])
                src, srcn = lw_f, "lw"
                bufs = [(c0_f, "c0"), (c1_f, "c1")]
                for si, sft in enumerate((1, 2, 4, 8, 16, 32)):
                    dst, dstn = bufs[si % 2]
                    fw.op("pool", lambda e, src=src, dst=dst, sft=sft: e.tensor_tensor(out=v8(dst[D64, :])[:, :, sft:64], in0=v8(src[D64, :])[:, :, sft:64],
                                                                                  in1=v8(src[D64, :])[:, :, 0:64 - sft], op=ALU.add),
                          reads=[BR[srcn]], writes=[BR[dstn]])
                    fw.op("pool", lambda e, src=src, dst=dst, sft=sft: e.tensor_copy(out=v8(dst[D64, :])[:, :, 0:sft], in_=v8(src[D64, :])[:, :, 0:sft]),
                          reads=[BR[srcn]], writes=[BR[dstn]])
                    src, srcn = dst, dstn
                cl, cln = src, srcn
                fw.op("act", lambda e, cl=cl: e.activation(out=ecl_f[D64, :], in_=cl[D64, :], func=AF.Exp), reads=[BR[cln]], writes=[BR["ecl"]])
                fw.op("dve", lambda e: e.tensor_tensor(out=rt_b[D64, :], in0=r_f[D64, :], in1=ecl_f[D64, :], op=ALU.mult),
                      reads=[BR["r"], BR["ecl"]], writes=[BR["rt"]])
                fw.op("act", lambda e, cl=cl: e.activation(out=t1_f[D64, :], in_=cl[D64, :], func=AF.Exp, scale=-1.0), reads=[BR[cln]], writes=[BR["t1"]])
                fw.op("dve", lambda e: e.tensor_tensor(out=kt_b[D64, :], in0=k2_f[D64, :], in1=t1_f[D64, :], op=ALU.mult),
                      reads=[BR["k2"], BR["t1"]], writes=[BR["kt"]])
                fw.op("dve", lambda e: e.tensor_tensor(out=bt_b[D64, :], in0=bb_f[D64, :], in1=t1_f[D64, :], op=ALU.mult),
                      reads=[BR["bb"], BR["t1"]], writes=[BR["bt"]])
                fw.op("dve", lambda e, cl=cl: e.tensor_tensor(out=t2_f[D64, :], in0=cl[D64, :], in1=lw_f[D64, :], op=ALU.subtract),
                      reads=[BR[cln], BR["lw"]], writes=[BR["t2"]])
                fw.op("act", lambda e: e.activation(out=t2_f[D64, :], in_=t2_f[D64, :], func=AF.Exp), reads=[BR["t2"]], writes=[BR["t2"]])
                fw.op("dve", lambda e: e.scalar_tensor_tensor(out=at_b[D64, :], in0=kkn_f[D64, :], scalar=-1.0, in1=t2_f[D64, :], op0=ALU.mult, op1=ALU.mult),
                      reads=[BR["kkn"], BR["t2"]], writes=[BR["at"]])
                fw.op("dve", lambda e, cl=cl: e.tensor_tensor(out=v8(t1_f[D64, :]), in0=v8(cl[D64, :])[:, :, 63:64].to_broadcast([64, 8, 64]),
                                                              in1=v8(cl[D64, :]), op=ALU.subtract),
                      reads=[BR[cln]], writes=[BR["t1"]])
                fw.op("act", lambda e: e.activation(out=t1_f[D64, :], in_=t1_f[D64, :], func=AF.Exp), reads=[BR["t1"]], writes=[BR["t1"]])
                fw.op("dve", lambda e: e.tensor_tensor(out=kd_b[D64, :], in0=k2_f[D64, :], in1=t1_f[D64, :], op=ALU.mult),
                      reads=[BR["k2"], BR["t1"]], writes=[BR["kd"]])
                fw.op("dve", lambda e: e.tensor_tensor(out=bd_b[D64, :], in0=bb_f[D64, :], in1=t1_f[D64, :], op=ALU.mult),
                      reads=[BR["bb"], BR["t1"]], writes=[BR["bd"]])
                if it == 0:
                    fw.op("pool", lambda e: e.memset(S_r[D64, :], 0.0), writes=[BR["S"]])
                else:
                    fw.dma("sp", lambda e, h=h: e.dma_start(out=S_r[0:64, :], in_=rst_d[ri, h, :, :]), sem_r[5],
                           reads=[B_rst[ri][h], B_h], writes=[BR["S"]])
                fw.op("pool", lambda e: e.tensor_copy(out=Sb_r[D64, :], in_=S_r[D64, :]), reads=[BR["S"]], writes=[BR["Sb"]])
                for c in range(8):
                    cs = slice(c * 64, (c + 1) * 64)
                    for ti, (src_b, srcn2) in enumerate(((vb_b, "vb"), (kd_b, "kd"), (bd_b, "bd"))):
                        fw.op("pe", lambda e, ti=ti, src_b=src_b, cs=cs: e.transpose(PT[0:64, ti * 64:(ti + 1) * 64], src_b[D64, cs], ident_bf[0:64, 0:64]),
                              reads=[BR[srcn2], B_const], writes=[B_pt[0]])
                    fw.op("dve", lambda e: e.tensor_copy(out=tok_b[D64, :], in_=PT[0:64, 0:192]), reads=[B_pt[0]], writes=[BR["tok"]])
                    prs = ((bt_b, "bt", at_b, "at"), (at_b, "at", bt_b, "bt"), (kt_b, "kt", at_b, "at"), (bt_b, "bt", rt_b, "rt"), (kt_b, "kt", rt_b, "rt"))
                    for pi, (la, lan, rr, rrn) in enumerate(prs):
                        fw.op("pe", lambda e, pi=pi, la=la, rr=rr, cs=cs: e.matmul(PS[2][0:64, pi * 64:(pi + 1) * 64], la[D64, cs], rr[D64, cs], start=True, stop=True),
                              reads=[BR[lan], BR[rrn]], writes=[B_ps[2]])
                    fw.op("dve", lambda e: e.tensor_tensor(out=sc_b[D64, 0:320], in0=PS[2][0:64, 0:320], in1=mask5[0:64, 0:320], op=ALU.mult),
                          reads=[B_ps[2], B_const], writes=[BR["sc"], BR["scM"]])
                    fw.op("dve", lambda e: e.tensor_tensor(out=PP_b[D64, :], in0=sc_b[D64, 0:128], in1=mask5[0:64, 320:448], op=ALU.add),
                          reads=[BR["scM"], B_const], writes=[BR["PP"]])
                    fw.op("pe", lambda e, cs=cs: e.matmul(PS[3][0:64, 0:64], at_b[D64, cs], Sb_r[D64, :], start=True, stop=False),
                          reads=[BR["at"], BR["Sb"]], writes=[B_ps[3]])
                    fw.op("pe", lambda e: e.matmul(PS[3][0:64, 0:64], sc_b[D64, 128:192], tok_b[D64, 0:64], start=False, stop=True),
                          reads=[BR["sc"], BR["tok"]], writes=[B_ps[3]])
                    fw.op("dve", lambda e: e.tensor_copy(out=X_b[D64, :], in_=PS[3][0:64, 0:64]), reads=[B_ps[3]], writes=[BR["X"]])
                    bufA, bufAn, bufB, bufBn = sc_b[D64, 0:128], "scM", M2_b[D64, :], "M2"
                    cur, curn, nxt, nxtn = bufA, bufAn, bufB, bufBn
                    fw.op("pe", lambda e, cur=cur: e.matmul(PS[4][0:64, 0:64], cur[:, 64:128], cur[:, 0:64], start=True, stop=True), reads=[BR[curn]], writes=[B_ps[4]])
                    fw.op("pe", lambda e, cur=cur: e.matmul(PS[4][0:64, 64:128], cur[:, 0:64], cur[:, 64:128], start=True, stop=True), reads=[BR[curn]], writes=[B_ps[4]])
                    fw.op("dve", lambda e, nxt=nxt: e.tensor_copy(out=nxt, in_=PS[4][0:64, 0:128]), reads=[B_ps[4]], writes=[BR[nxtn]])
                    for lev in range(5):
                        cur, curn, nxt, nxtn = nxt, nxtn, cur, curn
                        fw.op("pe", lambda e, cur=cur: e.matmul(PS[4][0:64, 128:192], PP_b[D64, 64:128], cur[:, 0:64], start=True, stop=True),
                              reads=[BR["PP"], BR[curn]], writes=[B_ps[4]])
                        fw.op("pe", lambda e, cur=cur: e.matmul(PS[4][0:64, 192:256], cur[:, 0:64], PP_b[D64, 64:128], start=True, stop=True),
                              reads=[BR["PP"], BR[curn]], writes=[B_ps[4]])
                        if lev < 4:
                            fw.op("pe", lambda e, cur=cur: e.matmul(PS[4][0:64, 0:64], cur[:, 64:128], cur[:, 0:64], start=True, stop=True), reads=[BR[curn]], writes=[B_ps[4]])
                            fw.op("pe", lambda e, cur=cur: e.matmul(PS[4][0:64, 64:128], cur[:, 0:64], cur[:, 64:128], start=True, stop=True), reads=[BR[curn]], writes=[B_ps[4]])
                        fw.op("dve", lambda e: e.tensor_tensor(out=PP_b[D64, :], in0=PP_b[D64, :], in1=PS[4][0:64, 128:256], op=ALU.add),
                              reads=[BR["PP"], B_ps[4]], writes=[BR["PP"]])
                        if lev < 4:
                            fw.op("dve", lambda e, nxt=nxt: e.tensor_copy(out=nxt, in_=PS[4][0:64, 0:128]), reads=[B_ps[4]], writes=[BR[nxtn]])
                    fw.op("pe", lambda e: e.matmul(PS[3][0:64, 64:128], PP_b[D64, 0:64], X_b[D64, :], start=True, stop=True),
                          reads=[BR["PP"], BR["X"]], writes=[B_ps[3]])
                    fw.op("dve", lambda e: e.tensor_copy(out=U_b[D64, :], in_=PS[3][0:64, 64:128]), reads=[B_ps[3]], writes=[BR["U"]])
                    fw.op("pe", lambda e, cs=cs: e.matmul(PS[5][0:64, 0:64], rt_b[D64, cs], Sb_r[D64, :], start=True, stop=False),
                          reads=[BR["rt"], BR["Sb"]], writes=[B_ps[5]])
                    fw.op("pe", lambda e: e.matmul(PS[5][0:64, 0:64], sc_b[D64, 192:256], U_b[D64, :], start=False, stop=False),
                          reads=[BR["sc"], BR["U"]], writes=[B_ps[5]])
                    fw.op("pe", lambda e: e.matmul(PS[5][0:64, 0:64], sc_b[D64, 256:320], tok_b[D64, 0:64], start=False, stop=True),
                          reads=[BR["sc"], BR["tok"]], writes=[B_ps[5]])
                    fw.op("dve", lambda e, cs=cs: e.tensor_copy(out=ytok_f[D64, cs], in_=PS[5][0:64, 0:64]), reads=[B_ps[5]], writes=[BR["ytok"]])
                    fw.op("pe", lambda e: e.matmul(PS[3][0:64, 128:192], tok_b[D64, 128:192], U_b[D64, :], start=True, stop=False),
                          reads=[BR["tok"], BR["U"]], writes=[B_ps[3]])
                    fw.op("pe", lambda e: e.matmul(PS[3][0:64, 128:192], tok_b[D64, 64:128], tok_b[D64, 0:64], start=False, stop=True),
                          reads=[BR["tok"]], writes=[B_ps[3]])
                    fw.op("dve", lambda e, c=c: e.scalar_tensor_tensor(out=S_r[D64, :], in0=S_r[D64, :], scalar=ecl_f[0:64, c * 64 + 63:c * 64 + 64],
                                                                   in1=PS[3][0:64, 128:192], op0=ALU.mult, op1=ALU.add),
                          reads=[BR["S"], BR["ecl"], B_ps[3]], writes=[BR["S"]])
                    fw.op("pool", lambda e: e.tensor_copy(out=Sb_r[D64, :], in_=S_r[D64, :]), reads=[BR["S"]], writes=[BR["Sb"]])
                fw.dma("sp", lambda e, h=h: e.dma_start(out=rst_d[ri, h, :, :], in_=S_r[0:64, :]), sem_r[4], reads=[BR["S"]], writes=[B_rst[ri][h]])
                fw.op("dve", lambda e: e.tensor_reduce(out=st8[D64, 0:8], in_=v8(ytok_f[D64, :]), axis=AX.X, op=ALU.add), reads=[BR["ytok"]], writes=[BR["st8"]])
                fw.op("dve", lambda e: e.tensor_tensor(out=t1_f[D64, :], in0=ytok_f[D64, :], in1=ytok_f[D64, :], op=ALU.mult), reads=[BR["ytok"]], writes=[BR["t1"]])
                fw.op("dve", lambda e: e.tensor_reduce(out=st8[D64, 8:16], in_=v8(t1_f[D64, :]), axis=AX.X, op=ALU.add), reads=[BR["t1"]], writes=[BR["st8"]])
                fw.op("dve", lambda e: e.tensor_scalar(out=st8[D64, 0:16], in0=st8[D64, 0:16], scalar1=1.0 / 64, scalar2=None, op0=ALU.mult), reads=[BR["st8"]], writes=[BR["st8"]])
                fw.op("dve", lambda e: e.tensor_tensor(out=st8[D64, 16:24], in0=st8[D64, 0:8], in1=st8[D64, 0:8], op=ALU.mult), reads=[BR["st8"]], writes=[BR["st8"]])
                fw.op("dve", lambda e: e.tensor_tensor(out=st8[D64, 8:16], in0=st8[D64, 8:16], in1=st8[D64, 16:24], op=ALU.subtract), reads=[BR["st8"]], writes=[BR["st8"]])
                fw.op("dve", lambda e: e.tensor_scalar(out=st8[D64, 8:16], in0=st8[D64, 8:16], scalar1=64e-5, scalar2=None, op0=ALU.add), reads=[BR["st8"]], writes=[BR["st8"]])
                fw.op("act", lambda e: e.activation(out=st8[D64, 8:16], in_=st8[D64, 8:16], func=AF.Sqrt), reads=[BR["st8"]], writes=[BR["st8"]])
                fw.op("dve", lambda e: e.reciprocal(out=st8[D64, 8:16], in_=st8[D64, 8:16]), reads=[BR["st8"]], writes=[BR["st8"]])
                fw.op("dve", lambda e: e.tensor_tensor(out=v8(t1_f[D64, :]), in0=v8(ytok_f[D64, :]), in1=st8[D64, 0:8].unsqueeze(2).to_broadcast([64, 8, 64]), op=ALU.subtract),
                      reads=[BR["ytok"], BR["st8"]], writes=[BR["t1"]])
                fw.op("dve", lambda e: e.tensor_tensor(out=v8(yn_b[D64, :]), in0=v8(t1_f[D64, :]), in1=st8[D64, 8:16].unsqueeze(2).to_broadcast([64, 8, 64]), op=ALU.mult),
                      reads=[BR["t1"], BR["st8"]], writes=[BR["yn"]])
                for c in range(8):
                    fw.op("pe", lambda e, c=c: e.transpose(PT[0:64, 256 + c * 64:256 + (c + 1) * 64], yn_b[D64, c * 64:(c + 1) * 64], ident_bf[0:64, 0:64]),
                          reads=[BR["yn"], B_const], writes=[B_pt[0]])
                fw.op("dve", lambda e, h=h: e.tensor_scalar(out=yfm_f[D64, :], in0=PT[0:64, 256:768], scalar1=P("rlw_%d" % l, h)[0:64, :],
                                                            scalar2=P("rlb_%d" % l, h)[0:64, :], op0=ALU.mult, op1=ALU.add),
                      reads=[B_pt[0], B_pv], writes=[BR["yfm"]])
                fw.op("dve", lambda e: e.tensor_tensor(out=yfm_f[D64, :], in0=yfm_f[D64, :], in1=bon_f[D64, :], op=ALU.add),
                      reads=[BR["yfm"], BR["bon"]], writes=[BR["yfm"]])
                fw.op("dve", lambda e, h=h: e.tensor_tensor(out=yh_all[:, h, :], in0=yfm_f[D64, :], in1=g_b[D64, :], op=ALU.mult),
                      reads=[BR["yfm"], BR["g"]], writes=[B_yh[h]])
            blocks_out = woff["r%d_wo" % l]
            for i in range(DC):
                pb = i % 2
                for q in range(2):
                    o, sz = blocks_out[i * 2 + q][0:2]
                    s = load_block(o, sz)
                    for hh in range(16):
                        hd = q * 16 + hh
                        fw.op("pe", lambda e, s=s, hh=hh, hd=hd, pb=pb, q=q: e.matmul(
                            PS[pb][:], ring[0:64, s, hh * 128:(hh + 1) * 128], yh_all[:, hd, :],
                            start=(q == 0 and hh == 0), stop=(q == 1 and hh == 15)),
                            reads=[B_ring[s], B_yh[hd]], writes=[B_ps[pb]])
                fw.op("dve", lambda e, i=i, pb=pb: e.tensor_copy(out=f_sb[:, i, :], in_=PS[pb][:]), reads=[B_ps[pb]], writes=[B_f[i]])
            postnorm_residual(l, 0)

        xv = x_d.rearrange("(j p) t -> p j t", p=128)
        yv = y_d.rearrange("(j p) t -> p j t", p=128)
        ytok = None
        for it in range(NT):
            t0 = it * TT
            fw.dma("sp", lambda e, t0=t0: e.dma_start(out=x_sb[:], in_=xv[:, :, t0:t0 + TT]), sem_x, writes=[B_x])
            for l in range(nl):
                if use_r(mixers) and l % 2 == 0:
                    prenorm(l, 0)
                    rwkv(l, it)
                if use_m(mixers) and l % 2 == 1:
                    prenorm(l, 0)
                    mamba(l, it)
                import os as _os4
                if _os4.environ.get("DBG_DUMP") and l >= 1:
                    continue
                prenorm(l, 1)
                ffn(l)
            import os as _os2
            if _os2.environ.get("DBG_CLAMP"):
                for j in range(DC):
                    fw.op("dve", lambda e, j=j: e.tensor_scalar(out=x_sb[:, j, :], in0=x_sb[:, j, :], scalar1=7777.0, scalar2=-7777.0,
                                                                 op0=ALU.min, op1=ALU.max), reads=[B_x], writes=[B_x])
            ytok = fw.dma("sp", lambda e, t0=t0: e.dma_start(out=yv[:, :, t0:t0 + TT], in_=x_sb[:]), sem_y, reads=[B_x])
        fw.emit([ytok])
    return nc


_CACHE = {}


def run(inp, T, nl, mixers=True):
    Bn = inp["x"].shape[0]
    pvs = [pack_host(inp, b, nl, mixers) for b in range(Bn)]
    wl = pack_weights(inp, nl, mixers)
    wall = wl.build()
    ada = pack_ada(inp)
    nc = build_nc(T, nl, pvs[0].off, pvs[0].n, mixers)
    in_maps = []
    for b in range(Bn):
        in_maps.append({"x": np.ascontiguousarray(inp["x"][b].T), "pv": pvs[b].build(), "wall": wall, "adaw": ada})
    res = run_bass_kernel_spmd(nc, in_maps, core_ids=list(range(Bn)))
    out = np.stack([np.ascontiguousarray(res.results[b]["y"].T) for b in range(Bn)], axis=0)
    return out


MIXERS = True


def kernel(**inputs):
    inp = {k: np.asarray(v) for k, v in inputs.items()}
    return run(inp, inp["x"].shape[1], NL_DEFAULT, mixers=MIXERS).astype(np.float32)
```
